# Optimizing a Trainium2 kernel written in Bass

```python
import jax, jax.numpy as jnp
from jax import lax
import numpy as np

D_MODEL = 2048
BATCH = 4
SEQ = 4096
DEPTH = 2

HEAD_DIM = 64
HALF_DIM = HEAD_DIM // 2
SCALE = HEAD_DIM ** -0.5
NORM_EPS = 1e-6
ROPE_THETA = 10000.0
NEG_INF = -1e30

RW_HEADS = 8
RW_WIDTH = RW_HEADS * HEAD_DIM
RW_DECAY_LORA = 32
RW_AAA_LORA = 32
RW_GATE_LORA = 96
RW_LNX_EPS = 64e-5
RW_SPLITS = (RW_WIDTH, RW_WIDTH, RW_WIDTH, 2 * RW_DECAY_LORA, 2 * RW_AAA_LORA, RW_GATE_LORA)
RW_COLS = sum(RW_SPLITS)

DIL_PATTERNS = ((128, 1), (512, 4), (2048, 16))
DIL_GROUPS = len(DIL_PATTERNS)
DIL_HEADS_PER_GROUP = 4
DIL_HEADS = DIL_GROUPS * DIL_HEADS_PER_GROUP
DIL_WIDTH = DIL_HEADS * HEAD_DIM
DIL_OUT_WIDTH = DIL_HEADS_PER_GROUP * HEAD_DIM
DIL_BLOCK = 64

GRID_W = 64
NA_HEADS = 12
NA_WIDTH = NA_HEADS * HEAD_DIM
NA_KH = 8
NA_KW = 16
NA_QCB = 16
NA_KCB = 32

N_BRANCH = 3
IN_SPLITS = (RW_COLS, DIL_WIDTH, DIL_WIDTH, DIL_WIDTH, NA_WIDTH, NA_WIDTH, NA_WIDTH, N_BRANCH * D_MODEL)
IN_COLS = sum(IN_SPLITS)

D_FF = 5632
PLE_DIM = 256

kernel_name = 'hybrid_rwkv7_dilated_natten_encoder'


def _split(z, sizes):
    idx = [int(s) for s in np.cumsum(sizes)[:-1]]
    return jnp.split(z, idx, axis=-1)


def rmsnorm(z, g):
    zf = z.astype(jnp.float32)
    zf = zf * lax.rsqrt(jnp.mean(zf * zf, axis=-1, keepdims=True) + NORM_EPS)
    return (zf * g).astype(z.dtype)


def centred_shift(z):
    zp = jnp.pad(z, ((0, 0), (1, 1), (0, 0)))
    return 0.5 * (zp[:, :-2] + zp[:, 2:])


def rope_tables(T):
    inv = ROPE_THETA ** (-jnp.arange(0, HEAD_DIM, 2, dtype=jnp.float32) / HEAD_DIM)
    ang = jnp.arange(T, dtype=jnp.float32)[:, None] * inv[None, :]
    return jnp.cos(ang), jnp.sin(ang)


def apply_rope(z, cos, sin):
    zf = z.astype(jnp.float32)
    z1, z2 = zf[..., :HALF_DIM], zf[..., HALF_DIM:]
    c, s = cos[:, None, :], sin[:, None, :]
    return jnp.concatenate([z1 * c - z2 * s, z2 * c + z1 * s], axis=-1).astype(z.dtype)


def rwkv7_step(S, inp):
    r_t, w_t, k_t, v_t, a_t, b_t = inp
    sa = jnp.einsum('dbhij,dbhj->dbhi', S, a_t)
    S = S * w_t[..., None, :] + sa[..., :, None] * b_t[..., None, :] + v_t[..., :, None] * k_t[..., None, :]
    y = jnp.einsum('dbhij,dbhj->dbhi', S, r_t)
    return S, y


def rwkv7_bidir(cols, mu, w0, w2, a0, a2, g2, k_k, k_a, r_k, lnx_g, lnx_b):
    dt = cols.dtype
    c = cols.astype(jnp.float32)
    c = c + mu * (centred_shift(c) - c)
    r, k, v, wd, ad, gd = _split(c, RW_SPLITS)
    B, T, C = r.shape
    H, N = RW_HEADS, HEAD_DIM
    wd = wd.reshape(B, T, 2, RW_DECAY_LORA)
    ad = ad.reshape(B, T, 2, RW_AAA_LORA)
    w = -jax.nn.softplus(-(w0 + jnp.einsum('btdl,dlc->btdc', jnp.tanh(wd), w2))) - 0.5
    decay = jnp.exp(-jnp.exp(w))
    lr = jax.nn.sigmoid(a0 + jnp.einsum('btdl,dlc->btdc', ad, a2))
    gate = jax.nn.sigmoid(gd) @ g2
    kk = (k * k_k).reshape(B, T, H, N)
    kk = kk / jnp.maximum(jnp.linalg.norm(kk, axis=-1, keepdims=True), 1e-12)
    kd = k[:, :, None] * (1.0 + (lr - 1.0) * k_a)
    heads = lambda z: z.reshape(z.shape[:-1] + (H, N))
    r_h, v_h = heads(r), heads(v)
    kd_h, decay_h, lr_h = heads(kd), heads(decay), heads(lr)
    both = lambda z: jnp.broadcast_to(z[:, :, None], kd_h.shape)
    kk2 = both(kk)

    def to_scan(z):
        z = jnp.stack([z[:, :, 0], z[:, ::-1, 1]], axis=0)
        return jnp.moveaxis(z, 2, 0)

    xs = (to_scan(both(r_h)), to_scan(decay_h), to_scan(kd_h), to_scan(both(v_h)),
          to_scan(-kk2), to_scan(kk2 * lr_h))
    S0 = jnp.zeros((2, B, H, N, N), jnp.float32)
    _, ys = lax.scan(rwkv7_step, S0, xs)
    y = jnp.moveaxis(ys[:, 0], 0, 1) + jnp.moveaxis(ys[::-1, 1], 0, 1)
    mean = jnp.mean(y, axis=-1, keepdims=True)
    var = jnp.mean(jnp.square(y - mean), axis=-1, keepdims=True)
    yn = ((y - mean) * lax.rsqrt(var + RW_LNX_EPS)).reshape(B, T, C) * lnx_g + lnx_b
    bonus = jnp.sum(jnp.sum(r_h[:, :, None] * kd_h * r_k, axis=-1, keepdims=True) * v_h[:, :, None], axis=2)
    return ((yn + bonus.reshape(B, T, C)) * gate).astype(dt)


def dilated_band(q, k, v, half, dil):
    B, T, H, E = q.shape
    n_sub = T // dil
    ns = -(-half // DIL_BLOCK)
    kw = (2 * ns + 1) * DIL_BLOCK
    nblk = -(-n_sub // DIL_BLOCK)
    n_pad = nblk * DIL_BLOCK
    sub = lambda z: jnp.swapaxes(z.reshape(B, n_sub, dil, H, E), 1, 2)
    qs = jnp.pad(sub(q), ((0, 0), (0, 0), (0, n_pad - n_sub), (0, 0), (0, 0)))
    qs = qs.reshape(B, dil, nblk, DIL_BLOCK, H, E)
    padk = ((0, 0), (0, 0), (ns * DIL_BLOCK, n_pad - n_sub + ns * DIL_BLOCK), (0, 0), (0, 0))

    def windows(z):
        zb = jnp.pad(sub(z), padk).reshape(B, dil, nblk + 2 * ns, DIL_BLOCK, H, E)
        return jnp.concatenate([zb[:, :, s:s + nblk] for s in range(2 * ns + 1)], axis=3)

    kwin, vwin = windows(k), windows(v)
    jq = np.arange(nblk)[:, None] * DIL_BLOCK + np.arange(DIL_BLOCK)[None, :]
    jk = np.arange(nblk)[:, None] * DIL_BLOCK - ns * DIL_BLOCK + np.arange(kw)[None, :]
    ok = ((np.abs(jk[:, None, :] - jq[:, :, None]) <= half)
          & (jk[:, None, :] >= 0) & (jk[:, None, :] < n_sub))
    s = jnp.einsum('briqhe,brimhe->brihqm', qs, kwin, preferred_element_type=jnp.float32)
    s = jnp.where(jnp.asarray(ok)[None, None, :, None, :, :], s, NEG_INF)
    lse = jax.nn.logsumexp(s, axis=-1)
    pr = jnp.exp(s - lse[..., None])
    o = jnp.einsum('brihqm,brimhe->briqhe', pr, vwin.astype(jnp.float32))
    o = jnp.swapaxes(o.reshape(B, dil, n_pad, H, E)[:, :, :n_sub], 1, 2).reshape(B, T, H, E)
    lse = jnp.transpose(lse, (0, 1, 2, 4, 3)).reshape(B, dil, n_pad, H)[:, :, :n_sub]
    lse = jnp.swapaxes(lse, 1, 2).reshape(B, T, H)
    return o, lse


def dilated_attention(q, k, v, cos, sin):
    B, T, _ = q.shape
    q = apply_rope(q.reshape(B, T, DIL_HEADS, HEAD_DIM), cos, sin) * SCALE
    k = apply_rope(k.reshape(B, T, DIL_HEADS, HEAD_DIM), cos, sin)
    v = v.reshape(B, T, DIL_HEADS, HEAD_DIM)
    outs, lses = [], []
    for g, (window, dil) in enumerate(DIL_PATTERNS):
        hs = slice(g * DIL_HEADS_PER_GROUP, (g + 1) * DIL_HEADS_PER_GROUP)
        o, l = dilated_band(q[:, :, hs], k[:, :, hs], v[:, :, hs], window // (2 * dil), dil)
        outs.append(o)
        lses.append(l)
    wts = jax.nn.softmax(jnp.stack(lses, axis=0), axis=0)
    o = jnp.einsum('gbth,gbthe->bthe', wts, jnp.stack(outs, axis=0))
    return o.reshape(B, T, DIL_OUT_WIDTH).astype(v.dtype)


def neighbourhood_attention(q, k, v, rel_bias):
    B, T, _ = q.shape
    rows = T // GRID_W
    kh = min(NA_KH, rows)
    n_cb = GRID_W // NA_QCB
    qc = np.arange(GRID_W).reshape(n_cb, NA_QCB)
    kc0 = np.clip(np.arange(n_cb) * NA_QCB - NA_KW // 2, 0, GRID_W - NA_KCB)
    kc = kc0[:, None] + np.arange(NA_KCB)[None, :]
    wc0 = np.clip(qc - NA_KW // 2, 0, GRID_W - NA_KW)
    col_ok = jnp.asarray((kc[:, None, :] >= wc0[:, :, None]) & (kc[:, None, :] < wc0[:, :, None] + NA_KW))
    dx_idx = np.clip(kc[:, None, :] - qc[:, :, None], 1 - NA_KW, NA_KW - 1) + NA_KW - 1
    qg = (q * SCALE).reshape(B, rows, n_cb, NA_QCB, NA_HEADS, HEAD_DIM)
    kg = k.reshape(B, rows, GRID_W, NA_HEADS, HEAD_DIM)
    vg = v.reshape(B, rows, GRID_W, NA_HEADS, HEAD_DIM)

    def row_fn(r):
        r0 = jnp.clip(r - kh // 2, 0, rows - kh)
        k_blk = lax.dynamic_slice_in_dim(kg, r0, kh, axis=1)[:, :, kc]
        v_blk = lax.dynamic_slice_in_dim(vg, r0, kh, axis=1)[:, :, kc]
        q_r = lax.dynamic_index_in_dim(qg, r, axis=1, keepdims=False)
        s = jnp.einsum('bcqhe,bycmhe->bhcqym', q_r, k_blk, preferred_element_type=jnp.float32)
        dy = r0 + jnp.arange(kh) - r
        bias = rel_bias[:, dy + NA_KH - 1][:, :, dx_idx]
        s = s + jnp.transpose(bias, (0, 2, 3, 1, 4))[None].astype(jnp.float32)
        s = jnp.where(col_ok[None, None, :, :, None, :], s, NEG_INF)
        pr = jax.nn.softmax(s, axis=(-2, -1))
        return jnp.einsum('bhcqym,bycmhe->bcqhe', pr, v_blk.astype(jnp.float32))

    out = lax.map(row_fn, jnp.arange(rows))
    return jnp.moveaxis(out, 0, 1).reshape(B, T, NA_WIDTH).astype(v.dtype)


def conv_ffn(h, w_gate, w_up, conv_w, conv_b, w_down):
    gp = jnp.pad(h @ w_gate, ((0, 0), (1, 1), (0, 0)))
    gc = gp[:, :-2] * conv_w[0] + gp[:, 1:-1] * conv_w[1] + gp[:, 2:] * conv_w[2] + conv_b
    return (jax.nn.gelu(gc) * (h @ w_up)) @ w_down


def setup_inputs(seed: int = 0) -> dict:
    key = jax.random.key(seed)
    ks = iter(jax.random.split(key, 32))
    nrm = lambda shape, scale: scale * jax.random.normal(next(ks), shape, jnp.float32)
    L, C = DEPTH, RW_WIDTH
    return {
        'x': nrm((BATCH, SEQ, D_MODEL), 1.0),
        'p': nrm((DEPTH, BATCH, SEQ, PLE_DIM), 1.0),
        'norm_mix': 1.0 + nrm((L, D_MODEL), 0.02),
        'w_in': nrm((L, D_MODEL, IN_COLS), D_MODEL ** -0.5),
        'rw_mu': jax.random.uniform(next(ks), (L, RW_COLS), jnp.float32),
        'rw_w0': jax.random.uniform(next(ks), (L, 2, C), jnp.float32, -5.0, 1.0),
        'rw_w2': nrm((L, 2, RW_DECAY_LORA, C), 0.1),
        'rw_a0': nrm((L, 2, C), 0.5),
        'rw_a2': nrm((L, 2, RW_AAA_LORA, C), 0.5 * RW_AAA_LORA ** -0.5),
        'rw_g2': nrm((L, RW_GATE_LORA, C), RW_GATE_LORA ** -0.5),
        'rw_k_k': 0.85 + nrm((L, C), 0.05),
        'rw_k_a': 1.0 + nrm((L, C), 0.05),
        'rw_r_k': nrm((L, RW_HEADS, HEAD_DIM), 0.1),
        'rw_lnx_g': 1.0 + nrm((L, C), 0.02),
        'rw_lnx_b': nrm((L, C), 0.02),
        'na_bias': nrm((L, NA_HEADS, 2 * NA_KH - 1, 2 * NA_KW - 1), 0.1),
        'w_br_a': nrm((L, RW_WIDTH, D_MODEL), RW_WIDTH ** -0.5),
        'w_br_b': nrm((L, DIL_OUT_WIDTH, D_MODEL), DIL_OUT_WIDTH ** -0.5),
        'w_br_c': nrm((L, NA_WIDTH, D_MODEL), NA_WIDTH ** -0.5),
        'w_out': nrm((L, D_MODEL, D_MODEL), D_MODEL ** -0.5),
        'norm_ffn': 1.0 + nrm((L, D_MODEL), 0.02),
        'w_ffn_gate': nrm((L, D_MODEL, D_FF), D_MODEL ** -0.5),
        'w_ffn_up': nrm((L, D_MODEL, D_FF), D_MODEL ** -0.5),
        'ffn_conv_w': nrm((L, 3, D_FF), 3 ** -0.5),
        'ffn_conv_b': nrm((L, D_FF), 0.02),
        'w_ffn_down': nrm((L, D_FF, D_MODEL), D_FF ** -0.5),
        'norm_ple': 1.0 + nrm((L, D_MODEL), 0.02),
        'w_ple_gate': nrm((L, D_MODEL, D_MODEL), D_MODEL ** -0.5),
        'w_ple': nrm((L, PLE_DIM, D_MODEL), PLE_DIM ** -0.5),
        'norm_final': 1.0 + nrm((D_MODEL,), 0.02),
    }


def reference(x, p, norm_mix, w_in, rw_mu, rw_w0, rw_w2, rw_a0, rw_a2, rw_g2, rw_k_k, rw_k_a,
              rw_r_k, rw_lnx_g, rw_lnx_b, na_bias, w_br_a, w_br_b, w_br_c, w_out, norm_ffn,
              w_ffn_gate, w_ffn_up, ffn_conv_w, ffn_conv_b, w_ffn_down, norm_ple, w_ple_gate,
              w_ple, norm_final):
    T = x.shape[1]
    cos, sin = rope_tables(T)
    for i in range(DEPTH):
        h = rmsnorm(x, norm_mix[i])
        rw_cols, dq, dk, dv, nq, nk, nv, gates = _split(h @ w_in[i], IN_SPLITS)
        ya = rwkv7_bidir(rw_cols, rw_mu[i], rw_w0[i], rw_w2[i], rw_a0[i], rw_a2[i], rw_g2[i],
                         rw_k_k[i], rw_k_a[i], rw_r_k[i], rw_lnx_g[i], rw_lnx_b[i])
        yb = dilated_attention(dq, dk, dv, cos, sin)
        yc = neighbourhood_attention(nq, nk, nv, na_bias[i])
        ga, gb, gc = jnp.split(jax.nn.sigmoid(gates), N_BRANCH, axis=-1)
        merged = ga * (ya @ w_br_a[i]) + gb * (yb @ w_br_b[i]) + gc * (yc @ w_br_c[i])
        x = x + merged @ w_out[i]
        h = rmsnorm(x, norm_ffn[i])
        x = x + conv_ffn(h, w_ffn_gate[i], w_ffn_up[i], ffn_conv_w[i], ffn_conv_b[i], w_ffn_down[i])
        h = rmsnorm(x, norm_ple[i])
        x = x + jax.nn.sigmoid(h @ w_ple_gate[i]) * (p[i] @ w_ple[i])
    return rmsnorm(x, norm_final)
```

```python
import os, sys, time
from contextlib import ExitStack
import numpy as np
import concourse.bass as bass
import concourse.mybir as mybir
from concourse.bass_utils import run_bass_kernel_spmd

F32 = mybir.dt.float32
BF16 = mybir.dt.bfloat16
AF = mybir.ActivationFunctionType
ALU = mybir.AluOpType
AX = mybir.AxisListType

EPOCH = 30000


class Res:
    __slots__ = ("name", "lw", "rd", "dsem", "dval")

    def __init__(self, name):
        self.name = name
        self.lw = None
        self.rd = {}
        self.dsem = None
        self.dval = 0


class _Rec:
    def __init__(self):
        self.call = None

    def __getattr__(self, name):
        def f(*a, **k):
            self.call = (name, a, k)
            return self
        return f


class Prog:
    ENG = ("tensor", "vector", "scalar", "gpsimd", "sync")

    def __init__(self, nc, stack):
        self.nc = nc
        self.stack = stack
        self.streams = {e: [] for e in self.ENG}
        self.cur = {}
        self.sems = {}
        self.seen = {e: {} for e in self.ENG}
        self.nsem = 0
        self.dma_final = {}
        for e in self.ENG:
            self._new_epoch(e)

    def _mksem(self, name):
        h = self.stack.enter_context(self.nc.semaphore(name))
        key = name
        self.sems[key] = h
        self.nsem += 1
        return key

    def _new_epoch(self, e):
        k = self._mksem("s_%s_%d" % (e, self.nsem))
        self.cur[e] = [k, 0]

    def res(self, name):
        return Res(name)

    def _deps(self, eng, reads, writes, skip_key=None):
        need = {}
        def add(dep):
            if dep is None:
                return
            k, v = dep
            if k == skip_key:
                return
            if need.get(k, 0) < v:
                need[k] = v
        for r in reads:
            if r.lw is not None and need.get(r.lw[0], 0) < r.lw[1]:
                need[r.lw[0]] = r.lw[1]
        for w in writes:
            add(w.lw)
            for k, v in w.rd.items():
                if need.get(k, 0) < v:
                    need[k] = v
        out = []
        seen = self.seen[eng]
        mykey = self.cur[eng][0]
        for k, v in need.items():
            if eng == "tensor" and k == mykey:
                continue
            if seen.get(k, 0) >= v:
                continue
            seen[k] = v
            out.append((k, v))
        return out

    def op(self, eng, fn, reads=(), writes=()):
        if self.cur[eng][1] >= EPOCH:
            self._new_epoch(eng)
        waits = self._deps(eng, reads, writes)
        cur = self.cur[eng]
        cur[1] += 1
        key, val = cur[0], cur[1]
        rec = _Rec()
        fn(rec)
        name, a, k = rec.call
        self.streams[eng].append((waits, (lambda e, name=name, a=a, k=k: getattr(e, name)(*a, **k)), key, 1))
        for w in writes:
            w.lw = (key, val)
            w.rd = {}
        for r in reads:
            if r not in writes:
                r.rd[key] = val

    def dma(self, eng, out, in_, reads=(), writes=(), owner=None):
        if owner is None:
            owner = writes[0] if writes else reads[0]
        if owner.dsem is None:
            owner.dsem = self._mksem("d_%s_%d" % (owner.name, self.nsem))
        key = owner.dsem
        waits = self._deps(eng, reads, writes, skip_key=key)
        owner.dval += 16
        val = owner.dval
        self.dma_final[key] = val

        def fn(e, out=out, in_=in_):
            return e.dma_start(out=out, in_=in_)
        self.streams[eng].append((waits, fn, key, 16))
        for w in writes:
            w.lw = (key, val)
            w.rd = {}
        for r in reads:
            if r not in writes:
                r.rd[key] = val

    def finish(self):
        waits = []
        for k, v in self.dma_final.items():
            if self.seen["sync"].get(k, 0) < v:
                waits.append((k, v))
        self.streams["sync"].append((waits, None, None, 0))

    def emit(self):
        nc = self.nc
        sems = self.sems
        streams = self.streams
        with nc.Block() as block:
            def mk(ename):
                def body(e):
                    for waits, fn, key, inc in streams[ename]:
                        for k, v in waits:
                            e.wait_ge(sems[k], v)
                        if fn is not None:
                            fn(e).then_inc(sems[key], inc)
                return body
            block.tensor(mk("tensor"))
            block.vector(mk("vector"))
            block.scalar(mk("scalar"))
            block.gpsimd(mk("gpsimd"))
            block.sync(mk("sync"))


NORM_EPS = 1e-6


def build_inproj(D, NT, NCOLS, CG=256, TB=256, nseg=1):
    KC = D // 128
    nc = bass.Bass("TRN2", target_bir_lowering=False)
    xT_all = nc.dram_tensor("xT", [nseg, D, NT], F32, kind="ExternalInput").ap()
    w = nc.dram_tensor("w", [D, NCOLS], F32, kind="ExternalInput").ap()
    g = nc.dram_tensor("g", [128, KC], F32, kind="ExternalInput").ap()
    zT_all = nc.dram_tensor("zT", [nseg, NCOLS, NT], F32, kind="ExternalOutput").ap()
    with ExitStack() as st:
        P = Prog(nc, st)
        sb = lambda name, shape, dt: st.enter_context(nc.sbuf_tensor("sb_" + name, shape, dt))
        ps = lambda name, shape, dt: st.enter_context(nc.psum_tensor("ps_" + name, shape, dt))
        hT = sb("hT", [128, KC, NT], BF16); r_hT = P.res("hT")
        gs = sb("gs", [128, KC], F32); r_gs = P.res("gs")
        ones = sb("ones", [128, 128], F32); r_ones = P.res("ones")
        xb = [sb("xb%d" % i, [128, KC, TB], F32) for i in range(2)]
        r_xb = [P.res("xb%d" % i) for i in range(2)]
        sq = [sb("sq%d" % i, [128, KC, TB], F32) for i in range(2)]
        r_sq = [P.res("sq%d" % i) for i in range(2)]
        rs = [sb("rs%d" % i, [128, TB], F32) for i in range(2)]
        r_rs = [P.res("rs%d" % i) for i in range(2)]
        wf = [sb("wf%d" % i, [128, KC, CG], F32) for i in range(2)]
        r_wf = [P.res("wf%d" % i) for i in range(2)]
        wb = [sb("wb%d" % i, [128, KC, CG], BF16) for i in range(2)]
        r_wb = [P.res("wb%d" % i) for i in range(2)]
        ob = [sb("ob%d" % i, [128, NT], F32) for i in range(2)]
        r_ob = [P.res("ob%d" % i) for i in range(2)]
        pn = ps("pn", [128, 512], F32); r_pn = P.res("pn")
        pm = [ps("pm%d" % i, [128, 512], F32) for i in range(4)]
        r_pm = [P.res("pm%d" % i) for i in range(4)]

        P.dma("sync", gs[:, :], g[:, :], writes=[r_gs])
        P.op("vector", lambda e: e.memset(ones[:, :], 1.0), writes=[r_ones])
        wv = w.rearrange("(c p) n -> p c n", p=128)
        oc = 0
        pmi = 0
        for sg in range(nseg):
            xTv = xT_all[sg].rearrange("(c p) t -> p c t", p=128)
            zT = zT_all[sg]
            for tb in range(NT // TB):
                i = tb % 2
                P.dma("sync" if i == 0 else "gpsimd", xb[i][:, :, :], xTv[:, :, tb * TB:(tb + 1) * TB], writes=[r_xb[i]])
                P.op("scalar", lambda e, i=i: e.activation(out=sq[i][:, :, :], in_=xb[i][:, :, :], func=AF.Square),
                     reads=[r_xb[i]], writes=[r_sq[i]])
                for kc in range(KC):
                    P.op("tensor", lambda e, i=i, kc=kc: e.matmul(pn[:, 0:TB], ones[:, :], sq[i][:, kc, :],
                                                                   start=(kc == 0), stop=(kc == KC - 1)),
                         reads=[r_ones, r_sq[i]], writes=[r_pn])
                P.op("scalar", lambda e, i=i: e.activation(out=rs[i][:, :], in_=pn[:, 0:TB], func=AF.Sqrt,
                                                           scale=1.0 / D, bias=NORM_EPS),
                     reads=[r_pn], writes=[r_rs[i]])
                P.op("vector", lambda e, i=i: e.reciprocal(out=rs[i][:, :], in_=rs[i][:, :]),
                     reads=[r_rs[i]], writes=[r_rs[i]])
                for kc in range(KC):
                    P.op("vector", lambda e, i=i, kc=kc, tb=tb: e.scalar_tensor_tensor(
                        out=hT[:, kc, tb * TB:(tb + 1) * TB], in0=xb[i][:, kc, :], scalar=gs[:, kc:kc + 1],
                        op0=ALU.mult, in1=rs[i][:, :], op1=ALU.mult),
                        reads=[r_xb[i], r_gs, r_rs[i]], writes=[r_hT])
            ncg = (NCOLS + CG - 1) // CG
            for cg in range(ncg):
                i = cg % 2
                c0 = cg * CG
                cw = min(CG, NCOLS - c0)
                P.dma("sync" if i == 0 else "gpsimd", wf[i][:, :, 0:cw], wv[:, :, c0:c0 + cw], writes=[r_wf[i]])
                P.op("gpsimd" if i == 0 else "vector",
                     lambda e, i=i, cw=cw: e.tensor_copy(out=wb[i][:, :, 0:cw], in_=wf[i][:, :, 0:cw]),
                     reads=[r_wf[i]], writes=[r_wb[i]])
                for s0 in range(0, cw, 128):
                    sw = min(128, cw - s0)
                    o = oc % 2
                    oc += 1
                    for t4 in range(NT // 512):
                        b = pmi % 4
                        pmi += 1
                        for kc in range(KC):
                            P.op("tensor", lambda e, i=i, kc=kc, s0=s0, sw=sw, t4=t4, b=b: e.matmul(
                                pm[b][0:sw, :], wb[i][:, kc, s0:s0 + sw], hT[:, kc, t4 * 512:(t4 + 1) * 512],
                                start=(kc == 0), stop=(kc == KC - 1)),
                                reads=[r_wb[i], r_hT], writes=[r_pm[b]])
                        if t4 % 2 == 0:
                            P.op("scalar", lambda e, o=o, sw=sw, t4=t4, b=b: e.activation(
                                out=ob[o][0:sw, t4 * 512:(t4 + 1) * 512], in_=pm[b][0:sw, :], func=AF.Copy),
                                reads=[r_pm[b]], writes=[r_ob[o]])
                        else:
                            P.op("vector", lambda e, o=o, sw=sw, t4=t4, b=b: e.tensor_copy(
                                out=ob[o][0:sw, t4 * 512:(t4 + 1) * 512], in_=pm[b][0:sw, :]),
                                reads=[r_pm[b]], writes=[r_ob[o]])
                    P.dma("sync", zT[c0 + s0:c0 + s0 + sw, :], ob[o][0:sw, :], reads=[r_ob[o]])
        P.finish()
        P.emit()
    return nc


SCALE = 64 ** -0.5
NTQ = 2048
WIN = 4096
NQB = NTQ // 128
KOFF = 8
DIL = ((128, 1), (512, 4), (2048, 16))
NEGFILL = -30000.0


def dil_deltas():
    out = []
    for g, (window, dil) in enumerate(DIL):
        r = (64 * dil) // 128 if dil > 1 else 1
        out.append(list(range(-r, r + 1)))
    return out


def dil_masks():
    ms = []
    kk = np.arange(128)[:, None]
    qq = np.arange(128)[None, :]
    for g, (window, dil) in enumerate(DIL):
        for d in dil_deltas()[g]:
            diff = (kk + 128 * d) - qq
            ok = (diff % dil == 0) & (np.abs(diff) <= 64 * dil)
            ms.append(ok.astype(np.float32))
    return np.ascontiguousarray(np.stack(ms, axis=1))


def na_tables(na_bias_h, m_list):
    out = np.full((128, len(m_list) * 7, 128), NEGFILL, np.float32)
    for mi, m in enumerate(m_list):
        q = m * 128 + np.arange(128)
        qr, qc = q // 64, q % 64
        r0 = np.clip(qr - 4, 0, 56)
        wc0 = np.clip(qc - 8, 0, 48)
        for di, d in enumerate(range(-3, 4)):
            kt = m + d
            if kt < 0 or kt >= 32:
                continue
            k = kt * 128 + np.arange(128)
            kr, kc = k // 64, k % 64
            ok = ((kr[:, None] >= r0[None, :]) & (kr[:, None] < r0[None, :] + 8)
                  & (kc[:, None] >= wc0[None, :]) & (kc[:, None] < wc0[None, :] + 16))
            dy = np.clip(kr[:, None] - qr[None, :], -7, 7) + 7
            dx = np.clip(kc[:, None] - qc[None, :], -15, 15) + 15
            vals = na_bias_h[dy, dx]
            out[:, mi * 7 + di, :] = np.where(ok, vals, NEGFILL)
    return out


def na_slot_of_qb(qb):
    if qb < 2:
        return qb
    if qb >= NQB - 2:
        return 3 + (qb - (NQB - 2))
    return 2


def na_slot_m(half):
    base = half * NQB
    return [base + 0, base + 1, base + 2, base + NQB - 2, base + NQB - 1]


def emit_attn(nc, P, st, io, n_hg=4, n_na=12, n_qb=NQB):
    sb = lambda name, shape, dt: st.enter_context(nc.sbuf_tensor("sb_" + name, shape, dt))
    ps = lambda name, shape, dt: st.enter_context(nc.psum_tensor("ps_" + name, shape, dt))
    cq = sb("cq", [64, NTQ], F32); r_cq = P.res("cq")
    sq_ = sb("sq_", [64, NTQ], F32); r_sq = P.res("sq_")
    ck = sb("ck", [64, WIN], F32); r_ck = P.res("ck")
    sk = sb("sk", [64, WIN], F32); r_sk = P.res("sk")
    P.dma("sync", cq[:, :], io["cq"][:, :], writes=[r_cq])
    P.dma("sync", sq_[:, :], io["sq"][:, :], writes=[r_sq])
    P.dma("gpsimd", ck[:, :], io["ck"][:, :], writes=[r_ck])
    P.dma("gpsimd", sk[:, :], io["sk"][:, :], writes=[r_sk])
    nef = sb("nef", [128, 25 * 128], F32); r_nef = P.res("nef")
    nebs = [sb("neb%d" % i, [128, 7 * 128], BF16) for i in range(5)]; r_nebs = [P.res("neb%d" % i) for i in range(5)]
    mdf = nef[:, 0:25 * 128]; r_mdf = r_nef
    md = sb("md", [128, 25 * 128], BF16); r_md = P.res("md")
    P.dma("sync", mdf, io["md"].rearrange("p a q -> p (a q)"), writes=[r_mdf])
    P.op("vector", lambda e: e.tensor_copy(out=md[:, :], in_=mdf), reads=[r_mdf], writes=[r_md])

    stq = sb("stq", [64, NTQ], F32); r_stq = P.res("stq")
    stqp = sb("stqp", [64, NTQ], F32); r_stqp = P.res("stqp")
    stk = sb("stk", [64, WIN], F32); r_stk = P.res("stk")
    stkp = sb("stkp", [64, WIN], F32); r_stkp = P.res("stkp")
    stv = sb("stv", [128, 32, 65], F32); r_stv = P.res("stv")
    qb_ = [sb("qb%d" % i, [64, NTQ], BF16) for i in range(3)]; r_qb = [P.res("qb%d" % i) for i in range(3)]
    kb_ = [sb("kb%d" % i, [64, WIN], BF16) for i in range(3)]; r_kb = [P.res("kb%d" % i) for i in range(3)]
    vb_ = [sb("vb%d" % i, [128, 32, 65], BF16) for i in range(3)]; r_vb = [P.res("vb%d" % i) for i in range(3)]
    pss = [ps("pss%d" % i, [128, 512], F32) for i in range(2)]; r_pss = [P.res("pss%d" % i) for i in range(2)]
    pso = [ps("pso%d" % i, [128, 512], F32) for i in range(2)]; r_pso = [P.res("pso%d" % i) for i in range(2)]
    pe = [sb("pe%d" % i, [128, 512], BF16) for i in range(2)]; r_pe = [P.res("pe%d" % i) for i in range(2)]
    pm = [sb("pm%d" % i, [128, 512], BF16) for i in range(2)]; r_pm = [P.res("pm%d" % i) for i in range(2)]
    rec = sb("rec", [128, 2], F32); r_rec = [P.res("rec0"), P.res("rec1")]
    yh = [sb("yh%d" % i, [128, NQB, 64], F32) for i in range(2)]; r_yh = [P.res("yh%d" % i) for i in range(2)]
    cnt = {"c": 0, "o": 0}

    def load_head(slot, qsrc, qpsrc, ksrc, kpsrc, vsrc, rope):
        P.dma("sync", stq[:, :], qsrc, writes=[r_stq])
        P.dma("sync", stk[:, :], ksrc, writes=[r_stk])
        P.dma("gpsimd", stv[:, :, :], vsrc, writes=[r_stv])
        if rope:
            P.dma("gpsimd", stqp[0:32, :], qsrc[32:64, :], writes=[r_stqp])
            P.dma("gpsimd", stqp[32:64, :], qsrc[0:32, :], writes=[r_stqp])
            P.dma("gpsimd", stkp[0:32, :], ksrc[32:64, :], writes=[r_stkp])
            P.dma("gpsimd", stkp[32:64, :], ksrc[0:32, :], writes=[r_stkp])
            P.op("vector", lambda e: e.tensor_tensor(out=stq[:, :], in0=stq[:, :], in1=cq[:, :], op=ALU.mult),
                 reads=[r_stq, r_cq], writes=[r_stq])
            P.op("gpsimd", lambda e: e.tensor_tensor(out=stqp[:, :], in0=stqp[:, :], in1=sq_[:, :], op=ALU.mult),
                 reads=[r_stqp, r_sq], writes=[r_stqp])
            P.op("vector", lambda e: e.tensor_tensor(out=qb_[slot][:, :], in0=stq[:, :], in1=stqp[:, :], op=ALU.add),
                 reads=[r_stq, r_stqp], writes=[r_qb[slot]])
            P.op("vector", lambda e: e.tensor_tensor(out=stk[:, :], in0=stk[:, :], in1=ck[:, :], op=ALU.mult),
                 reads=[r_stk, r_ck], writes=[r_stk])
            P.op("gpsimd", lambda e: e.tensor_tensor(out=stkp[:, :], in0=stkp[:, :], in1=sk[:, :], op=ALU.mult),
                 reads=[r_stkp, r_sk], writes=[r_stkp])
            P.op("vector", lambda e: e.tensor_tensor(out=kb_[slot][:, :], in0=stk[:, :], in1=stkp[:, :], op=ALU.add),
                 reads=[r_stk, r_stkp], writes=[r_kb[slot]])
        else:
            P.op("vector", lambda e: e.tensor_copy(out=qb_[slot][:, :], in_=stq[:, :]), reads=[r_stq], writes=[r_qb[slot]])
            P.op("gpsimd", lambda e: e.tensor_copy(out=kb_[slot][:, :], in_=stk[:, :]), reads=[r_stk], writes=[r_kb[slot]])
        P.op("gpsimd", lambda e: e.tensor_copy(out=vb_[slot][:, :, :], in_=stv[:, :, :]), reads=[r_stv], writes=[r_vb[slot]])

    def qblock(qb, chunks, E, r_E, ydst, r_y):
        o = cnt["o"] % 2
        cnt["o"] += 1
        ntile = sum(len(c) for c in chunks)
        ti = 0
        for ch in chunks:
            c = cnt["c"] % 2
            cnt["c"] += 1
            n = len(ch)
            for j, (s, kt, ei) in enumerate(ch):
                P.op("tensor", lambda e, c=c, j=j, s=s, kt=kt: e.matmul(
                    pss[c][:, j * 128:(j + 1) * 128], kb_[s][:, kt * 128:(kt + 1) * 128],
                    qb_[s][:, qb * 128:(qb + 1) * 128], start=True, stop=True),
                    reads=[r_kb[s], r_qb[s]], writes=[r_pss[c]])
            P.op("scalar", lambda e, c=c, n=n: e.activation(out=pe[c][:, 0:n * 128], in_=pss[c][:, 0:n * 128],
                                                           func=AF.Exp, scale=SCALE),
                 reads=[r_pss[c]], writes=[r_pe[c]])
            e0 = ch[0][2]
            P.op("vector" if c == 0 else "gpsimd", lambda e, c=c, n=n, e0=e0: e.tensor_tensor(
                out=pm[c][:, 0:n * 128], in0=pe[c][:, 0:n * 128], in1=E[:, e0 * 128:(e0 + n) * 128], op=ALU.mult),
                reads=[r_pe[c], r_E], writes=[r_pm[c]])
            for j, (s, kt, ei) in enumerate(ch):
                P.op("tensor", lambda e, c=c, j=j, s=s, kt=kt, ti=ti: e.matmul(
                    pso[o][:, 0:65], pm[c][:, j * 128:(j + 1) * 128], vb_[s][:, kt, :],
                    start=(ti == 0), stop=(ti == ntile - 1)),
                    reads=[r_pm[c], r_vb[s]], writes=[r_pso[o]])
                ti += 1
        P.op("vector", lambda e, o=o: e.reciprocal(out=rec[:, o:o + 1], in_=pso[o][:, 64:65]),
             reads=[r_pso[o]], writes=[r_rec[o]])
        P.op("vector", lambda e, o=o: e.tensor_scalar(out=ydst, in0=pso[o][:, 0:64], scalar1=rec[:, o:o + 1],
                                                      scalar2=None, op0=ALU.mult),
             reads=[r_pso[o], r_rec[o]], writes=[r_y])

    dd = dil_deltas()
    ebase = [0, 3, 8]
    for hg in range(n_hg):
        for g in range(3):
            h = g * 4 + hg
            load_head(g, io["dq"][h], None, io["dk"][h], None, io["dv"][h], True)
        for qb in range(n_qb):
            chunks = []
            for g in range(3):
                tl = [(g, KOFF + qb + d, ebase[g] + di) for di, d in enumerate(dd[g])]
                for a in range(0, len(tl), 4):
                    chunks.append(tl[a:a + 4])
            qblock(qb, chunks, md, r_md, yh[hg % 2][:, qb, :], r_yh[hg % 2])
        P.dma("sync", io["yb"].rearrange("(qb p) c -> p qb c", p=128)[:, :, hg * 64:(hg + 1) * 64], yh[hg % 2][:, :, :],
              reads=[r_yh[hg % 2]])
    for h in range(n_na):
        load_head(0, io["nq"][h], None, io["nk"][h], None, io["nv"][h], False)
        for sl_ in range(5):
            P.dma("sync", nef[:, 0:7 * 128], io["ne"][h][:, sl_ * 7:(sl_ + 1) * 7, :].rearrange("p a q -> p (a q)"), writes=[r_nef])
            P.op("scalar", lambda e, sl_=sl_: e.activation(out=nebs[sl_][:, :], in_=nef[:, 0:7 * 128], func=AF.Exp),
                 reads=[r_nef], writes=[r_nebs[sl_]])
        for qb in range(n_qb):
            sl = na_slot_of_qb(qb)
            tl = [(0, KOFF + qb + d, di) for di, d in enumerate(range(-3, 4))]
            chunks = [tl[0:4], tl[4:7]]
            qblock(qb, chunks, nebs[sl], r_nebs[sl], yh[h % 2][:, qb, :], r_yh[h % 2])
        P.dma("sync", io["yc"].rearrange("(qb p) c -> p qb c", p=128)[:, :, h * 64:(h + 1) * 64], yh[h % 2][:, :, :],
              reads=[r_yh[h % 2]])


def build_attn(**kw):
    nc = bass.Bass("TRN2", target_bir_lowering=False)
    di = lambda name, shape: nc.dram_tensor(name, shape, F32, kind="ExternalInput").ap()
    io = {
        "cq": di("cq", [64, NTQ]), "sq": di("sq", [64, NTQ]), "ck": di("ck", [64, WIN]), "sk": di("sk", [64, WIN]),
        "md": di("md", [128, 25, 128]),
        "dq": di("dq", [12, 64, NTQ]),
        "dk": di("dk", [12, 64, WIN]), "dv": di("dv", [12, 128, 32, 65]),
        "nq": di("nq", [12, 64, NTQ]), "nk": di("nk", [12, 64, WIN]), "nv": di("nv", [12, 128, 32, 65]),
        "ne": di("ne", [12, 128, 35, 128]),
        "yb": nc.dram_tensor("yb", [NTQ, 256], F32, kind="ExternalOutput").ap(),
        "yc": nc.dram_tensor("yc", [NTQ, 768], F32, kind="ExternalOutput").ap(),
    }
    with ExitStack() as st:
        P = Prog(nc, st)
        emit_attn(nc, P, st, io, **kw)
        P.finish()
        P.emit()
    return nc


def rope_consts():
    inv = 10000.0 ** (-np.arange(0, 64, 2, dtype=np.float32) / 64)
    ang = np.arange(4096, dtype=np.float32)[:, None] * inv[None, :]
    cos, sin = np.cos(ang).astype(np.float32), np.sin(ang).astype(np.float32)
    C = np.concatenate([cos, cos], axis=1).T
    S = np.concatenate([-sin, sin], axis=1).T
    return np.ascontiguousarray(C), np.ascontiguousarray(S)


def window(a, half, axis):
    t0 = half * NTQ
    lo, hi = t0 - 1024, t0 + 3072
    pad_lo, pad_hi = max(0, -lo), max(0, hi - 4096)
    sl = [slice(None)] * a.ndim
    sl[axis] = slice(max(lo, 0), min(hi, 4096))
    b = a[tuple(sl)]
    pw = [(0, 0)] * a.ndim
    pw[axis] = (pad_lo, pad_hi)
    return np.pad(b, pw)


def attn_inputs(dq, dk, dv, nq, nk, nv, na_bias, half):
    C, S = rope_consts()
    T = 4096
    t0 = half * NTQ
    perm = np.concatenate([np.arange(32, 64), np.arange(0, 32)])
    def fm(a):
        return a.reshape(T, 12, 64).transpose(1, 2, 0)
    def vaug(v):
        vh = v.reshape(T, 12, 64).transpose(1, 0, 2)
        va = np.concatenate([vh, np.ones((12, T, 1), np.float32)], axis=2)
        vw = window(va, half, 1)
        return np.ascontiguousarray(vw.reshape(12, 32, 128, 65).transpose(0, 2, 1, 3))
    dqf, dkf, nqf, nkf = fm(dq), fm(dk), fm(nq), fm(nk)
    m = {
        "cq": np.ascontiguousarray(C[:, t0:t0 + NTQ]), "sq": np.ascontiguousarray(S[:, t0:t0 + NTQ]),
        "ck": np.ascontiguousarray(window(C, half, 1)), "sk": np.ascontiguousarray(window(S, half, 1)),
        "md": dil_masks(),
        "dq": np.ascontiguousarray(dqf[:, :, t0:t0 + NTQ]),
        "dk": np.ascontiguousarray(window(dkf, half, 2)),
        "dv": vaug(dv),
        "nq": np.ascontiguousarray(nqf[:, :, t0:t0 + NTQ]), "nk": np.ascontiguousarray(window(nkf, half, 2)),
        "nv": vaug(nv),
        "ne": np.ascontiguousarray(np.stack([na_tables(na_bias[h], na_slot_m(half)) for h in range(12)])),
    }
    return m


T = 4096
L = 64
SEG = 256
NC = SEG // L
NSEG = T // SEG
NST = NC * 4
NBK = NST // 8
C0 = -float(np.exp(-0.5))
LNX_EPS = 64e-5


def emit_rwkv(nc, P, st, io, nseg=NSEG, dirs=(0, 1), stage=9):
    sb = lambda name, shape, dt=F32: st.enter_context(nc.sbuf_tensor("sb_" + name, shape, dt))
    ps = lambda name, shape, dt=F32: st.enter_context(nc.psum_tensor("ps_" + name, shape, dt))
    V = lambda fn, r, w: P.op("vector", fn, reads=r, writes=w)
    G = lambda fn, r, w: P.op("gpsimd", fn, reads=r, writes=w)
    A = lambda fn, r, w: P.op("scalar", fn, reads=r, writes=w)
    M = lambda fn, r, w: P.op("tensor", fn, reads=r, writes=w)

    def tile(name, shape, dt=F32):
        return sb(name, shape, dt), P.res(name)

    ident, r_ident = tile("ident", [128, 128])
    icat, r_icat = tile("icat", [128, 64])
    bones, r_bones = tile("bones", [128, 128])
    msk, r_msk = tile("msk", [64, 4, 64])
    rmask, r_rmask = tile("rmask", [128, SEG])
    prm, r_prm = tile("prm", [128, 32])
    mul_, r_mul = tile("mul", [128, 3])
    w2s, r_w2s = tile("w2s", [128, 512])
    a2s, r_a2s = tile("a2s", [128, 512])
    g2s, r_g2s = tile("g2s", [128, 256])
    hm, r_hm = tile("hm", [128, 2])
    lng, r_lng = tile("lng", [64, 256])
    lnb, r_lnb = tile("lnb", [64, 256])
    for (t_, r_, src) in ((ident, r_ident, "ident"), (icat, r_icat, "icat"), (bones, r_bones, "bones"),
                          (rmask, r_rmask, "rmask"), (prm, r_prm, "prm"), (mul_, r_mul, "mul"),
                          (w2s, r_w2s, "w2s"), (a2s, r_a2s, "a2s"), (g2s, r_g2s, "g2s"), (hm, r_hm, "hm"),
                          (lng, r_lng, "lng"), (lnb, r_lnb, "lnb")):
        P.dma("sync", t_[:, :], io[src][:, :], writes=[r_])
    P.dma("sync", msk[:, :, :], io["msk"][:, :, :], writes=[r_msk])
    V(lambda e: e.tensor_scalar(out=prm[:, 18:20], in0=prm[:, 16:18], scalar1=-1.0, scalar2=1.0, op0=ALU.mult, op1=ALU.add),
      [r_prm], [r_prm])
    pc_col = lambda base, pc: prm[:, base + pc:base + pc + 1]

    SP = SEG + 2
    raw = {}
    for nm in ("r0", "r1", "k0", "k1", "v0", "v1"):
        raw[nm] = tile("raw_" + nm, [128, SP])
    raw["wd"] = tile("raw_wd", [64, SP]); raw["ad"] = tile("raw_ad", [64, SP]); raw["gd"] = tile("raw_gd", [96, SP])
    shf = {}
    for nm in ("r0", "r1", "k0", "k1", "v0", "v1"):
        shf[nm] = tile("shf_" + nm, [128, SEG])
    shf["wd"] = tile("shf_wd", [64, SEG]); shf["ad"] = tile("shf_ad", [128, SEG]); shf["gd"] = tile("shf_gd", [96, SEG])
    tmpa, r_tmpa = tile("tmpa", [128, SEG]); tmpb, r_tmpb = tile("tmpb", [128, SEG])
    tw, r_tw = tile("tw", [128, SEG])
    sgd, r_sgd = tile("sgd", [128, SEG])
    fm = {}
    for nm in ("lw", "lr", "kk", "kd", "bb", "cs", "ci", "ce", "e1", "e2", "e3", "e4", "t1", "t2", "lr0", "kd0", "bon"):
        fm[nm] = tile("fm_" + nm, [128, SEG])
    tot, r_tot = tile("tot", [128, NC]); WL, r_WL = tile("WL", [128, NC])
    outf = {}
    for nm in ("Af", "Bf", "Kf", "Rf", "Bhf", "Khf"):
        for pc in range(2):
            outf[nm, pc] = tile("of_%s%d" % (nm, pc), [128, SEG])
    Dg = [tile("Dg%d" % pc, [128, NC, 64]) for pc in range(2)]
    XT = {}
    for nm in ("At", "Bht", "Kht", "Vt", "bont"):
        XT[nm] = tile("xt_" + nm, [128, NC, 256])
    gat, r_gat = tile("gat", [64, NC, 256])
    if os.environ.get("PADKB"):
        tile("padx", [128, 256 * int(os.environ["PADKB"])])
    Wd = [tile("wide%d" % i, [128, NST * 64]) for i in range(10)]
    Sst, r_Sst = tile("Sst", [128, NC + 1, 256])
    mskd = {}
    for nm in ("Af", "Bf", "Rf"):
        for pc in range(2):
            for hh in range(2):
                mskd[nm, pc, hh] = tile("mk_%s%d%d" % (nm, pc, hh), [128, SEG])
    Dgm = {(pc, hh): tile("Dgm%d%d" % (pc, hh), [128, NC, 64]) for pc in range(2) for hh in range(2)}
    for (t_, r_) in [Wd[i] for i in range(10)] + [XT[k] for k in XT]:
        G(lambda e, t_=t_: e.memset(t_[:], 0.0), [], [r_])
    for (t_, r_) in ((tw, r_tw), (sgd, r_sgd), shf["ad"]):
        G(lambda e, t_=t_: e.memset(t_[:, :], 0.0), [], [r_])
    V(lambda e: e.memset(Sst[:, :, :], 0.0), [], [r_Sst])
    yfb, r_yfb = tile("yfb", [64, NC, 256])
    yo, r_yo = tile("yo", [64, NC, 256])
    stat, r_stat = tile("stat", [64, NC * 4 * 2])
    pbank = [(ps("pb%d" % i, [128, 512]), P.res("pb%d" % i)) for i in range(8)]
    bk = {"i": 0}

    def nextbank():
        b = pbank[bk["i"] % 8]
        bk["i"] += 1
        return b

    def shift(nm, rows, mu_ap, r_mu):
        (rw, r_rw), (o, r_o) = raw[nm], shf[nm]
        G(lambda e: e.tensor_tensor(out=tmpa[0:rows, :], in0=rw[0:rows, 0:SEG], in1=rw[0:rows, 2:SEG + 2], op=ALU.add),
          [r_rw], [r_tmpa])
        V(lambda e: e.scalar_tensor_tensor(out=tmpb[0:rows, :], in0=tmpa[0:rows, :], scalar=0.5, op0=ALU.mult,
                                           in1=rw[0:rows, 1:SEG + 1], op1=ALU.subtract), [r_tmpa, r_rw], [r_tmpb])
        V(lambda e: e.scalar_tensor_tensor(out=o[0:rows, :], in0=tmpb[0:rows, :], scalar=mu_ap, op0=ALU.mult,
                                           in1=rw[0:rows, 1:SEG + 1], op1=ALU.add), [r_tmpb, r_rw, r_mu], [r_o])

    def lora_sig(d, pc, src, r_src, wts, r_wts, bias_base, out, r_out):
        pb, r_pb = nextbank()
        M(lambda e: e.matmul(pb[:, 0:SEG], wts[:, d * 256 + pc * 128:d * 256 + (pc + 1) * 128],
                             src[:, :], start=True, stop=True), [r_wts, r_src], [r_pb])
        A(lambda e: e.activation(out=out[:, :], in_=pb[:, 0:SEG], func=AF.Sigmoid,
                                 bias=prm[:, bias_base + d * 2 + pc:bias_base + d * 2 + pc + 1]), [r_pb, r_prm], [r_out])

    def kd_from(pc, lr_t, r_lr, out, r_out):
        kp, r_kp = shf["k%d" % pc]
        V(lambda e: e.tensor_scalar(out=fm["t1"][0][:, :], in0=lr_t[:, :], scalar1=pc_col(16, pc), scalar2=pc_col(18, pc),
                                    op0=ALU.mult, op1=ALU.add), [r_lr, r_prm], [fm["t1"][1]])
        V(lambda e: e.tensor_tensor(out=out[:, :], in0=fm["t1"][0][:, :], in1=kp[:, :], op=ALU.mult),
          [fm["t1"][1], r_kp], [r_out])

    yf_res = P.res("yf_dram")

    for d in dirs:
        segs = list(range(nseg)) if d == 0 else list(range(nseg - 1, -1, -1))
        V(lambda e: e.memset(Sst[0:64, 0, :], 0.0), [], [r_Sst])
        for s in segs:
            s0 = s * SEG
            for wi, wn in enumerate(("r", "k", "v")):
                for pc in range(2):
                    t_, r_ = raw["%s%d" % (wn, pc)]
                    P.dma("sync" if pc == 0 else "gpsimd", t_[:, :], io["rkv"][wi, pc * 128:(pc + 1) * 128, s0:s0 + SP], writes=[r_])
            for nm, rows in (("wd", 64), ("ad", 64), ("gd", 96)):
                t_, r_ = raw[nm]
                P.dma("sync", t_[:, :], io[nm][:, s0:s0 + SP], writes=[r_])
            for wi, wn in enumerate(("r", "k", "v")):
                for pc in range(2):
                    shift("%s%d" % (wn, pc), 128, prm[:, wi * 2 + pc:wi * 2 + pc + 1], r_prm)
            shift("wd", 64, mul_[0:64, 0:1], r_mul)
            shift("ad", 64, mul_[0:64, 1:2], r_mul)
            if d == 1:
                shift("gd", 96, mul_[0:96, 2:3], r_mul)
            if stage < 2:
                continue
            A(lambda e: e.activation(out=tw[0:64, :], in_=shf["wd"][0][:, :], func=AF.Tanh), [shf["wd"][1]], [r_tw])
            for pc in range(2):
                kp, r_kp = shf["k%d" % pc]
                rp, r_rp = shf["r%d" % pc]
                F = lambda nm: fm[nm][0]
                Rr = lambda nm: fm[nm][1]
                lora_sig(d, pc, tw, r_tw, w2s, r_w2s, 6, F("lw"), Rr("lw"))
                lora_sig(d, pc, shf["ad"][0], shf["ad"][1], a2s, r_a2s, 10, F("lr"), Rr("lr"))
                V(lambda e: e.tensor_scalar(out=F("lw")[:, :], in0=F("lw")[:, :], scalar1=C0, scalar2=None, op0=ALU.mult),
                  [Rr("lw")], [Rr("lw")])
                V(lambda e: e.tensor_scalar(out=F("t1")[:, :], in0=kp[:, :], scalar1=pc_col(14, pc), scalar2=None, op0=ALU.mult),
                  [r_kp, r_prm], [Rr("t1")])
                A(lambda e: e.activation(out=F("t2")[:, :], in_=F("t1")[:, :], func=AF.Square), [Rr("t1")], [Rr("t2")])
                pb, r_pb = nextbank()
                M(lambda e, pb=pb: e.matmul(pb[:, 0:SEG], bones[:, :], F("t2")[:, :], start=True, stop=True),
                  [r_bones, Rr("t2")], [r_pb])
                A(lambda e, pb=pb: e.activation(out=F("t2")[:, :], in_=pb[:, 0:SEG], func=AF.Sqrt), [r_pb], [Rr("t2")])
                V(lambda e: e.tensor_scalar(out=F("t2")[:, :], in0=F("t2")[:, :], scalar1=1e-12, scalar2=None, op0=ALU.max),
                  [Rr("t2")], [Rr("t2")])
                V(lambda e: e.reciprocal(out=F("t2")[:, :], in_=F("t2")[:, :]), [Rr("t2")], [Rr("t2")])
                V(lambda e: e.tensor_tensor(out=F("kk")[:, :], in0=F("t1")[:, :], in1=F("t2")[:, :], op=ALU.mult),
                  [Rr("t1"), Rr("t2")], [Rr("kk")])
                kd_from(pc, F("lr"), Rr("lr"), F("kd"), Rr("kd"))
                G(lambda e: e.tensor_tensor(out=F("bb")[:, :], in0=F("kk")[:, :], in1=F("lr")[:, :], op=ALU.mult),
                  [Rr("kk"), Rr("lr")], [Rr("bb")])
                V(lambda e: e.tensor_tensor_scan(out=F("cs")[:, :], data0=rmask[:, :], data1=F("lw")[:, :], initial=0.0,
                                                 op0=ALU.mult, op1=ALU.add), [r_rmask, Rr("lw")], [Rr("cs")])
                cs3 = F("cs")[:, :].rearrange("p (c l) -> p c l", l=L)
                V(lambda e: e.tensor_copy(out=tot[:, :].unsqueeze(2), in_=cs3[:, :, L - 1:L]), [Rr("cs")], [r_tot])
                totbc = tot[:, :].unsqueeze(2).broadcast_to([128, NC, L])
                v3 = lambda t_: t_[:, :].rearrange("p (c l) -> p c l", l=L)
                if d == 0:
                    ci, r_ci = F("cs"), Rr("cs")
                else:
                    ci, r_ci = F("ci"), Rr("ci")
                    V(lambda e: e.tensor_tensor(out=F("t1")[:, :], in0=F("lw")[:, :], in1=F("cs")[:, :], op=ALU.subtract),
                      [Rr("lw"), Rr("cs")], [Rr("t1")])
                    V(lambda e: e.tensor_tensor(out=v3(F("ci")), in0=v3(F("t1")), in1=totbc, op=ALU.add),
                      [Rr("t1"), r_tot], [Rr("ci")])
                V(lambda e: e.tensor_tensor(out=F("ce")[:, :], in0=ci[:, :], in1=F("lw")[:, :], op=ALU.subtract),
                  [r_ci, Rr("lw")], [Rr("ce")])
                V(lambda e: e.tensor_tensor(out=v3(F("t2")), in0=totbc, in1=v3(ci), op=ALU.subtract),
                  [r_ci, r_tot], [Rr("t2")])
                A(lambda e: e.activation(out=F("e1")[:, :], in_=F("ce")[:, :], func=AF.Exp), [Rr("ce")], [Rr("e1")])
                A(lambda e: e.activation(out=F("e2")[:, :], in_=ci[:, :], func=AF.Exp, scale=-1.0), [r_ci], [Rr("e2")])
                A(lambda e: e.activation(out=F("e3")[:, :], in_=ci[:, :], func=AF.Exp), [r_ci], [Rr("e3")])
                A(lambda e: e.activation(out=F("e4")[:, :], in_=F("t2")[:, :], func=AF.Exp), [Rr("t2")], [Rr("e4")])
                A(lambda e: e.activation(out=WL[:, :], in_=tot[:, :], func=AF.Exp), [r_tot], [r_WL])
                O = lambda nm: outf[nm, pc][0]
                Ro = lambda nm: outf[nm, pc][1]
                V(lambda e: e.scalar_tensor_tensor(out=O("Af")[:, :], in0=F("kk")[:, :], scalar=-1.0, op0=ALU.mult,
                                                   in1=F("e1")[:, :], op1=ALU.mult), [Rr("kk"), Rr("e1")], [Ro("Af")])
                G(lambda e: e.tensor_tensor(out=O("Bf")[:, :], in0=F("bb")[:, :], in1=F("e2")[:, :], op=ALU.mult),
                  [Rr("bb"), Rr("e2")], [Ro("Bf")])
                V(lambda e: e.tensor_tensor(out=O("Kf")[:, :], in0=F("kd")[:, :], in1=F("e2")[:, :], op=ALU.mult),
                  [Rr("kd"), Rr("e2")], [Ro("Kf")])
                G(lambda e: e.tensor_tensor(out=O("Rf")[:, :], in0=rp[:, :], in1=F("e3")[:, :], op=ALU.mult),
                  [r_rp, Rr("e3")], [Ro("Rf")])
                V(lambda e: e.tensor_tensor(out=O("Bhf")[:, :], in0=F("bb")[:, :], in1=F("e4")[:, :], op=ALU.mult),
                  [Rr("bb"), Rr("e4")], [Ro("Bhf")])
                G(lambda e: e.tensor_tensor(out=O("Khf")[:, :], in0=F("kd")[:, :], in1=F("e4")[:, :], op=ALU.mult),
                  [Rr("kd"), Rr("e4")], [Ro("Khf")])
                for nm_ in ("Af", "Bf", "Rf"):
                    for hh in range(2):
                        mt, r_mt = mskd[nm_, pc, hh]
                        (G if hh == 0 else V)(lambda e, mt=mt, nm_=nm_, hh=hh: e.tensor_scalar(
                            out=mt[:, :], in0=O(nm_)[:, :], scalar1=hm[:, hh:hh + 1], scalar2=None, op0=ALU.mult),
                            [Ro(nm_), r_hm], [r_mt])
                dg, r_dg = Dg[pc]
                V(lambda e, dg=dg: e.tensor_tensor(out=dg[:, :, :], in0=icat[:, :].unsqueeze(1).broadcast_to([128, NC, 64]),
                                                   in1=WL[:, :].unsqueeze(2).broadcast_to([128, NC, 64]), op=ALU.mult),
                  [r_icat, r_WL], [r_dg])
                for hh in range(2):
                    dm, r_dm = Dgm[pc, hh]
                    V(lambda e, dm=dm, dg=dg, hh=hh: e.tensor_scalar(out=dm[:, :, :], in0=dg[:, :, :], scalar1=hm[:, hh:hh + 1],
                                                                   scalar2=None, op0=ALU.mult), [r_dg, r_hm], [r_dm])
                if d == 1:
                    lora_sig(0, pc, shf["ad"][0], shf["ad"][1], a2s, r_a2s, 10, F("lr0"), Rr("lr0"))
                    kd_from(pc, F("lr0"), Rr("lr0"), F("kd0"), Rr("kd0"))
                    V(lambda e: e.tensor_tensor(out=F("kd0")[:, :], in0=F("kd0")[:, :], in1=F("kd")[:, :], op=ALU.add),
                      [Rr("kd0"), Rr("kd")], [Rr("kd0")])
                    V(lambda e: e.scalar_tensor_tensor(out=F("t1")[:, :], in0=rp[:, :], scalar=pc_col(20, pc), op0=ALU.mult,
                                                       in1=F("kd0")[:, :], op1=ALU.mult), [r_rp, r_prm, Rr("kd0")], [Rr("t1")])
                    pb, r_pb = nextbank()
                    M(lambda e, pb=pb: e.matmul(pb[:, 0:SEG], bones[:, :], F("t1")[:, :], start=True, stop=True),
                      [r_bones, Rr("t1")], [r_pb])
                    vp, r_vp = shf["v%d" % pc]
                    V(lambda e, pb=pb: e.tensor_tensor(out=F("bon")[:, :], in0=pb[:, 0:SEG], in1=vp[:, :], op=ALU.mult),
                      [r_pb, r_vp], [Rr("bon")])
                if stage < 3:
                    continue
                tlist = [("At", O("Af"), Ro("Af")), ("Bht", O("Bhf"), Ro("Bhf")), ("Kht", O("Khf"), Ro("Khf")),
                         ("Vt", shf["v%d" % pc][0], shf["v%d" % pc][1])]
                if d == 1:
                    tlist.append(("bont", F("bon"), Rr("bon")))
                for (xn, src, r_src) in tlist:
                    pb, r_pb = nextbank()
                    for c in range(NC):
                        M(lambda e, pb=pb, c=c, src=src: e.transpose(pb[0:64, c * 128:(c + 1) * 128], src[:, c * L:(c + 1) * L],
                                                                     ident[:, :]), [r_src, r_ident], [r_pb])
                    xt, r_xt = XT[xn]
                    A(lambda e, pb=pb, xt=xt: e.activation(out=xt[0:64, :, pc * 128:(pc + 1) * 128],
                                                           in_=pb[0:64, 0:NC * 128].rearrange("p (c n) -> p c n", n=128),
                                                           func=AF.Copy), [r_pb], [r_xt])
            if d == 1:
                A(lambda e: e.activation(out=sgd[0:96, :], in_=shf["gd"][0][:, :], func=AF.Sigmoid), [shf["gd"][1]], [r_sgd])
                pb, r_pb = nextbank()
                pb2, r_pb2 = nextbank()
                for c in range(NC):
                    tgt, r_tgt = (pb, r_pb) if c < 2 else (pb2, r_pb2)
                    M(lambda e, tgt=tgt, c=c: e.matmul(tgt[0:64, (c % 2) * 256:(c % 2 + 1) * 256], sgd[:, c * L:(c + 1) * L],
                                                       g2s[:, :], start=True, stop=True), [r_sgd, r_g2s], [r_tgt])
                V(lambda e, pb=pb: e.tensor_copy(out=gat[:, 0:2, :], in_=pb[0:64, :].rearrange("p (c n) -> p c n", n=256)),
                  [r_pb], [r_gat])
                V(lambda e, pb2=pb2: e.tensor_copy(out=gat[:, 2:4, :], in_=pb2[0:64, :].rearrange("p (c n) -> p c n", n=256)),
                  [r_pb2], [r_gat])

            if stage < 4:
                continue
            pcnt = {'n': 0}
            def fmop(nm, c, h):
                t_, r_ = outf[nm, h // 2]
                return t_[:, c * L:(c + 1) * L], r_

            def fmm(nm, c, h):
                t_, r_ = mskd[nm, h // 2, h % 2]
                return t_[:, c * L:(c + 1) * L], r_

            def tmop(nm, c, h):
                t_, r_ = XT[nm]
                return t_[:, c, h * 64:(h + 1) * 64], r_

            def wop(i, c, h):
                t_, r_ = Wd[i]
                stn = (h % 2) * 8 + c * 2 + h // 2
                return t_[:, stn * 64:(stn + 1) * 64], r_

            def product(terms_fn, evac_fn):
                pcnt["n"] += 1
                if pcnt["n"] > int(os.environ.get("MAXP", 999)):
                    return
                banks = [nextbank() for _ in range(NBK)]
                for c in range(NC):
                    for h in range(4):
                        if os.environ.get("EVENH") and h % 2 == 1:
                            continue
                        stn = (h % 2) * 8 + c * 2 + h // 2
                        pb, r_pb = banks[stn // 8]
                        terms = terms_fn(c, h)
                        for ti, (la, rl, ra, rr) in enumerate(terms):
                            M(lambda e, pb=pb, stn=stn, la=la, ra=ra, ti=ti, n=len(terms): e.matmul(
                                pb[0:64, (stn % 8) * 64:(stn % 8 + 1) * 64], la, ra, start=(ti == 0), stop=(ti == n - 1)),
                                [rl, rr], [r_pb])
                for bi, (pb, r_pb) in enumerate(banks):
                    evac_fn(bi, pb[0:64, :], r_pb)

            mi = {"su": 0, "sl": 1, "iu": 2, "il": 3}
            if d == 1:
                mi = {"su": 1, "sl": 0, "iu": 3, "il": 2}
            mbc = lambda nm: msk[:, mi[nm], :].unsqueeze(1).broadcast_to([64, 8, 64])
            w3 = lambda i, bi: Wd[i][0][0:64, bi * 512:(bi + 1) * 512].rearrange("p (s n) -> p s n", n=64)
            p3 = lambda pa: pa.rearrange("p (s n) -> p s n", n=64)
            ibc = ident[0:64, 0:64].unsqueeze(1).broadcast_to([64, 8, 64])
            eng_rr = {"i": 0}

            def ev_mask(wi, mname):
                def f(bi, pa, r_pb):
                    eng_rr["i"] += 1
                    V(lambda e: e.tensor_tensor(out=w3(wi, bi), in0=p3(pa), in1=mbc(mname), op=ALU.mult),
                      [r_pb, r_msk], [Wd[wi][1]])
                return f

            def ev_copy(wi, eng="scalar"):
                def f(bi, pa, r_pb):
                    if eng == "scalar":
                        A(lambda e: e.activation(out=Wd[wi][0][0:64, bi * 512:(bi + 1) * 512], in_=pa, func=AF.Copy),
                          [r_pb], [Wd[wi][1]])
                    else:
                        V(lambda e: e.tensor_copy(out=Wd[wi][0][0:64, bi * 512:(bi + 1) * 512], in_=pa), [r_pb], [Wd[wi][1]])
                return f

            def ev_copy_plus_ident(wi_raw, wi_id):
                def f(bi, pa, r_pb):
                    if wi_raw is not None:
                        A(lambda e: e.activation(out=Wd[wi_raw][0][0:64, bi * 512:(bi + 1) * 512], in_=pa, func=AF.Copy),
                          [r_pb], [Wd[wi_raw][1]])
                        G(lambda e: e.tensor_tensor(out=w3(wi_id, bi), in0=w3(wi_raw, bi), in1=ibc, op=ALU.add),
                          [Wd[wi_raw][1], r_ident], [Wd[wi_id][1]])
                    else:
                        V(lambda e: e.tensor_tensor(out=w3(wi_id, bi), in0=p3(pa), in1=ibc, op=ALU.add),
                          [r_pb, r_ident], [Wd[wi_id][1]])
                return f

            Pa, Pb_, Qa, Qb, IQ, Ta, Tb, MAK, NBR, NKR = range(10)
            def ev_N(bi, pa, r_pb):
                evn = int(os.environ.get("EVN", 2))
                if evn == 0:
                    return
                if evn == 1:
                    V(lambda e: e.tensor_tensor(out=w3(Pa, bi), in0=p3(pa), in1=mbc("su"), op=ALU.mult), [r_pb, r_msk], [Wd[Pa][1]])
                    return
                V(lambda e: e.tensor_tensor(out=w3(Pa, bi), in0=p3(pa), in1=mbc("su"), op=ALU.mult), [r_pb, r_msk], [Wd[Pa][1]])
                G(lambda e: e.tensor_tensor(out=w3(Ta, bi), in0=w3(Pa, bi), in1=ibc, op=ALU.add), [Wd[Pa][1], r_ident], [Wd[Ta][1]])
            product(lambda c, h: [fmop("Bf", c, h) + fmm("Af", c, h)], ev_N)
            product(lambda c, h: [fmop("Af", c, h) + fmm("Bf", c, h)], ev_mask(Qa, "sl"))
            product(lambda c, h: [fmop("Kf", c, h) + fmm("Af", c, h)], ev_mask(MAK, "su"))
            product(lambda c, h: [fmop("Bf", c, h) + fmm("Rf", c, h)], ev_mask(NBR, "iu"))
            product(lambda c, h: [fmop("Kf", c, h) + fmm("Rf", c, h)], ev_mask(NKR, "iu"))
            Pc, Pn, Qc, Qn, Tc, Tn = Pa, Pb_, Qa, Qb, Ta, Tb
            for kq in range(5):
                if kq < 4:
                    product(lambda c, h, Qc=Qc, Pc=Pc: [wop(Qc, c, h) + wop(Pc, c, h)], ev_copy(Pn, "vector"))
                product(lambda c, h, Qc=Qc, Pc=Pc: [wop(Pc, c, h) + wop(Qc, c, h)], ev_copy_plus_ident(Qn if kq < 4 else None, IQ))
                product(lambda c, h, Tc=Tc: [wop(IQ, c, h) + wop(Tc, c, h)], ev_copy(Tn, "scalar"))
                Pc, Pn, Qc, Qn, Tc, Tn = Pn, Pc, Qn, Qc, Tn, Tc
            TT = Tc
            Z, UV, ATT, RH, YV, GT, HH = Pa, Pb_, Qa, Qb, IQ, (Ta if TT == Tb else Tb), MAK
            product(lambda c, h: [wop(MAK, c, h) + tmop("Vt", c, h)], ev_copy(Z, "vector"))
            product(lambda c, h: [wop(TT, c, h) + wop(Z, c, h)], ev_copy(UV, "scalar"))
            product(lambda c, h: [wop(TT, c, h) + tmop("At", c, h)], ev_copy(ATT, "vector"))

            def icat_op(h):
                return icat[:, :], r_icat

            def dg_op(c, h):
                t_, r_ = Dgm[h // 2, h % 2]
                return t_[:, c, :], r_
            def ev_add(wi, wsrc):
                def f(bi, pa, r_pb):
                    V(lambda e: e.tensor_tensor(out=Wd[wi][0][0:64, bi * 512:(bi + 1) * 512], in0=pa,
                                                in1=Wd[wsrc][0][0:64, bi * 512:(bi + 1) * 512], op=ALU.add),
                      [r_pb, Wd[wsrc][1]], [Wd[wi][1]])
                return f
            product(lambda c, h: [icat_op(h) + fmm("Rf", c, h)], ev_copy(RH, "scalar"))
            product(lambda c, h: [wop(ATT, c, h) + wop(NBR, c, h)], ev_add(RH, RH))
            product(lambda c, h: [wop(NBR, c, h) + wop(UV, c, h), wop(NKR, c, h) + tmop("Vt", c, h)], ev_copy(YV, "vector"))
            product(lambda c, h: [icat_op(h) + dg_op(c, h)], ev_copy(GT, "scalar"))
            product(lambda c, h: [wop(ATT, c, h) + tmop("Bht", c, h)], ev_add(GT, GT))
            product(lambda c, h: [tmop("Bht", c, h) + wop(UV, c, h), tmop("Kht", c, h) + tmop("Vt", c, h)], ev_copy(HH, "vector"))
            if stage < 5:
                continue
            corder = list(range(NC)) if d == 0 else list(range(NC - 1, -1, -1))
            for ci_, c in enumerate(corder):
                pb, r_pb = nextbank()
                for h in range(4):
                    ga, rg = wop(GT, c, h)
                    sc = (h % 2) * 128 + (h // 2) * 64
                    M(lambda e, pb=pb, sc=sc, ga=ga, ci_=ci_: e.matmul(pb[0:64, sc:sc + 64], ga,
                                                                       Sst[:, ci_, sc:sc + 64], start=True, stop=True),
                      [rg, r_Sst], [r_pb])
                hh_ = Wd[HH][0][0:64, :].rearrange("p (b n) -> p b n", b=2)[:, :, c * 128:(c + 1) * 128]
                V(lambda e, pb=pb, ci_=ci_, hh_=hh_: e.tensor_tensor(out=Sst[0:64, ci_ + 1, :].rearrange("p (b n) -> p b n", b=2),
                                                                     in0=pb[0:64, 0:256].rearrange("p (b n) -> p b n", b=2),
                                                                     in1=hh_, op=ALU.add),
                  [r_pb, Wd[HH][1]], [r_Sst])
            pby = [nextbank() for _ in range(2)]
            for ci_, c in enumerate(corder):
                pb, r_pb = pby[c // 2]
                for h in range(4):
                    ra_, rr_ = wop(RH, c, h)
                    sc = (h % 2) * 128 + (h // 2) * 64
                    M(lambda e, pb=pb, c=c, sc=sc, ra_=ra_, ci_=ci_: e.matmul(
                        pb[0:64, (c % 2) * 256 + sc:(c % 2) * 256 + sc + 64], ra_, Sst[:, ci_, sc:sc + 64],
                        start=True, stop=True), [rr_, r_Sst], [r_pb])
            for c in range(NC):
                pb, r_pb = pby[c // 2]
                yv_ = Wd[YV][0][0:64, :].rearrange("p (b n) -> p b n", b=2)[:, :, c * 128:(c + 1) * 128].rearrange("p hp (hq n) -> p hp hq n", hq=2)
                V(lambda e, pb=pb, c=c, yv_=yv_: e.tensor_tensor(
                    out=yo[:, c, :].rearrange("p (hq hp n) -> p hp hq n", hq=2, hp=2),
                    in0=pb[0:64, (c % 2) * 256:(c % 2 + 1) * 256].rearrange("p (hp hq n) -> p hp hq n", hp=2, hq=2),
                    in1=yv_, op=ALU.add), [r_pb, Wd[YV][1]], [r_yo])
            V(lambda e: e.tensor_copy(out=Sst[0:64, 0, :], in_=Sst[0:64, NC, :]), [r_Sst], [r_Sst])
            ydst = io["yf"][s0:s0 + SEG, :].rearrange("(c p) n -> p c n", p=64)
            if d == 0:
                P.dma("sync", ydst, yo[:, :, :], reads=[r_yo], writes=[yf_res])
            else:
                if 0 in dirs:
                    P.dma("sync", yfb[:, :, :], ydst, reads=[yf_res], writes=[r_yfb])
                    V(lambda e: e.tensor_tensor(out=yo[:, :, :], in0=yo[:, :, :], in1=yfb[:, :, :], op=ALU.add), [r_yo, r_yfb], [r_yo])
                y4 = yo[:, :, :].rearrange("p c (h n) -> p (c h) n", n=64)
                NH_ = NC * 4
                mean = stat[:, 0:NH_]
                var = stat[:, NH_:2 * NH_]
                V(lambda e: e.tensor_reduce(out=mean, in_=y4, axis=AX.X, op=ALU.add), [r_yo], [r_stat])
                V(lambda e: e.tensor_scalar(out=mean, in0=mean, scalar1=1.0 / 64, scalar2=None, op0=ALU.mult), [r_stat], [r_stat])
                V(lambda e: e.tensor_tensor(out=y4, in0=y4, in1=mean.unsqueeze(2).broadcast_to([64, NH_, 64]), op=ALU.subtract),
                  [r_yo, r_stat], [r_yo])
                yq = yfb[:, :, :].rearrange("p c (h n) -> p (c h) n", n=64)
                V(lambda e: e.tensor_tensor(out=yq, in0=y4, in1=y4, op=ALU.mult), [r_yo], [r_yfb])
                V(lambda e: e.tensor_reduce(out=var, in_=yq, axis=AX.X, op=ALU.add), [r_yfb], [r_stat])
                A(lambda e: e.activation(out=var, in_=var, func=AF.Sqrt, scale=1.0 / 64, bias=LNX_EPS), [r_stat], [r_stat])
                V(lambda e: e.reciprocal(out=var, in_=var), [r_stat], [r_stat])
                V(lambda e: e.tensor_tensor(out=y4, in0=y4, in1=var.unsqueeze(2).broadcast_to([64, NH_, 64]), op=ALU.mult),
                  [r_yo, r_stat], [r_yo])
                V(lambda e: e.tensor_tensor(out=yo[:, :, :], in0=yo[:, :, :], in1=lng[:, :].unsqueeze(1).broadcast_to([64, NC, 256]),
                                            op=ALU.mult), [r_yo, r_lng], [r_yo])
                V(lambda e: e.tensor_tensor(out=yo[:, :, :], in0=yo[:, :, :], in1=lnb[:, :].unsqueeze(1).broadcast_to([64, NC, 256]),
                                            op=ALU.add), [r_yo, r_lnb], [r_yo])
                V(lambda e: e.tensor_tensor(out=yo[:, :, :], in0=yo[:, :, :], in1=XT["bont"][0][0:64, :, :], op=ALU.add),
                  [r_yo, XT["bont"][1]], [r_yo])
                V(lambda e: e.tensor_tensor(out=yo[:, :, :], in0=yo[:, :, :], in1=gat[:, :, :], op=ALU.mult), [r_yo, r_gat], [r_yo])
                P.dma("sync", io["ya"][s0:s0 + SEG, :].rearrange("(c p) n -> p c n", p=64), yo[:, :, :], reads=[r_yo])


def build_rwkv(**kw):
    nc = bass.Bass("TRN2", target_bir_lowering=False)
    di = lambda name, shape: nc.dram_tensor(name, shape, F32, kind="ExternalInput").ap()
    io = {
        "rkv": di("rkv", [3, 256, T + 2]), "wd": di("wd", [64, T + 2]), "ad": di("ad", [64, T + 2]), "gd": di("gd", [96, T + 2]),
        "ident": di("ident", [128, 128]), "icat": di("icat", [128, 64]), "bones": di("bones", [128, 128]),
        "msk": di("msk", [64, 4, 64]), "rmask": di("rmask", [128, SEG]), "prm": di("prm", [128, 32]), "mul": di("mul", [128, 3]),
        "w2s": di("w2s", [128, 512]), "a2s": di("a2s", [128, 512]), "g2s": di("g2s", [128, 256]), "hm": di("hm", [128, 2]),
        "lng": di("lng", [64, 256]), "lnb": di("lnb", [64, 256]),
        "yf": nc.dram_tensor("yf", [T, 256], F32, kind="ExternalOutput").ap(),
        "ya": nc.dram_tensor("ya", [T, 256], F32, kind="ExternalOutput").ap(),
    }
    with ExitStack() as st:
        P = Prog(nc, st)
        emit_rwkv(nc, P, st, io, **kw)
        P.finish()
        P.emit()
    return nc


def rwkv_consts():
    idx = np.arange(64)
    su = (idx[:, None] < idx[None, :]).astype(np.float32)
    iu = (idx[:, None] <= idx[None, :]).astype(np.float32)
    msk = np.stack([su, su.T, iu, iu.T], axis=1)
    ident = np.eye(128, dtype=np.float32)
    icat = np.concatenate([np.eye(64), np.eye(64)], axis=0).astype(np.float32)
    bones = np.kron(np.eye(2), np.ones((64, 64))).astype(np.float32)
    rmask = np.ones((128, SEG), np.float32)
    rmask[:, ::L] = 0.0
    return {"msk": np.ascontiguousarray(msk), "ident": ident, "icat": icat, "bones": bones, "rmask": rmask}


def lora_pad(w):
    o = np.zeros((128, 512), np.float32)
    for d in range(2):
        o[d * 32:(d + 1) * 32, d * 256:(d + 1) * 256] = w[d]
    return o


def rwkv_inputs(rw_colsT, hq, prm_in):
    ch = slice(hq * 256, (hq + 1) * 256)
    padT = lambda a: np.pad(a, ((0, 0), (1, 1)))
    rkv = np.stack([padT(rw_colsT[w * 512 + hq * 256: w * 512 + (hq + 1) * 256]) for w in range(3)])
    wd = padT(rw_colsT[1536:1600]); ad = padT(rw_colsT[1600:1664]); gd = padT(rw_colsT[1664:1760])
    mu = prm_in["rw_mu"]
    prm = np.zeros((128, 32), np.float32)
    for w in range(3):
        for pc in range(2):
            prm[:, w * 2 + pc] = mu[w * 512 + hq * 256 + pc * 128: w * 512 + hq * 256 + (pc + 1) * 128]
    for d in range(2):
        for pc in range(2):
            cs = slice(hq * 256 + pc * 128, hq * 256 + (pc + 1) * 128)
            prm[:, 6 + d * 2 + pc] = prm_in["rw_w0"][d, cs]
            prm[:, 10 + d * 2 + pc] = prm_in["rw_a0"][d, cs]
    for pc in range(2):
        cs = slice(hq * 256 + pc * 128, hq * 256 + (pc + 1) * 128)
        prm[:, 14 + pc] = prm_in["rw_k_k"][cs]
        prm[:, 16 + pc] = prm_in["rw_k_a"][cs]
        prm[:, 20 + pc] = prm_in["rw_r_k"].reshape(-1)[cs]
    mul = np.zeros((128, 3), np.float32)
    mul[0:64, 0] = mu[1536:1600]; mul[0:64, 1] = mu[1600:1664]; mul[0:96, 2] = mu[1664:1760]
    m = dict(rwkv_consts())
    m.update({
        "rkv": np.ascontiguousarray(rkv), "wd": np.ascontiguousarray(wd), "ad": np.ascontiguousarray(ad), "gd": np.ascontiguousarray(gd),
        "prm": prm, "mul": mul,
        "w2s": lora_pad(prm_in["rw_w2"][:, :, ch]), "a2s": lora_pad(prm_in["rw_a2"][:, :, ch]),
        "g2s": np.ascontiguousarray(np.pad(prm_in["rw_g2"][:, ch], ((0, 32), (0, 0)))),
        "hm": np.ascontiguousarray(np.stack([(np.arange(128) < 64), (np.arange(128) >= 64)], axis=1).astype(np.float32)),
        "lng": np.ascontiguousarray(np.broadcast_to(prm_in["rw_lnx_g"][ch][None, :], (64, 256))),
        "lnb": np.ascontiguousarray(np.broadcast_to(prm_in["rw_lnx_b"][ch][None, :], (64, 256))),
    })
    return m


NORM_EPS = 1e-6
D = 2048
KC = 16
NT = 2048
TI = 512
TW = TI + 2
HB = TW // 2
DFF = 5632
NF = DFF // 128
FG = 2
GELU_C = 1.5957691216057308


def emit_post(nc, P, st, io, final, ntiles=NT // TI, nfg=NF // FG, nseg=1):
    sb = lambda name, shape, dt=F32: st.enter_context(nc.sbuf_tensor("sb_" + name, shape, dt))
    ps = lambda name, shape, dt=F32: st.enter_context(nc.psum_tensor("ps_" + name, shape, dt))
    V = lambda fn, r, w: P.op("vector", fn, reads=r, writes=w)
    G = lambda fn, r, w: P.op("gpsimd", fn, reads=r, writes=w)
    A = lambda fn, r, w: P.op("scalar", fn, reads=r, writes=w)
    M = lambda fn, r, w: P.op("tensor", fn, reads=r, writes=w)

    def tile(name, shape, dt=F32):
        return sb(name, shape, dt), P.res(name)

    x, r_x = tile("x", [128, KC, TW])
    h, r_h = tile("h", [128, KC, TW], BF16)
    mg, r_mg = tile("mg", [128, KC, TW], BF16)
    ybf, r_ybf = tile("ybf", [128, 12, TW], BF16)
    stg = [tile("stg%d" % i, [128, 4096]) for i in range(2)]
    wbf = [tile("wbf%d" % i, [128, 4096], BF16) for i in range(4)]
    gt = [tile("gt%d" % i, [128, TW]) for i in range(3)]
    tmp = [tile("tmp%d" % i, [128, TW]) for i in range(4)]
    abf, r_abf = tile("abf", [128, FG, TI], BF16)
    rs, r_rs = tile("rs", [128, TW])
    sq, r_sq = tile("sq", [128, KC, TW])
    ones, r_ones = tile("ones", [128, 128])
    nrm, r_nrm = tile("nrm", [128, 4, KC])
    cw, r_cw = tile("cw", [128, NF, 4])
    pbank = [(ps("pb%d" % i, [128, 512]), P.res("pb%d" % i)) for i in range(8)]
    bk = {"i": 0}
    dq = {"i": 0}

    def bank2():
        i = bk["i"] % 4
        bk["i"] += 1
        return pbank[2 * i], pbank[2 * i + 1]

    def dmaq():
        dq["i"] += 1
        return "sync" if dq["i"] % 2 == 0 else "gpsimd"

    V(lambda e: e.memset(ones[:, :], 1.0), [], [r_ones])
    P.dma("sync", nrm[:, :, :], io["nrm"][:, :, :], writes=[r_nrm])
    P.dma("sync", cw[:, :, :], io["cw"][:, :, :], writes=[r_cw])
    cvi = {"i": 0}

    def load_w(dst_i, src_ap, rows, cols):
        s_i = cvi["i"] % 2
        cvi["i"] += 1
        stt, r_st = stg[s_i]
        wb, r_wb = wbf[dst_i]
        n = rows * cols
        P.dma(dmaq(), stt[:, 0:n].rearrange("p (r c) -> p r c", c=cols), src_ap, writes=[r_st])
        (G if s_i == 0 else V)(lambda e: e.tensor_copy(out=wb[:, 0:n], in_=stt[:, 0:n]), [r_st], [r_wb])
        return wb[:, 0:n].rearrange("p (r c) -> p r c", c=cols), r_wb

    def mm_blocks(lhs_list, rhs_fn, r_list, ntok_off=0, ntok=TW):
        (b0, r0), (b1, r1) = bank2()
        half = ntok // 2
        for blk, (pb, r_pb) in enumerate(((b0, r0), (b1, r1))):
            for k, la in enumerate(lhs_list):
                M(lambda e, pb=pb, la=la, k=k, blk=blk: e.matmul(pb[:, 0:half], la, rhs_fn(k, ntok_off + blk * half, half),
                                                                 start=(k == 0), stop=(k == len(lhs_list) - 1)),
                  r_list, [r_pb])
        return ((b0, r0), (b1, r1)), half

    def rmsnorm_to_h(gain_idx, off, ntok):
        A(lambda e: e.activation(out=sq[:, :, 0:ntok], in_=x[:, :, off:off + ntok], func=AF.Square), [r_x], [r_sq])
        half = ntok // 2
        (b0, r0), (b1, r1) = bank2()
        for blk, (pb, r_pb) in enumerate(((b0, r0), (b1, r1))):
            for k in range(KC):
                M(lambda e, pb=pb, k=k, blk=blk: e.matmul(pb[:, 0:half], ones[:, :], sq[:, k, blk * half:(blk + 1) * half],
                                                          start=(k == 0), stop=(k == KC - 1)), [r_ones, r_sq], [r_pb])
            A(lambda e, pb=pb, blk=blk: e.activation(out=rs[:, blk * half:(blk + 1) * half], in_=pb[:, 0:half], func=AF.Sqrt,
                                                     scale=1.0 / D, bias=NORM_EPS), [r_pb], [r_rs])
        V(lambda e: e.reciprocal(out=rs[:, 0:ntok], in_=rs[:, 0:ntok]), [r_rs], [r_rs])
        for k in range(KC):
            V(lambda e, k=k: e.scalar_tensor_tensor(out=h[:, k, off:off + ntok], in0=x[:, k, off:off + ntok],
                                                    scalar=nrm[:, gain_idx, k:k + 1], op0=ALU.mult, in1=rs[:, 0:ntok], op1=ALU.mult),
              [r_x, r_nrm, r_rs], [r_h])

    wbr = [(io["w_br_a"], 4, 0), (io["w_br_b"], 2, 4), (io["w_br_c"], 6, 6)]

    for tl_all in range(ntiles * nseg):
        sg, tl = tl_all // ntiles, tl_all % ntiles
        xv = io["xT"][sg].rearrange("(c p) t -> p c t", p=128)
        yv = io["yT"][sg].rearrange("(c p) t -> p c t", p=128)
        gv = io["gT"][sg].rearrange("(g c p) t -> g c p t", g=3, p=128)
        pv = io["pT"][sg].rearrange("(c p) t -> p c t", p=128)
        ov = io["out"][sg].rearrange("(c p) t -> p c t", p=128)
        c0 = tl * TI
        for q in range(4):
            P.dma(dmaq(), x[:, q * 4:(q + 1) * 4, :], xv[:, q * 4:(q + 1) * 4, c0:c0 + TW], writes=[r_x])
        for q in range(3):
            stt, r_st = stg[q % 2]
            P.dma(dmaq(), stt[:, 0:4 * TW].rearrange("p (r c) -> p r c", c=TW), yv[:, q * 4:(q + 1) * 4, c0:c0 + TW], writes=[r_st])
            V(lambda e, stt=stt, q=q: e.tensor_copy(out=ybf[:, q * 4:(q + 1) * 4, :],
                                                    in_=stt[:, 0:4 * TW].rearrange("p (r c) -> p r c", c=TW)), [r_st], [r_ybf])
        for jg in range(8):
            wts = []
            for bi, (wap, nk, yoff) in enumerate(wbr):
                wv_ = wap.rearrange("(c p) n -> p c n", p=128)
                wts.append(load_w(bi, wv_[:, :, jg * 256:(jg + 1) * 256], nk, 256))
            for jj in range(2):
                j = jg * 2 + jj
                for bi, (wap, nk, yoff) in enumerate(wbr):
                    g_t, r_g = gt[bi]
                    P.dma(dmaq(), g_t[:, :], gv[bi, j, :, c0:c0 + TW], writes=[r_g])
                    A(lambda e, g_t=g_t: e.activation(out=g_t[:, :], in_=g_t[:, :], func=AF.Sigmoid), [r_g], [r_g])
                for bi, (wap, nk, yoff) in enumerate(wbr):
                    w3_, r_w = wts[bi]
                    g_t, r_g = gt[bi]
                    banks, half = mm_blocks([w3_[:, k, jj * 128:(jj + 1) * 128] for k in range(nk)],
                                            lambda k, o, n, yoff=yoff: ybf[:, yoff + k, o:o + n], [r_w, r_ybf])
                    for blk, (pb, r_pb) in enumerate(banks):
                        sl = slice(blk * half, (blk + 1) * half)
                        if bi == 0:
                            V(lambda e, pb=pb, sl=sl, g_t=g_t: e.tensor_tensor(out=tmp[0][0][:, sl], in0=pb[:, 0:half], in1=g_t[:, sl], op=ALU.mult),
                              [r_pb, r_g], [tmp[0][1]])
                        else:
                            V(lambda e, pb=pb, sl=sl, g_t=g_t: e.tensor_tensor(out=tmp[1][0][:, sl], in0=pb[:, 0:half], in1=g_t[:, sl], op=ALU.mult),
                              [r_pb, r_g], [tmp[1][1]])
                            if bi == 1:
                                G(lambda e, sl=sl: e.tensor_tensor(out=tmp[0][0][:, sl], in0=tmp[0][0][:, sl], in1=tmp[1][0][:, sl], op=ALU.add),
                                  [tmp[0][1], tmp[1][1]], [tmp[0][1]])
                            else:
                                G(lambda e, sl=sl, j=j: e.tensor_tensor(out=mg[:, j, sl], in0=tmp[0][0][:, sl], in1=tmp[1][0][:, sl], op=ALU.add),
                                  [tmp[0][1], tmp[1][1]], [r_mg])
        wov = io["w_out"].rearrange("(c p) n -> p c n", p=128)
        for jg in range(8):
            w3_, r_w = load_w(3, wov[:, :, jg * 256:(jg + 1) * 256], KC, 256)
            for jj in range(2):
                j = jg * 2 + jj
                banks, half = mm_blocks([w3_[:, k, jj * 128:(jj + 1) * 128] for k in range(KC)],
                                        lambda k, o, n: mg[:, k, o:o + n], [r_w, r_mg])
                for blk, (pb, r_pb) in enumerate(banks):
                    sl = slice(blk * half, (blk + 1) * half)
                    V(lambda e, pb=pb, sl=sl, j=j: e.tensor_tensor(out=x[:, j, sl], in0=x[:, j, sl], in1=pb[:, 0:half], op=ALU.add),
                      [r_pb, r_x], [r_x])
        rmsnorm_to_h(0, 0, TW)
        wgv = io["w_gate"].rearrange("(c p) n -> p c n", p=128)
        wuv = io["w_up"].rearrange("(c p) n -> p c n", p=128)
        wdv = io["w_down"].rearrange("(c p) n -> p c n", p=128)
        for fg in range(nfg):
            f0 = fg * FG
            wg3, r_wg = load_w(0, wgv[:, :, f0 * 128:(f0 + FG) * 128], KC, FG * 128)
            wu3, r_wu = load_w(1, wuv[:, :, f0 * 128:(f0 + FG) * 128], KC, FG * 128)
            for ff in range(FG):
                f = f0 + ff
                gb_, half = mm_blocks([wg3[:, k, ff * 128:(ff + 1) * 128] for k in range(KC)],
                                      lambda k, o, n: h[:, k, o:o + n], [r_wg, r_h])
                g_s, r_gs = tmp[0]
                for blk, (pb, r_pb) in enumerate(gb_):
                    A(lambda e, pb=pb, blk=blk: e.activation(out=g_s[:, blk * half:(blk + 1) * half], in_=pb[:, 0:half], func=AF.Copy),
                      [r_pb], [r_gs])
                ub_, half = mm_blocks([wu3[:, k, ff * 128:(ff + 1) * 128] for k in range(KC)],
                                      lambda k, o, n: h[:, k, o:o + n], [r_wu, r_h], ntok_off=1, ntok=TI)
                c_s, r_cs = tmp[1]
                t_s, r_ts = tmp[2]
                V(lambda e, f=f: e.tensor_scalar(out=c_s[:, 0:TI], in0=g_s[:, 1:TI + 1], scalar1=cw[:, f, 1:2], scalar2=cw[:, f, 3:4],
                                                 op0=ALU.mult, op1=ALU.add), [r_gs, r_cw], [r_cs])
                V(lambda e, f=f: e.scalar_tensor_tensor(out=c_s[:, 0:TI], in0=g_s[:, 0:TI], scalar=cw[:, f, 0:1], op0=ALU.mult,
                                                        in1=c_s[:, 0:TI], op1=ALU.add), [r_gs, r_cw, r_cs], [r_cs])
                V(lambda e, f=f: e.scalar_tensor_tensor(out=c_s[:, 0:TI], in0=g_s[:, 2:TI + 2], scalar=cw[:, f, 2:3], op0=ALU.mult,
                                                        in1=c_s[:, 0:TI], op1=ALU.add), [r_gs, r_cw, r_cs], [r_cs])
                A(lambda e: e.activation(out=t_s[:, 0:TI], in_=c_s[:, 0:TI], func=AF.Square), [r_cs], [r_ts])
                G(lambda e: e.tensor_scalar(out=t_s[:, 0:TI], in0=t_s[:, 0:TI], scalar1=0.044715, scalar2=1.0, op0=ALU.mult, op1=ALU.add),
                  [r_ts], [r_ts])
                G(lambda e: e.tensor_tensor(out=t_s[:, 0:TI], in0=t_s[:, 0:TI], in1=c_s[:, 0:TI], op=ALU.mult), [r_ts, r_cs], [r_ts])
                A(lambda e: e.activation(out=t_s[:, 0:TI], in_=t_s[:, 0:TI], func=AF.Sigmoid, scale=GELU_C), [r_ts], [r_ts])
                G(lambda e: e.tensor_tensor(out=t_s[:, 0:TI], in0=t_s[:, 0:TI], in1=c_s[:, 0:TI], op=ALU.mult), [r_ts, r_cs], [r_ts])
                for blk, (pb, r_pb) in enumerate(ub_):
                    V(lambda e, pb=pb, blk=blk, ff=ff: e.tensor_tensor(out=abf[:, ff, blk * half:(blk + 1) * half],
                                                                      in0=pb[:, 0:half], in1=t_s[:, blk * half:(blk + 1) * half], op=ALU.mult),
                      [r_pb, r_ts], [r_abf])
            for hc in range(2):
                wd3, r_wd = load_w(2, wdv[:, f0:f0 + FG, hc * 1024:(hc + 1) * 1024], FG, 1024)
                for jj in range(8):
                    j = hc * 8 + jj
                    banks, half = mm_blocks([wd3[:, k, jj * 128:(jj + 1) * 128] for k in range(FG)],
                                            lambda k, o, n: abf[:, k, o:o + n], [r_wd, r_abf], ntok_off=0, ntok=TI)
                    for blk, (pb, r_pb) in enumerate(banks):
                        sl = slice(1 + blk * half, 1 + (blk + 1) * half)
                        V(lambda e, pb=pb, sl=sl, j=j: e.tensor_tensor(out=x[:, j, sl], in0=x[:, j, sl], in1=pb[:, 0:half], op=ALU.add),
                          [r_pb, r_x], [r_x])
        rmsnorm_to_h(1, 1, TI)
        stt, r_st = stg[0]
        P.dma(dmaq(), stt[:, 0:2 * TI].rearrange("p (r c) -> p r c", c=TI), pv[:, :, tl * TI:(tl + 1) * TI], writes=[r_st])
        V(lambda e: e.tensor_copy(out=ybf[:, 0:2, 0:TI], in_=stt[:, 0:2 * TI].rearrange("p (r c) -> p r c", c=TI)), [r_st], [r_ybf])
        wpv = io["w_pg"].rearrange("(c p) n -> p c n", p=128)
        wlv = io["w_ple"].rearrange("(c p) n -> p c n", p=128)
        for jg in range(8):
            wp3, r_wp = load_w(0, wpv[:, :, jg * 256:(jg + 1) * 256], KC, 256)
            wl3, r_wl = load_w(1, wlv[:, :, jg * 256:(jg + 1) * 256], 2, 256)
            for jj in range(2):
                j = jg * 2 + jj
                gb_, half = mm_blocks([wp3[:, k, jj * 128:(jj + 1) * 128] for k in range(KC)],
                                      lambda k, o, n: h[:, k, o:o + n], [r_wp, r_h], ntok_off=1, ntok=TI)
                g_s, r_gs = tmp[0]
                for blk, (pb, r_pb) in enumerate(gb_):
                    A(lambda e, pb=pb, blk=blk: e.activation(out=g_s[:, blk * half:(blk + 1) * half], in_=pb[:, 0:half], func=AF.Sigmoid),
                      [r_pb], [r_gs])
                pb_, half = mm_blocks([wl3[:, k, jj * 128:(jj + 1) * 128] for k in range(2)],
                                      lambda k, o, n: ybf[:, k, o:o + n], [r_wl, r_ybf], ntok_off=0, ntok=TI)
                for blk, (pb, r_pb) in enumerate(pb_):
                    V(lambda e, pb=pb, blk=blk: e.tensor_tensor(out=g_s[:, blk * half:(blk + 1) * half], in0=pb[:, 0:half],
                                                                in1=g_s[:, blk * half:(blk + 1) * half], op=ALU.mult), [r_pb, r_gs], [r_gs])
                G(lambda e, j=j: e.tensor_tensor(out=x[:, j, 1:TI + 1], in0=x[:, j, 1:TI + 1], in1=g_s[:, 0:TI], op=ALU.add),
                  [r_x, r_gs], [r_x])
        if final:
            A(lambda e: e.activation(out=sq[:, :, 0:TI], in_=x[:, :, 1:TI + 1], func=AF.Square), [r_x], [r_sq])
            half = TI // 2
            (b0, r0), (b1, r1) = bank2()
            for blk, (pb, r_pb) in enumerate(((b0, r0), (b1, r1))):
                for k in range(KC):
                    M(lambda e, pb=pb, k=k, blk=blk: e.matmul(pb[:, 0:half], ones[:, :], sq[:, k, blk * half:(blk + 1) * half],
                                                              start=(k == 0), stop=(k == KC - 1)), [r_ones, r_sq], [r_pb])
                A(lambda e, pb=pb, blk=blk: e.activation(out=rs[:, blk * half:(blk + 1) * half], in_=pb[:, 0:half], func=AF.Sqrt,
                                                         scale=1.0 / D, bias=NORM_EPS), [r_pb], [r_rs])
            V(lambda e: e.reciprocal(out=rs[:, 0:TI], in_=rs[:, 0:TI]), [r_rs], [r_rs])
            for k in range(KC):
                V(lambda e, k=k: e.scalar_tensor_tensor(out=sq[:, k, 0:TI], in0=x[:, k, 1:TI + 1], scalar=nrm[:, 2, k:k + 1],
                                                        op0=ALU.mult, in1=rs[:, 0:TI], op1=ALU.mult), [r_x, r_nrm, r_rs], [r_sq])
            for q in range(4):
                P.dma(dmaq(), ov[:, q * 4:(q + 1) * 4, tl * TI:(tl + 1) * TI], sq[:, q * 4:(q + 1) * 4, 0:TI], reads=[r_sq])
        else:
            for q in range(4):
                P.dma(dmaq(), ov[:, q * 4:(q + 1) * 4, tl * TI:(tl + 1) * TI], x[:, q * 4:(q + 1) * 4, 1:TI + 1], reads=[r_x])


def build_post(final, nseg=1, **kw):
    nc = bass.Bass("TRN2", target_bir_lowering=False)
    di = lambda name, shape: nc.dram_tensor(name, shape, F32, kind="ExternalInput").ap()
    io = {
        "xT": di("xT", [nseg, D, NT + 2]), "yT": di("yT", [nseg, 1536, NT + 2]), "gT": di("gT", [nseg, 3 * D, NT + 2]), "pT": di("pT", [nseg, 256, NT]),
        "nrm": di("nrm", [128, 4, KC]), "cw": di("cw", [128, NF, 4]),
        "w_br_a": di("w_br_a", [512, D]), "w_br_b": di("w_br_b", [256, D]), "w_br_c": di("w_br_c", [768, D]),
        "w_out": di("w_out", [D, D]), "w_gate": di("w_gate", [D, DFF]), "w_up": di("w_up", [D, DFF]), "w_down": di("w_down", [DFF, D]),
        "w_pg": di("w_pg", [D, D]), "w_ple": di("w_ple", [256, D]),
        "out": nc.dram_tensor("out", [nseg, D, NT], F32, kind="ExternalOutput").ap(),
    }
    with ExitStack() as st:
        P = Prog(nc, st)
        emit_post(nc, P, st, io, final, nseg=nseg, **kw)
        P.finish()
        P.emit()
    return nc


def halo(a, t0, n, axis=0):
    T_ = a.shape[axis]
    lo, hi = t0 - 1, t0 + n + 1
    sl = [slice(None)] * a.ndim
    sl[axis] = slice(max(lo, 0), min(hi, T_))
    pw = [(0, 0)] * a.ndim
    pw[axis] = (max(0, -lo), max(0, hi - T_))
    return np.pad(a[tuple(sl)], pw)


def post_inputs(x_b, ya_b, yb_b, yc_b, gates_b, p_b, half, W, li, final_g):
    t0 = half * NT
    yall = np.concatenate([ya_b, yb_b, yc_b], axis=1)
    nrm = np.zeros((128, 4, KC), np.float32)
    nrm[:, 0, :] = W["norm_ffn"][li].reshape(KC, 128).T
    nrm[:, 1, :] = W["norm_ple"][li].reshape(KC, 128).T
    nrm[:, 2, :] = final_g.reshape(KC, 128).T
    cw = np.zeros((128, NF, 4), np.float32)
    for i in range(3):
        cw[:, :, i] = W["ffn_conv_w"][li][i].reshape(NF, 128).T
    cw[:, :, 3] = W["ffn_conv_b"][li].reshape(NF, 128).T
    return {
        "xT": np.ascontiguousarray(halo(x_b, t0, NT).T), "yT": np.ascontiguousarray(halo(yall, t0, NT).T),
        "gT": np.ascontiguousarray(halo(gates_b, t0, NT).T), "pT": np.ascontiguousarray(p_b[t0:t0 + NT].T),
        "nrm": nrm, "cw": cw,
        "w_br_a": W["w_br_a"][li], "w_br_b": W["w_br_b"][li], "w_br_c": W["w_br_c"][li], "w_out": W["w_out"][li],
        "w_gate": W["w_ffn_gate"][li], "w_up": W["w_ffn_up"][li], "w_down": W["w_ffn_down"][li],
        "w_pg": W["w_ple_gate"][li], "w_ple": W["w_ple"][li],
    }


NDENSE = 2
OFF = {"rw": 0, "dq": 1760, "dk": 2528, "dv": 3296, "nq": 4064, "nk": 4832, "nv": 5600, "g": 6368}


def attn_inputs_fm(zb, na_bias, half):
    C, S = rope_consts()
    T_ = 4096
    t0 = half * NTQ
    rows = lambda k: zb[OFF[k]:OFF[k] + 768].reshape(12, 64, T_)

    def vaug(k):
        vh = rows(k).transpose(0, 2, 1)
        va = np.concatenate([vh, np.ones((12, T_, 1), np.float32)], axis=2)
        vw = window(va, half, 1)
        return np.ascontiguousarray(vw.reshape(12, 32, 128, 65).transpose(0, 2, 1, 3))
    return {
        "cq": np.ascontiguousarray(C[:, t0:t0 + NTQ]), "sq": np.ascontiguousarray(S[:, t0:t0 + NTQ]),
        "ck": np.ascontiguousarray(window(C, half, 1)), "sk": np.ascontiguousarray(window(S, half, 1)),
        "md": dil_masks(),
        "dq": np.ascontiguousarray(rows("dq")[:, :, t0:t0 + NTQ]), "dk": np.ascontiguousarray(window(rows("dk"), half, 2)),
        "dv": vaug("dv"),
        "nq": np.ascontiguousarray(rows("nq")[:, :, t0:t0 + NTQ]), "nk": np.ascontiguousarray(window(rows("nk"), half, 2)),
        "nv": vaug("nv"),
        "ne": np.ascontiguousarray(np.stack([na_tables(na_bias[h], na_slot_m(half)) for h in range(12)])),
    }


def post_inputs_fm(xfm_b, yT_b, gT_b, pT_b, half, W, li):
    t0 = half * NT
    nrm = np.zeros((128, 4, KC), np.float32)
    nrm[:, 0, :] = W["norm_ffn"][li].reshape(KC, 128).T
    nrm[:, 1, :] = W["norm_ple"][li].reshape(KC, 128).T
    nrm[:, 2, :] = W["norm_final"].reshape(KC, 128).T
    cw = np.zeros((128, NF, 4), np.float32)
    for i in range(3):
        cw[:, :, i] = W["ffn_conv_w"][li][i].reshape(NF, 128).T
    cw[:, :, 3] = W["ffn_conv_b"][li].reshape(NF, 128).T
    return {
        "xT": halo(xfm_b, t0, NT, axis=1), "yT": halo(yT_b, t0, NT, axis=1), "gT": halo(gT_b, t0, NT, axis=1),
        "pT": pT_b[:, t0:t0 + NT], "nrm": nrm, "cw": cw,
    }


def kernel(**inp):
    inp = {k: np.asarray(v) for k, v in inp.items()}
    x, p = inp["x"], inp["p"]
    W = inp
    B_, T_ = 4, 4096
    segs = [(b, hf) for b in range(B_) for hf in range(2)]
    nseg = 8 // NDENSE
    xfm = [np.ascontiguousarray(x[b].T) for b in range(B_)]
    ncA = build_inproj(2048, 2048, 12512, nseg=nseg)
    ncB1 = build_attn()
    ncB2 = build_rwkv()
    out = np.empty((B_, T_, 2048), np.float32)
    for li in range(2):
        gl = np.ascontiguousarray(W["norm_mix"][li].reshape(16, 128).T)
        in_maps = []
        for c in range(NDENSE):
            xs = np.stack([xfm[b][:, hf * 2048:(hf + 1) * 2048] for (b, hf) in segs[c * nseg:(c + 1) * nseg]])
            in_maps.append({"xT": np.ascontiguousarray(xs), "w": np.ascontiguousarray(W["w_in"][li]), "g": gl})
        res = run_bass_kernel_spmd(ncA, in_maps, core_ids=list(range(NDENSE)))
        zfull = [np.empty((12512, T_), np.float32) for _ in range(B_)]
        for c in range(NDENSE):
            for j, (b, hf) in enumerate(segs[c * nseg:(c + 1) * nseg]):
                zfull[b][:, hf * 2048:(hf + 1) * 2048] = res.results[c]["zT"][j]
        del res, in_maps
        in_maps = [attn_inputs_fm(zfull[c // 2], W["na_bias"][li], c % 2) for c in range(8)]
        resa = run_bass_kernel_spmd(ncB1, in_maps, core_ids=list(range(8))).results
        del in_maps
        prm_in = {k: W[k][li] for k in W if k.startswith("rw_")}
        in_maps = [rwkv_inputs(zfull[c // 2][0:1760], c % 2, prm_in) for c in range(8)]
        resr = run_bass_kernel_spmd(ncB2, in_maps, core_ids=list(range(8))).results
        del in_maps
        final = (li == 1)
        ncC = build_post(final, nseg=nseg)
        wmap = {"w_br_a": W["w_br_a"][li], "w_br_b": W["w_br_b"][li], "w_br_c": W["w_br_c"][li], "w_out": W["w_out"][li],
                "w_gate": W["w_ffn_gate"][li], "w_up": W["w_ffn_up"][li], "w_down": W["w_ffn_down"][li],
                "w_pg": W["w_ple_gate"][li], "w_ple": W["w_ple"][li]}
        wmap = {k: np.ascontiguousarray(v) for k, v in wmap.items()}
        yT = []
        for b in range(B_):
            ya = np.concatenate([resr[2 * b]["ya"], resr[2 * b + 1]["ya"]], axis=1)
            yb = np.concatenate([resa[2 * b]["yb"], resa[2 * b + 1]["yb"]], axis=0)
            yc = np.concatenate([resa[2 * b]["yc"], resa[2 * b + 1]["yc"]], axis=0)
            yT.append(np.ascontiguousarray(np.concatenate([ya, yb, yc], axis=1).T))
        in_maps = []
        for c in range(NDENSE):
            per = [post_inputs_fm(xfm[b], yT[b], zfull[b][OFF["g"]:], np.ascontiguousarray(p[li, b].T), hf, W, li)
                   for (b, hf) in segs[c * nseg:(c + 1) * nseg]]
            m = {k: np.ascontiguousarray(np.stack([q[k] for q in per])) for k in ("xT", "yT", "gT", "pT")}
            m["nrm"] = per[0]["nrm"]
            m["cw"] = per[0]["cw"]
            m.update(wmap)
            in_maps.append(m)
        res = run_bass_kernel_spmd(ncC, in_maps, core_ids=list(range(NDENSE)))
        for c in range(NDENSE):
            for j, (b, hf) in enumerate(segs[c * nseg:(c + 1) * nseg]):
                o = res.results[c]["out"][j]
                if final:
                    out[b, hf * 2048:(hf + 1) * 2048, :] = o.T
                else:
                    xfm[b][:, hf * 2048:(hf + 1) * 2048] = o
        del res, in_maps, zfull
    return out
```

```python
import os, sys, time
from contextlib import ExitStack
import numpy as np
import concourse.bass as bass
import concourse.mybir as mybir
from concourse.bass_utils import run_bass_kernel_spmd

F32 = mybir.dt.float32
BF16 = mybir.dt.bfloat16
AF = mybir.ActivationFunctionType
ALU = mybir.AluOpType
AX = mybir.AxisListType

EPOCH = 30000


class Res:
    __slots__ = ("name", "lw", "rd", "dsem", "dval")

    def __init__(self, name):
        self.name = name
        self.lw = None
        self.rd = {}
        self.dsem = None
        self.dval = 0


class _Rec:
    def __init__(self):
        self.call = None

    def __getattr__(self, name):
        def f(*a, **k):
            self.call = (name, a, k)
            return self
        return f


class Prog:
    ENG = ("tensor", "vector", "scalar", "gpsimd", "sync")

    def __init__(self, nc, stack):
        self.nc = nc
        self.stack = stack
        self.streams = {e: [] for e in self.ENG}
        self.cur = {}
        self.sems = {}
        self.seen = {e: {} for e in self.ENG}
        self.nsem = 0
        self.dma_final = {}
        for e in self.ENG:
            self._new_epoch(e)

    def _mksem(self, name):
        h = self.stack.enter_context(self.nc.semaphore(name))
        key = name
        self.sems[key] = h
        self.nsem += 1
        return key

    def _new_epoch(self, e):
        k = self._mksem("s_%s_%d" % (e, self.nsem))
        self.cur[e] = [k, 0]

    def res(self, name):
        return Res(name)

    def _deps(self, eng, reads, writes, skip_key=None):
        need = {}
        def add(dep):
            if dep is None:
                return
            k, v = dep
            if k == skip_key:
                return
            if need.get(k, 0) < v:
                need[k] = v
        for r in reads:
            if r.lw is not None and need.get(r.lw[0], 0) < r.lw[1]:
                need[r.lw[0]] = r.lw[1]
        for w in writes:
            add(w.lw)
            for k, v in w.rd.items():
                if need.get(k, 0) < v:
                    need[k] = v
        out = []
        seen = self.seen[eng]
        mykey = self.cur[eng][0]
        for k, v in need.items():
            if eng == "tensor" and k == mykey:
                continue
            if seen.get(k, 0) >= v:
                continue
            seen[k] = v
            out.append((k, v))
        return out

    def op(self, eng, fn, reads=(), writes=()):
        if self.cur[eng][1] >= EPOCH:
            self._new_epoch(eng)
        waits = self._deps(eng, reads, writes)
        cur = self.cur[eng]
        cur[1] += 1
        key, val = cur[0], cur[1]
        rec = _Rec()
        fn(rec)
        name, a, k = rec.call
        self.streams[eng].append((waits, (lambda e, name=name, a=a, k=k: getattr(e, name)(*a, **k)), key, 1))
        for w in writes:
            w.lw = (key, val)
            w.rd = {}
        for r in reads:
            if r not in writes:
                r.rd[key] = val

    def dma(self, eng, out, in_, reads=(), writes=(), owner=None):
        if owner is None:
            owner = writes[0] if writes else reads[0]
        if owner.dsem is None:
            owner.dsem = self._mksem("d_%s_%d" % (owner.name, self.nsem))
        key = owner.dsem
        waits = self._deps(eng, reads, writes, skip_key=key)
        owner.dval += 16
        val = owner.dval
        self.dma_final[key] = val

        def fn(e, out=out, in_=in_):
            return e.dma_start(out=out, in_=in_)
        self.streams[eng].append((waits, fn, key, 16))
        for w in writes:
            w.lw = (key, val)
            w.rd = {}
        for r in reads:
            if r not in writes:
                r.rd[key] = val

    def finish(self):
        waits = []
        for k, v in self.dma_final.items():
            if self.seen["sync"].get(k, 0) < v:
                waits.append((k, v))
        self.streams["sync"].append((waits, None, None, 0))

    def emit(self):
        nc = self.nc
        sems = self.sems
        streams = self.streams
        with nc.Block() as block:
            def mk(ename):
                def body(e):
                    for waits, fn, key, inc in streams[ename]:
                        for k, v in waits:
                            e.wait_ge(sems[k], v)
                        if fn is not None:
                            fn(e).then_inc(sems[key], inc)
                return body
            block.tensor(mk("tensor"))
            block.vector(mk("vector"))
            block.scalar(mk("scalar"))
            block.gpsimd(mk("gpsimd"))
            block.sync(mk("sync"))


NORM_EPS = 1e-6


def build_inproj(D, NT, NCOLS, CG=256, TB=256, nseg=1):
    KC = D // 128
    nc = bass.Bass("TRN2", target_bir_lowering=False)
    xT_all = nc.dram_tensor("xT", [nseg, D, NT], F32, kind="ExternalInput").ap()
    w = nc.dram_tensor("w", [D, NCOLS], F32, kind="ExternalInput").ap()
    g = nc.dram_tensor("g", [128, KC], F32, kind="ExternalInput").ap()
    zT_all = nc.dram_tensor("zT", [nseg, NCOLS, NT], F32, kind="ExternalOutput").ap()
    with ExitStack() as st:
        P = Prog(nc, st)
        sb = lambda name, shape, dt: st.enter_context(nc.sbuf_tensor("sb_" + name, shape, dt))
        ps = lambda name, shape, dt: st.enter_context(nc.psum_tensor("ps_" + name, shape, dt))
        hT = sb("hT", [128, KC, NT], BF16); r_hT = P.res("hT")
        gs = sb("gs", [128, KC], F32); r_gs = P.res("gs")
        ones = sb("ones", [128, 128], F32); r_ones = P.res("ones")
        xb = [sb("xb%d" % i, [128, KC, TB], F32) for i in range(2)]
        r_xb = [P.res("xb%d" % i) for i in range(2)]
        sq = [sb("sq%d" % i, [128, KC, TB], F32) for i in range(2)]
        r_sq = [P.res("sq%d" % i) for i in range(2)]
        rs = [sb("rs%d" % i, [128, TB], F32) for i in range(2)]
        r_rs = [P.res("rs%d" % i) for i in range(2)]
        wf = [sb("wf%d" % i, [128, KC, CG], F32) for i in range(2)]
        r_wf = [P.res("wf%d" % i) for i in range(2)]
        wb = [sb("wb%d" % i, [128, KC, CG], BF16) for i in range(2)]
        r_wb = [P.res("wb%d" % i) for i in range(2)]
        ob = [sb("ob%d" % i, [128, NT], F32) for i in range(2)]
        r_ob = [P.res("ob%d" % i) for i in range(2)]
        pn = ps("pn", [128, 512], F32); r_pn = P.res("pn")
        pm = [ps("pm%d" % i, [128, 512], F32) for i in range(4)]
        r_pm = [P.res("pm%d" % i) for i in range(4)]

        P.dma("sync", gs[:, :], g[:, :], writes=[r_gs])
        P.op("vector", lambda e: e.memset(ones[:, :], 1.0), writes=[r_ones])
        wv = w.rearrange("(c p) n -> p c n", p=128)
        oc = 0
        pmi = 0
        for sg in range(nseg):
            xTv = xT_all[sg].rearrange("(c p) t -> p c t", p=128)
            zT = zT_all[sg]
            for tb in range(NT // TB):
                i = tb % 2
                P.dma("sync" if i == 0 else "gpsimd", xb[i][:, :, :], xTv[:, :, tb * TB:(tb + 1) * TB], writes=[r_xb[i]])
                P.op("scalar", lambda e, i=i: e.activation(out=sq[i][:, :, :], in_=xb[i][:, :, :], func=AF.Square),
                     reads=[r_xb[i]], writes=[r_sq[i]])
                for kc in range(KC):
                    P.op("tensor", lambda e, i=i, kc=kc: e.matmul(pn[:, 0:TB], ones[:, :], sq[i][:, kc, :],
                                                                   start=(kc == 0), stop=(kc == KC - 1)),
                         reads=[r_ones, r_sq[i]], writes=[r_pn])
                P.op("scalar", lambda e, i=i: e.activation(out=rs[i][:, :], in_=pn[:, 0:TB], func=AF.Sqrt,
                                                           scale=1.0 / D, bias=NORM_EPS),
                     reads=[r_pn], writes=[r_rs[i]])
                P.op("vector", lambda e, i=i: e.reciprocal(out=rs[i][:, :], in_=rs[i][:, :]),
                     reads=[r_rs[i]], writes=[r_rs[i]])
                for kc in range(KC):
                    P.op("vector", lambda e, i=i, kc=kc, tb=tb: e.scalar_tensor_tensor(
                        out=hT[:, kc, tb * TB:(tb + 1) * TB], in0=xb[i][:, kc, :], scalar=gs[:, kc:kc + 1],
                        op0=ALU.mult, in1=rs[i][:, :], op1=ALU.mult),
                        reads=[r_xb[i], r_gs, r_rs[i]], writes=[r_hT])
            ncg = (NCOLS + CG - 1) // CG
            for cg in range(ncg):
                i = cg % 2
                c0 = cg * CG
                cw = min(CG, NCOLS - c0)
                P.dma("sync" if i == 0 else "gpsimd", wf[i][:, :, 0:cw], wv[:, :, c0:c0 + cw], writes=[r_wf[i]])
                P.op("gpsimd" if i == 0 else "vector",
                     lambda e, i=i, cw=cw: e.tensor_copy(out=wb[i][:, :, 0:cw], in_=wf[i][:, :, 0:cw]),
                     reads=[r_wf[i]], writes=[r_wb[i]])
                for s0 in range(0, cw, 128):
                    sw = min(128, cw - s0)
                    o = oc % 2
                    oc += 1
                    for t4 in range(NT // 512):
                        b = pmi % 4
                        pmi += 1
                        for kc in range(KC):
                            P.op("tensor", lambda e, i=i, kc=kc, s0=s0, sw=sw, t4=t4, b=b: e.matmul(
                                pm[b][0:sw, :], wb[i][:, kc, s0:s0 + sw], hT[:, kc, t4 * 512:(t4 + 1) * 512],
                                start=(kc == 0), stop=(kc == KC - 1)),
                                reads=[r_wb[i], r_hT], writes=[r_pm[b]])
                        if t4 % 2 == 0:
                            P.op("scalar", lambda e, o=o, sw=sw, t4=t4, b=b: e.activation(
                                out=ob[o][0:sw, t4 * 512:(t4 + 1) * 512], in_=pm[b][0:sw, :], func=AF.Copy),
                                reads=[r_pm[b]], writes=[r_ob[o]])
                        else:
                            P.op("vector", lambda e, o=o, sw=sw, t4=t4, b=b: e.tensor_copy(
                                out=ob[o][0:sw, t4 * 512:(t4 + 1) * 512], in_=pm[b][0:sw, :]),
                                reads=[r_pm[b]], writes=[r_ob[o]])
                    P.dma("sync", zT[c0 + s0:c0 + s0 + sw, :], ob[o][0:sw, :], reads=[r_ob[o]])
        P.finish()
        P.emit()
    return nc


SCALE = 64 ** -0.5
NTQ = 2048
WIN = 4096
NQB = NTQ // 128
KOFF = 8
DIL = ((128, 1), (512, 4), (2048, 16))
NEGFILL = -30000.0


def dil_deltas():
    out = []
    for g, (window, dil) in enumerate(DIL):
        r = (64 * dil) // 128 if dil > 1 else 1
        out.append(list(range(-r, r + 1)))
    return out


def dil_masks():
    ms = []
    kk = np.arange(128)[:, None]
    qq = np.arange(128)[None, :]
    for g, (window, dil) in enumerate(DIL):
        for d in dil_deltas()[g]:
            diff = (kk + 128 * d) - qq
            ok = (diff % dil == 0) & (np.abs(diff) <= 64 * dil)
            ms.append(ok.astype(np.float32))
    return np.ascontiguousarray(np.stack(ms, axis=1))


def na_tables(na_bias_h, m_list):
    out = np.full((128, len(m_list) * 7, 128), NEGFILL, np.float32)
    for mi, m in enumerate(m_list):
        q = m * 128 + np.arange(128)
        qr, qc = q // 64, q % 64
        r0 = np.clip(qr - 4, 0, 56)
        wc0 = np.clip(qc - 8, 0, 48)
        for di, d in enumerate(range(-3, 4)):
            kt = m + d
            if kt < 0 or kt >= 32:
                continue
            k = kt * 128 + np.arange(128)
            kr, kc = k // 64, k % 64
            ok = ((kr[:, None] >= r0[None, :]) & (kr[:, None] < r0[None, :] + 8)
                  & (kc[:, None] >= wc0[None, :]) & (kc[:, None] < wc0[None, :] + 16))
            dy = np.clip(kr[:, None] - qr[None, :], -7, 7) + 7
            dx = np.clip(kc[:, None] - qc[None, :], -15, 15) + 15
            vals = na_bias_h[dy, dx]
            out[:, mi * 7 + di, :] = np.where(ok, vals, NEGFILL)
    return out


def na_slot_of_qb(qb):
    if qb < 2:
        return qb
    if qb >= NQB - 2:
        return 3 + (qb - (NQB - 2))
    return 2


def na_slot_m(half):
    base = half * NQB
    return [base + 0, base + 1, base + 2, base + NQB - 2, base + NQB - 1]


def emit_attn(nc, P, st, io, n_hg=4, n_na=12, n_qb=NQB):
    sb = lambda name, shape, dt: st.enter_context(nc.sbuf_tensor("sb_" + name, shape, dt))
    ps = lambda name, shape, dt: st.enter_context(nc.psum_tensor("ps_" + name, shape, dt))
    cq = sb("cq", [64, NTQ], F32); r_cq = P.res("cq")
    sq_ = sb("sq_", [64, NTQ], F32); r_sq = P.res("sq_")
    ck = sb("ck", [64, WIN], F32); r_ck = P.res("ck")
    sk = sb("sk", [64, WIN], F32); r_sk = P.res("sk")
    P.dma("sync", cq[:, :], io["cq"][:, :], writes=[r_cq])
    P.dma("sync", sq_[:, :], io["sq"][:, :], writes=[r_sq])
    P.dma("gpsimd", ck[:, :], io["ck"][:, :], writes=[r_ck])
    P.dma("gpsimd", sk[:, :], io["sk"][:, :], writes=[r_sk])
    nef = sb("nef", [128, 25 * 128], F32); r_nef = P.res("nef")
    nebs = [sb("neb%d" % i, [128, 7 * 128], BF16) for i in range(5)]; r_nebs = [P.res("neb%d" % i) for i in range(5)]
    mdf = nef[:, 0:25 * 128]; r_mdf = r_nef
    md = sb("md", [128, 25 * 128], BF16); r_md = P.res("md")
    P.dma("sync", mdf, io["md"].rearrange("p a q -> p (a q)"), writes=[r_mdf])
    P.op("vector", lambda e: e.tensor_copy(out=md[:, :], in_=mdf), reads=[r_mdf], writes=[r_md])

    stq = sb("stq", [64, NTQ], F32); r_stq = P.res("stq")
    stqp = sb("stqp", [64, NTQ], F32); r_stqp = P.res("stqp")
    stk = sb("stk", [64, WIN], F32); r_stk = P.res("stk")
    stkp = sb("stkp", [64, WIN], F32); r_stkp = P.res("stkp")
    stv = sb("stv", [128, 32, 65], F32); r_stv = P.res("stv")
    qb_ = [sb("qb%d" % i, [64, NTQ], BF16) for i in range(3)]; r_qb = [P.res("qb%d" % i) for i in range(3)]
    kb_ = [sb("kb%d" % i, [64, WIN], BF16) for i in range(3)]; r_kb = [P.res("kb%d" % i) for i in range(3)]
    vb_ = [sb("vb%d" % i, [128, 32, 65], BF16) for i in range(3)]; r_vb = [P.res("vb%d" % i) for i in range(3)]
    pss = [ps("pss%d" % i, [128, 512], F32) for i in range(2)]; r_pss = [P.res("pss%d" % i) for i in range(2)]
    pso = [ps("pso%d" % i, [128, 512], F32) for i in range(2)]; r_pso = [P.res("pso%d" % i) for i in range(2)]
    pe = [sb("pe%d" % i, [128, 512], BF16) for i in range(2)]; r_pe = [P.res("pe%d" % i) for i in range(2)]
    pm = [sb("pm%d" % i, [128, 512], BF16) for i in range(2)]; r_pm = [P.res("pm%d" % i) for i in range(2)]
    rec = sb("rec", [128, 2], F32); r_rec = [P.res("rec0"), P.res("rec1")]
    yh = [sb("yh%d" % i, [128, NQB, 64], F32) for i in range(2)]; r_yh = [P.res("yh%d" % i) for i in range(2)]
    cnt = {"c": 0, "o": 0}

    def load_head(slot, qsrc, qpsrc, ksrc, kpsrc, vsrc, rope):
        P.dma("sync", stq[:, :], qsrc, writes=[r_stq])
        P.dma("sync", stk[:, :], ksrc, writes=[r_stk])
        P.dma("gpsimd", stv[:, :, :], vsrc, writes=[r_stv])
        if rope:
            P.dma("gpsimd", stqp[0:32, :], qsrc[32:64, :], writes=[r_stqp])
            P.dma("gpsimd", stqp[32:64, :], qsrc[0:32, :], writes=[r_stqp])
            P.dma("gpsimd", stkp[0:32, :], ksrc[32:64, :], writes=[r_stkp])
            P.dma("gpsimd", stkp[32:64, :], ksrc[0:32, :], writes=[r_stkp])
            P.op("vector", lambda e: e.tensor_tensor(out=stq[:, :], in0=stq[:, :], in1=cq[:, :], op=ALU.mult),
                 reads=[r_stq, r_cq], writes=[r_stq])
            P.op("gpsimd", lambda e: e.tensor_tensor(out=stqp[:, :], in0=stqp[:, :], in1=sq_[:, :], op=ALU.mult),
                 reads=[r_stqp, r_sq], writes=[r_stqp])
            P.op("vector", lambda e: e.tensor_tensor(out=qb_[slot][:, :], in0=stq[:, :], in1=stqp[:, :], op=ALU.add),
                 reads=[r_stq, r_stqp], writes=[r_qb[slot]])
            P.op("vector", lambda e: e.tensor_tensor(out=stk[:, :], in0=stk[:, :], in1=ck[:, :], op=ALU.mult),
                 reads=[r_stk, r_ck], writes=[r_stk])
            P.op("gpsimd", lambda e: e.tensor_tensor(out=stkp[:, :], in0=stkp[:, :], in1=sk[:, :], op=ALU.mult),
                 reads=[r_stkp, r_sk], writes=[r_stkp])
            P.op("vector", lambda e: e.tensor_tensor(out=kb_[slot][:, :], in0=stk[:, :], in1=stkp[:, :], op=ALU.add),
                 reads=[r_stk, r_stkp], writes=[r_kb[slot]])
        else:
            P.op("vector", lambda e: e.tensor_copy(out=qb_[slot][:, :], in_=stq[:, :]), reads=[r_stq], writes=[r_qb[slot]])
            P.op("gpsimd", lambda e: e.tensor_copy(out=kb_[slot][:, :], in_=stk[:, :]), reads=[r_stk], writes=[r_kb[slot]])
        P.op("gpsimd", lambda e: e.tensor_copy(out=vb_[slot][:, :, :], in_=stv[:, :, :]), reads=[r_stv], writes=[r_vb[slot]])

    def qblock(qb, chunks, E, r_E, ydst, r_y):
        o = cnt["o"] % 2
        cnt["o"] += 1
        ntile = sum(len(c) for c in chunks)
        ti = 0
        for ch in chunks:
            c = cnt["c"] % 2
            cnt["c"] += 1
            n = len(ch)
            for j, (s, kt, ei) in enumerate(ch):
                P.op("tensor", lambda e, c=c, j=j, s=s, kt=kt: e.matmul(
                    pss[c][:, j * 128:(j + 1) * 128], kb_[s][:, kt * 128:(kt + 1) * 128],
                    qb_[s][:, qb * 128:(qb + 1) * 128], start=True, stop=True),
                    reads=[r_kb[s], r_qb[s]], writes=[r_pss[c]])
            P.op("scalar", lambda e, c=c, n=n: e.activation(out=pe[c][:, 0:n * 128], in_=pss[c][:, 0:n * 128],
                                                           func=AF.Exp, scale=SCALE),
                 reads=[r_pss[c]], writes=[r_pe[c]])
            e0 = ch[0][2]
            P.op("vector" if c == 0 else "gpsimd", lambda e, c=c, n=n, e0=e0: e.tensor_tensor(
                out=pm[c][:, 0:n * 128], in0=pe[c][:, 0:n * 128], in1=E[:, e0 * 128:(e0 + n) * 128], op=ALU.mult),
                reads=[r_pe[c], r_E], writes=[r_pm[c]])
            for j, (s, kt, ei) in enumerate(ch):
                P.op("tensor", lambda e, c=c, j=j, s=s, kt=kt, ti=ti: e.matmul(
                    pso[o][:, 0:65], pm[c][:, j * 128:(j + 1) * 128], vb_[s][:, kt, :],
                    start=(ti == 0), stop=(ti == ntile - 1)),
                    reads=[r_pm[c], r_vb[s]], writes=[r_pso[o]])
                ti += 1
        P.op("vector", lambda e, o=o: e.reciprocal(out=rec[:, o:o + 1], in_=pso[o][:, 64:65]),
             reads=[r_pso[o]], writes=[r_rec[o]])
        P.op("vector", lambda e, o=o: e.tensor_scalar(out=ydst, in0=pso[o][:, 0:64], scalar1=rec[:, o:o + 1],
                                                      scalar2=None, op0=ALU.mult),
             reads=[r_pso[o], r_rec[o]], writes=[r_y])

    dd = dil_deltas()
    ebase = [0, 3, 8]
    for hg in range(n_hg):
        for g in range(3):
            h = g * 4 + hg
            load_head(g, io["dq"][h], None, io["dk"][h], None, io["dv"][h], True)
        for qb in range(n_qb):
            chunks = []
            for g in range(3):
                tl = [(g, KOFF + qb + d, ebase[g] + di) for di, d in enumerate(dd[g])]
                for a in range(0, len(tl), 4):
                    chunks.append(tl[a:a + 4])
            qblock(qb, chunks, md, r_md, yh[hg % 2][:, qb, :], r_yh[hg % 2])
        P.dma("sync", io["yb"].rearrange("(qb p) c -> p qb c", p=128)[:, :, hg * 64:(hg + 1) * 64], yh[hg % 2][:, :, :],
              reads=[r_yh[hg % 2]])
    for h in range(n_na):
        load_head(0, io["nq"][h], None, io["nk"][h], None, io["nv"][h], False)
        for sl_ in range(5):
            P.dma("sync", nef[:, 0:7 * 128], io["ne"][h][:, sl_ * 7:(sl_ + 1) * 7, :].rearrange("p a q -> p (a q)"), writes=[r_nef])
            P.op("scalar", lambda e, sl_=sl_: e.activation(out=nebs[sl_][:, :], in_=nef[:, 0:7 * 128], func=AF.Exp),
                 reads=[r_nef], writes=[r_nebs[sl_]])
        for qb in range(n_qb):
            sl = na_slot_of_qb(qb)
            tl = [(0, KOFF + qb + d, di) for di, d in enumerate(range(-3, 4))]
            chunks = [tl[0:4], tl[4:7]]
            qblock(qb, chunks, nebs[sl], r_nebs[sl], yh[h % 2][:, qb, :], r_yh[h % 2])
        P.dma("sync", io["yc"].rearrange("(qb p) c -> p qb c", p=128)[:, :, h * 64:(h + 1) * 64], yh[h % 2][:, :, :],
              reads=[r_yh[h % 2]])


def build_attn(**kw):
    nc = bass.Bass("TRN2", target_bir_lowering=False)
    di = lambda name, shape: nc.dram_tensor(name, shape, F32, kind="ExternalInput").ap()
    io = {
        "cq": di("cq", [64, NTQ]), "sq": di("sq", [64, NTQ]), "ck": di("ck", [64, WIN]), "sk": di("sk", [64, WIN]),
        "md": di("md", [128, 25, 128]),
        "dq": di("dq", [12, 64, NTQ]),
        "dk": di("dk", [12, 64, WIN]), "dv": di("dv", [12, 128, 32, 65]),
        "nq": di("nq", [12, 64, NTQ]), "nk": di("nk", [12, 64, WIN]), "nv": di("nv", [12, 128, 32, 65]),
        "ne": di("ne", [12, 128, 35, 128]),
        "yb": nc.dram_tensor("yb", [NTQ, 256], F32, kind="ExternalOutput").ap(),
        "yc": nc.dram_tensor("yc", [NTQ, 768], F32, kind="ExternalOutput").ap(),
    }
    with ExitStack() as st:
        P = Prog(nc, st)
        emit_attn(nc, P, st, io, **kw)
        P.finish()
        P.emit()
    return nc


def rope_consts():
    inv = 10000.0 ** (-np.arange(0, 64, 2, dtype=np.float32) / 64)
    ang = np.arange(4096, dtype=np.float32)[:, None] * inv[None, :]
    cos, sin = np.cos(ang).astype(np.float32), np.sin(ang).astype(np.float32)
    C = np.concatenate([cos, cos], axis=1).T
    S = np.concatenate([-sin, sin], axis=1).T
    return np.ascontiguousarray(C), np.ascontiguousarray(S)


def window(a, half, axis):
    t0 = half * NTQ
    lo, hi = t0 - 1024, t0 + 3072
    pad_lo, pad_hi = max(0, -lo), max(0, hi - 4096)
    sl = [slice(None)] * a.ndim
    sl[axis] = slice(max(lo, 0), min(hi, 4096))
    b = a[tuple(sl)]
    pw = [(0, 0)] * a.ndim
    pw[axis] = (pad_lo, pad_hi)
    return np.pad(b, pw)


def attn_inputs(dq, dk, dv, nq, nk, nv, na_bias, half):
    C, S = rope_consts()
    T = 4096
    t0 = half * NTQ
    perm = np.concatenate([np.arange(32, 64), np.arange(0, 32)])
    def fm(a):
        return a.reshape(T, 12, 64).transpose(1, 2, 0)
    def vaug(v):
        vh = v.reshape(T, 12, 64).transpose(1, 0, 2)
        va = np.concatenate([vh, np.ones((12, T, 1), np.float32)], axis=2)
        vw = window(va, half, 1)
        return np.ascontiguousarray(vw.reshape(12, 32, 128, 65).transpose(0, 2, 1, 3))
    dqf, dkf, nqf, nkf = fm(dq), fm(dk), fm(nq), fm(nk)
    m = {
        "cq": np.ascontiguousarray(C[:, t0:t0 + NTQ]), "sq": np.ascontiguousarray(S[:, t0:t0 + NTQ]),
        "ck": np.ascontiguousarray(window(C, half, 1)), "sk": np.ascontiguousarray(window(S, half, 1)),
        "md": dil_masks(),
        "dq": np.ascontiguousarray(dqf[:, :, t0:t0 + NTQ]),
        "dk": np.ascontiguousarray(window(dkf, half, 2)),
        "dv": vaug(dv),
        "nq": np.ascontiguousarray(nqf[:, :, t0:t0 + NTQ]), "nk": np.ascontiguousarray(window(nkf, half, 2)),
        "nv": vaug(nv),
        "ne": np.ascontiguousarray(np.stack([na_tables(na_bias[h], na_slot_m(half)) for h in range(12)])),
    }
    return m


T = 4096
L = 64
SEG = 256
NC = SEG // L
NSEG = T // SEG
NST = NC * 4
NBK = NST // 8
C0 = -float(np.exp(-0.5))
LNX_EPS = 64e-5


def emit_rwkv(nc, P, st, io, nseg=NSEG, dirs=(0, 1), stage=9):
    sb = lambda name, shape, dt=F32: st.enter_context(nc.sbuf_tensor("sb_" + name, shape, dt))
    ps = lambda name, shape, dt=F32: st.enter_context(nc.psum_tensor("ps_" + name, shape, dt))
    V = lambda fn, r, w: P.op("vector", fn, reads=r, writes=w)
    G = lambda fn, r, w: P.op("gpsimd", fn, reads=r, writes=w)
    A = lambda fn, r, w: P.op("scalar", fn, reads=r, writes=w)
    M = lambda fn, r, w: P.op("tensor", fn, reads=r, writes=w)

    def tile(name, shape, dt=F32):
        return sb(name, shape, dt), P.res(name)

    ident, r_ident = tile("ident", [128, 128])
    icat, r_icat = tile("icat", [128, 64])
    bones, r_bones = tile("bones", [128, 128])
    msk, r_msk = tile("msk", [64, 4, 64])
    rmask, r_rmask = tile("rmask", [128, SEG])
    prm, r_prm = tile("prm", [128, 32])
    mul_, r_mul = tile("mul", [128, 3])
    w2s, r_w2s = tile("w2s", [128, 512])
    a2s, r_a2s = tile("a2s", [128, 512])
    g2s, r_g2s = tile("g2s", [128, 256])
    hm, r_hm = tile("hm", [128, 2])
    lng, r_lng = tile("lng", [64, 256])
    lnb, r_lnb = tile("lnb", [64, 256])
    for (t_, r_, src) in ((ident, r_ident, "ident"), (icat, r_icat, "icat"), (bones, r_bones, "bones"),
                          (rmask, r_rmask, "rmask"), (prm, r_prm, "prm"), (mul_, r_mul, "mul"),
                          (w2s, r_w2s, "w2s"), (a2s, r_a2s, "a2s"), (g2s, r_g2s, "g2s"), (hm, r_hm, "hm"),
                          (lng, r_lng, "lng"), (lnb, r_lnb, "lnb")):
        P.dma("sync", t_[:, :], io[src][:, :], writes=[r_])
    P.dma("sync", msk[:, :, :], io["msk"][:, :, :], writes=[r_msk])
    V(lambda e: e.tensor_scalar(out=prm[:, 18:20], in0=prm[:, 16:18], scalar1=-1.0, scalar2=1.0, op0=ALU.mult, op1=ALU.add),
      [r_prm], [r_prm])
    pc_col = lambda base, pc: prm[:, base + pc:base + pc + 1]

    SP = SEG + 2
    raw = {}
    for nm in ("r0", "r1", "k0", "k1", "v0", "v1"):
        raw[nm] = tile("raw_" + nm, [128, SP])
    raw["wd"] = tile("raw_wd", [64, SP]); raw["ad"] = tile("raw_ad", [64, SP]); raw["gd"] = tile("raw_gd", [96, SP])
    shf = {}
    for nm in ("r0", "r1", "k0", "k1", "v0", "v1"):
        shf[nm] = tile("shf_" + nm, [128, SEG])
    shf["wd"] = tile("shf_wd", [64, SEG]); shf["ad"] = tile("shf_ad", [128, SEG]); shf["gd"] = tile("shf_gd", [96, SEG])
    tmpa, r_tmpa = tile("tmpa", [128, SEG]); tmpb, r_tmpb = tile("tmpb", [128, SEG])
    tw, r_tw = tile("tw", [128, SEG])
    sgd, r_sgd = tile("sgd", [128, SEG])
    fm = {}
    for nm in ("lw", "lr", "kk", "kd", "bb", "cs", "ci", "ce", "e1", "e2", "e3", "e4", "t1", "t2", "lr0", "kd0", "bon"):
        fm[nm] = tile("fm_" + nm, [128, SEG])
    tot, r_tot = tile("tot", [128, NC]); WL, r_WL = tile("WL", [128, NC])
    outf = {}
    for nm in ("Af", "Bf", "Kf", "Rf", "Bhf", "Khf"):
        for pc in range(2):
            outf[nm, pc] = tile("of_%s%d" % (nm, pc), [128, SEG])
    Dg = [tile("Dg%d" % pc, [128, NC, 64]) for pc in range(2)]
    XT = {}
    for nm in ("At", "Bht", "Kht", "Vt", "bont"):
        XT[nm] = tile("xt_" + nm, [128, NC, 256])
    gat, r_gat = tile("gat", [64, NC, 256])
    if os.environ.get("PADKB"):
        tile("padx", [128, 256 * int(os.environ["PADKB"])])
    Wd = [tile("wide%d" % i, [128, NST * 64]) for i in range(10)]
    Sst, r_Sst = tile("Sst", [128, NC + 1, 256])
    mskd = {}
    for nm in ("Af", "Bf", "Rf"):
        for pc in range(2):
            for hh in range(2):
                mskd[nm, pc, hh] = tile("mk_%s%d%d" % (nm, pc, hh), [128, SEG])
    Dgm = {(pc, hh): tile("Dgm%d%d" % (pc, hh), [128, NC, 64]) for pc in range(2) for hh in range(2)}
    for (t_, r_) in [Wd[i] for i in range(10)] + [XT[k] for k in XT]:
        G(lambda e, t_=t_: e.memset(t_[:], 0.0), [], [r_])
    for (t_, r_) in ((tw, r_tw), (sgd, r_sgd), shf["ad"]):
        G(lambda e, t_=t_: e.memset(t_[:, :], 0.0), [], [r_])
    V(lambda e: e.memset(Sst[:, :, :], 0.0), [], [r_Sst])
    yfb, r_yfb = tile("yfb", [64, NC, 256])
    yo, r_yo = tile("yo", [64, NC, 256])
    stat, r_stat = tile("stat", [64, NC * 4 * 2])
    pbank = [(ps("pb%d" % i, [128, 512]), P.res("pb%d" % i)) for i in range(8)]
    bk = {"i": 0}

    def nextbank():
        b = pbank[bk["i"] % 8]
        bk["i"] += 1
        return b

    def shift(nm, rows, mu_ap, r_mu):
        (rw, r_rw), (o, r_o) = raw[nm], shf[nm]
        G(lambda e: e.tensor_tensor(out=tmpa[0:rows, :], in0=rw[0:rows, 0:SEG], in1=rw[0:rows, 2:SEG + 2], op=ALU.add),
          [r_rw], [r_tmpa])
        V(lambda e: e.scalar_tensor_tensor(out=tmpb[0:rows, :], in0=tmpa[0:rows, :], scalar=0.5, op0=ALU.mult,
                                           in1=rw[0:rows, 1:SEG + 1], op1=ALU.subtract), [r_tmpa, r_rw], [r_tmpb])
        V(lambda e: e.scalar_tensor_tensor(out=o[0:rows, :], in0=tmpb[0:rows, :], scalar=mu_ap, op0=ALU.mult,
                                           in1=rw[0:rows, 1:SEG + 1], op1=ALU.add), [r_tmpb, r_rw, r_mu], [r_o])

    def lora_sig(d, pc, src, r_src, wts, r_wts, bias_base, out, r_out):
        pb, r_pb = nextbank()
        M(lambda e: e.matmul(pb[:, 0:SEG], wts[:, d * 256 + pc * 128:d * 256 + (pc + 1) * 128],
                             src[:, :], start=True, stop=True), [r_wts, r_src], [r_pb])
        A(lambda e: e.activation(out=out[:, :], in_=pb[:, 0:SEG], func=AF.Sigmoid,
                                 bias=prm[:, bias_base + d * 2 + pc:bias_base + d * 2 + pc + 1]), [r_pb, r_prm], [r_out])

    def kd_from(pc, lr_t, r_lr, out, r_out):
        kp, r_kp = shf["k%d" % pc]
        V(lambda e: e.tensor_scalar(out=fm["t1"][0][:, :], in0=lr_t[:, :], scalar1=pc_col(16, pc), scalar2=pc_col(18, pc),
                                    op0=ALU.mult, op1=ALU.add), [r_lr, r_prm], [fm["t1"][1]])
        V(lambda e: e.tensor_tensor(out=out[:, :], in0=fm["t1"][0][:, :], in1=kp[:, :], op=ALU.mult),
          [fm["t1"][1], r_kp], [r_out])

    yf_res = P.res("yf_dram")

    for d in dirs:
        segs = list(range(nseg)) if d == 0 else list(range(nseg - 1, -1, -1))
        V(lambda e: e.memset(Sst[0:64, 0, :], 0.0), [], [r_Sst])
        for s in segs:
            s0 = s * SEG
            for wi, wn in enumerate(("r", "k", "v")):
                for pc in range(2):
                    t_, r_ = raw["%s%d" % (wn, pc)]
                    P.dma("sync" if pc == 0 else "gpsimd", t_[:, :], io["rkv"][wi, pc * 128:(pc + 1) * 128, s0:s0 + SP], writes=[r_])
            for nm, rows in (("wd", 64), ("ad", 64), ("gd", 96)):
                t_, r_ = raw[nm]
                P.dma("sync", t_[:, :], io[nm][:, s0:s0 + SP], writes=[r_])
            for wi, wn in enumerate(("r", "k", "v")):
                for pc in range(2):
                    shift("%s%d" % (wn, pc), 128, prm[:, wi * 2 + pc:wi * 2 + pc + 1], r_prm)
            shift("wd", 64, mul_[0:64, 0:1], r_mul)
            shift("ad", 64, mul_[0:64, 1:2], r_mul)
            if d == 1:
                shift("gd", 96, mul_[0:96, 2:3], r_mul)
            if stage < 2:
                continue
            A(lambda e: e.activation(out=tw[0:64, :], in_=shf["wd"][0][:, :], func=AF.Tanh), [shf["wd"][1]], [r_tw])
            for pc in range(2):
                kp, r_kp = shf["k%d" % pc]
                rp, r_rp = shf["r%d" % pc]
                F = lambda nm: fm[nm][0]
                Rr = lambda nm: fm[nm][1]
                lora_sig(d, pc, tw, r_tw, w2s, r_w2s, 6, F("lw"), Rr("lw"))
                lora_sig(d, pc, shf["ad"][0], shf["ad"][1], a2s, r_a2s, 10, F("lr"), Rr("lr"))
                V(lambda e: e.tensor_scalar(out=F("lw")[:, :], in0=F("lw")[:, :], scalar1=C0, scalar2=None, op0=ALU.mult),
                  [Rr("lw")], [Rr("lw")])
                V(lambda e: e.tensor_scalar(out=F("t1")[:, :], in0=kp[:, :], scalar1=pc_col(14, pc), scalar2=None, op0=ALU.mult),
                  [r_kp, r_prm], [Rr("t1")])
                A(lambda e: e.activation(out=F("t2")[:, :], in_=F("t1")[:, :], func=AF.Square), [Rr("t1")], [Rr("t2")])
                pb, r_pb = nextbank()
                M(lambda e, pb=pb: e.matmul(pb[:, 0:SEG], bones[:, :], F("t2")[:, :], start=True, stop=True),
                  [r_bones, Rr("t2")], [r_pb])
                A(lambda e, pb=pb: e.activation(out=F("t2")[:, :], in_=pb[:, 0:SEG], func=AF.Sqrt), [r_pb], [Rr("t2")])
                V(lambda e: e.tensor_scalar(out=F("t2")[:, :], in0=F("t2")[:, :], scalar1=1e-12, scalar2=None, op0=ALU.max),
                  [Rr("t2")], [Rr("t2")])
                V(lambda e: e.reciprocal(out=F("t2")[:, :], in_=F("t2")[:, :]), [Rr("t2")], [Rr("t2")])
                V(lambda e: e.tensor_tensor(out=F("kk")[:, :], in0=F("t1")[:, :], in1=F("t2")[:, :], op=ALU.mult),
                  [Rr("t1"), Rr("t2")], [Rr("kk")])
                kd_from(pc, F("lr"), Rr("lr"), F("kd"), Rr("kd"))
                G(lambda e: e.tensor_tensor(out=F("bb")[:, :], in0=F("kk")[:, :], in1=F("lr")[:, :], op=ALU.mult),
                  [Rr("kk"), Rr("lr")], [Rr("bb")])
                V(lambda e: e.tensor_tensor_scan(out=F("cs")[:, :], data0=rmask[:, :], data1=F("lw")[:, :], initial=0.0,
                                                 op0=ALU.mult, op1=ALU.add), [r_rmask, Rr("lw")], [Rr("cs")])
                cs3 = F("cs")[:, :].rearrange("p (c l) -> p c l", l=L)
                V(lambda e: e.tensor_copy(out=tot[:, :].unsqueeze(2), in_=cs3[:, :, L - 1:L]), [Rr("cs")], [r_tot])
                totbc = tot[:, :].unsqueeze(2).broadcast_to([128, NC, L])
                v3 = lambda t_: t_[:, :].rearrange("p (c l) -> p c l", l=L)
                if d == 0:
                    ci, r_ci = F("cs"), Rr("cs")
                else:
                    ci, r_ci = F("ci"), Rr("ci")
                    V(lambda e: e.tensor_tensor(out=F("t1")[:, :], in0=F("lw")[:, :], in1=F("cs")[:, :], op=ALU.subtract),
                      [Rr("lw"), Rr("cs")], [Rr("t1")])
                    V(lambda e: e.tensor_tensor(out=v3(F("ci")), in0=v3(F("t1")), in1=totbc, op=ALU.add),
                      [Rr("t1"), r_tot], [Rr("ci")])
                V(lambda e: e.tensor_tensor(out=F("ce")[:, :], in0=ci[:, :], in1=F("lw")[:, :], op=ALU.subtract),
                  [r_ci, Rr("lw")], [Rr("ce")])
                V(lambda e: e.tensor_tensor(out=v3(F("t2")), in0=totbc, in1=v3(ci), op=ALU.subtract),
                  [r_ci, r_tot], [Rr("t2")])
                A(lambda e: e.activation(out=F("e1")[:, :], in_=F("ce")[:, :], func=AF.Exp), [Rr("ce")], [Rr("e1")])
                A(lambda e: e.activation(out=F("e2")[:, :], in_=ci[:, :], func=AF.Exp, scale=-1.0), [r_ci], [Rr("e2")])
                A(lambda e: e.activation(out=F("e3")[:, :], in_=ci[:, :], func=AF.Exp), [r_ci], [Rr("e3")])
                A(lambda e: e.activation(out=F("e4")[:, :], in_=F("t2")[:, :], func=AF.Exp), [Rr("t2")], [Rr("e4")])
                A(lambda e: e.activation(out=WL[:, :], in_=tot[:, :], func=AF.Exp), [r_tot], [r_WL])
                O = lambda nm: outf[nm, pc][0]
                Ro = lambda nm: outf[nm, pc][1]
                V(lambda e: e.scalar_tensor_tensor(out=O("Af")[:, :], in0=F("kk")[:, :], scalar=-1.0, op0=ALU.mult,
                                                   in1=F("e1")[:, :], op1=ALU.mult), [Rr("kk"), Rr("e1")], [Ro("Af")])
                G(lambda e: e.tensor_tensor(out=O("Bf")[:, :], in0=F("bb")[:, :], in1=F("e2")[:, :], op=ALU.mult),
                  [Rr("bb"), Rr("e2")], [Ro("Bf")])
                V(lambda e: e.tensor_tensor(out=O("Kf")[:, :], in0=F("kd")[:, :], in1=F("e2")[:, :], op=ALU.mult),
                  [Rr("kd"), Rr("e2")], [Ro("Kf")])
                G(lambda e: e.tensor_tensor(out=O("Rf")[:, :], in0=rp[:, :], in1=F("e3")[:, :], op=ALU.mult),
                  [r_rp, Rr("e3")], [Ro("Rf")])
                V(lambda e: e.tensor_tensor(out=O("Bhf")[:, :], in0=F("bb")[:, :], in1=F("e4")[:, :], op=ALU.mult),
                  [Rr("bb"), Rr("e4")], [Ro("Bhf")])
                G(lambda e: e.tensor_tensor(out=O("Khf")[:, :], in0=F("kd")[:, :], in1=F("e4")[:, :], op=ALU.mult),
                  [Rr("kd"), Rr("e4")], [Ro("Khf")])
                for nm_ in ("Af", "Bf", "Rf"):
                    for hh in range(2):
                        mt, r_mt = mskd[nm_, pc, hh]
                        (G if hh == 0 else V)(lambda e, mt=mt, nm_=nm_, hh=hh: e.tensor_scalar(
                            out=mt[:, :], in0=O(nm_)[:, :], scalar1=hm[:, hh:hh + 1], scalar2=None, op0=ALU.mult),
                            [Ro(nm_), r_hm], [r_mt])
                dg, r_dg = Dg[pc]
                V(lambda e, dg=dg: e.tensor_tensor(out=dg[:, :, :], in0=icat[:, :].unsqueeze(1).broadcast_to([128, NC, 64]),
                                                   in1=WL[:, :].unsqueeze(2).broadcast_to([128, NC, 64]), op=ALU.mult),
                  [r_icat, r_WL], [r_dg])
                for hh in range(2):
                    dm, r_dm = Dgm[pc, hh]
                    V(lambda e, dm=dm, dg=dg, hh=hh: e.tensor_scalar(out=dm[:, :, :], in0=dg[:, :, :], scalar1=hm[:, hh:hh + 1],
                                                                   scalar2=None, op0=ALU.mult), [r_dg, r_hm], [r_dm])
                if d == 1:
                    lora_sig(0, pc, shf["ad"][0], shf["ad"][1], a2s, r_a2s, 10, F("lr0"), Rr("lr0"))
                    kd_from(pc, F("lr0"), Rr("lr0"), F("kd0"), Rr("kd0"))
                    V(lambda e: e.tensor_tensor(out=F("kd0")[:, :], in0=F("kd0")[:, :], in1=F("kd")[:, :], op=ALU.add),
                      [Rr("kd0"), Rr("kd")], [Rr("kd0")])
                    V(lambda e: e.scalar_tensor_tensor(out=F("t1")[:, :], in0=rp[:, :], scalar=pc_col(20, pc), op0=ALU.mult,
                                                       in1=F("kd0")[:, :], op1=ALU.mult), [r_rp, r_prm, Rr("kd0")], [Rr("t1")])
                    pb, r_pb = nextbank()
                    M(lambda e, pb=pb: e.matmul(pb[:, 0:SEG], bones[:, :], F("t1")[:, :], start=True, stop=True),
                      [r_bones, Rr("t1")], [r_pb])
                    vp, r_vp = shf["v%d" % pc]
                    V(lambda e, pb=pb: e.tensor_tensor(out=F("bon")[:, :], in0=pb[:, 0:SEG], in1=vp[:, :], op=ALU.mult),
                      [r_pb, r_vp], [Rr("bon")])
                if stage < 3:
                    continue
                tlist = [("At", O("Af"), Ro("Af")), ("Bht", O("Bhf"), Ro("Bhf")), ("Kht", O("Khf"), Ro("Khf")),
                         ("Vt", shf["v%d" % pc][0], shf["v%d" % pc][1])]
                if d == 1:
                    tlist.append(("bont", F("bon"), Rr("bon")))
                for (xn, src, r_src) in tlist:
                    pb, r_pb = nextbank()
                    for c in range(NC):
                        M(lambda e, pb=pb, c=c, src=src: e.transpose(pb[0:64, c * 128:(c + 1) * 128], src[:, c * L:(c + 1) * L],
                                                                     ident[:, :]), [r_src, r_ident], [r_pb])
                    xt, r_xt = XT[xn]
                    A(lambda e, pb=pb, xt=xt: e.activation(out=xt[0:64, :, pc * 128:(pc + 1) * 128],
                                                           in_=pb[0:64, 0:NC * 128].rearrange("p (c n) -> p c n", n=128),
                                                           func=AF.Copy), [r_pb], [r_xt])
            if d == 1:
                A(lambda e: e.activation(out=sgd[0:96, :], in_=shf["gd"][0][:, :], func=AF.Sigmoid), [shf["gd"][1]], [r_sgd])
                pb, r_pb = nextbank()
                pb2, r_pb2 = nextbank()
                for c in range(NC):
                    tgt, r_tgt = (pb, r_pb) if c < 2 else (pb2, r_pb2)
                    M(lambda e, tgt=tgt, c=c: e.matmul(tgt[0:64, (c % 2) * 256:(c % 2 + 1) * 256], sgd[:, c * L:(c + 1) * L],
                                                       g2s[:, :], start=True, stop=True), [r_sgd, r_g2s], [r_tgt])
                V(lambda e, pb=pb: e.tensor_copy(out=gat[:, 0:2, :], in_=pb[0:64, :].rearrange("p (c n) -> p c n", n=256)),
                  [r_pb], [r_gat])
                V(lambda e, pb2=pb2: e.tensor_copy(out=gat[:, 2:4, :], in_=pb2[0:64, :].rearrange("p (c n) -> p c n", n=256)),
                  [r_pb2], [r_gat])

            if stage < 4:
                continue
            pcnt = {'n': 0}
            def fmop(nm, c, h):
                t_, r_ = outf[nm, h // 2]
                return t_[:, c * L:(c + 1) * L], r_

            def fmm(nm, c, h):
                t_, r_ = mskd[nm, h // 2, h % 2]
                return t_[:, c * L:(c + 1) * L], r_

            def tmop(nm, c, h):
                t_, r_ = XT[nm]
                return t_[:, c, h * 64:(h + 1) * 64], r_

            def wop(i, c, h):
                t_, r_ = Wd[i]
                stn = (h % 2) * 8 + c * 2 + h // 2
                return t_[:, stn * 64:(stn + 1) * 64], r_

            def product(terms_fn, evac_fn):
                pcnt["n"] += 1
                if pcnt["n"] > int(os.environ.get("MAXP", 999)):
                    return
                banks = [nextbank() for _ in range(NBK)]
                for c in range(NC):
                    for h in range(4):
                        if os.environ.get("EVENH") and h % 2 == 1:
                            continue
                        stn = (h % 2) * 8 + c * 2 + h // 2
                        pb, r_pb = banks[stn // 8]
                        terms = terms_fn(c, h)
                        for ti, (la, rl, ra, rr) in enumerate(terms):
                            M(lambda e, pb=pb, stn=stn, la=la, ra=ra, ti=ti, n=len(terms): e.matmul(
                                pb[0:64, (stn % 8) * 64:(stn % 8 + 1) * 64], la, ra, start=(ti == 0), stop=(ti == n - 1)),
                                [rl, rr], [r_pb])
                for bi, (pb, r_pb) in enumerate(banks):
                    evac_fn(bi, pb[0:64, :], r_pb)

            mi = {"su": 0, "sl": 1, "iu": 2, "il": 3}
            if d == 1:
                mi = {"su": 1, "sl": 0, "iu": 3, "il": 2}
            mbc = lambda nm: msk[:, mi[nm], :].unsqueeze(1).broadcast_to([64, 8, 64])
            w3 = lambda i, bi: Wd[i][0][0:64, bi * 512:(bi + 1) * 512].rearrange("p (s n) -> p s n", n=64)
            p3 = lambda pa: pa.rearrange("p (s n) -> p s n", n=64)
            ibc = ident[0:64, 0:64].unsqueeze(1).broadcast_to([64, 8, 64])
            eng_rr = {"i": 0}

            def ev_mask(wi, mname):
                def f(bi, pa, r_pb):
                    eng_rr["i"] += 1
                    V(lambda e: e.tensor_tensor(out=w3(wi, bi), in0=p3(pa), in1=mbc(mname), op=ALU.mult),
                      [r_pb, r_msk], [Wd[wi][1]])
                return f

            def ev_copy(wi, eng="scalar"):
                def f(bi, pa, r_pb):
                    if eng == "scalar":
                        A(lambda e: e.activation(out=Wd[wi][0][0:64, bi * 512:(bi + 1) * 512], in_=pa, func=AF.Copy),
                          [r_pb], [Wd[wi][1]])
                    else:
                        V(lambda e: e.tensor_copy(out=Wd[wi][0][0:64, bi * 512:(bi + 1) * 512], in_=pa), [r_pb], [Wd[wi][1]])
                return f

            def ev_copy_plus_ident(wi_raw, wi_id):
                def f(bi, pa, r_pb):
                    if wi_raw is not None:
                        A(lambda e: e.activation(out=Wd[wi_raw][0][0:64, bi * 512:(bi + 1) * 512], in_=pa, func=AF.Copy),
                          [r_pb], [Wd[wi_raw][1]])
                        G(lambda e: e.tensor_tensor(out=w3(wi_id, bi), in0=w3(wi_raw, bi), in1=ibc, op=ALU.add),
                          [Wd[wi_raw][1], r_ident], [Wd[wi_id][1]])
                    else:
                        V(lambda e: e.tensor_tensor(out=w3(wi_id, bi), in0=p3(pa), in1=ibc, op=ALU.add),
                          [r_pb, r_ident], [Wd[wi_id][1]])
                return f

            Pa, Pb_, Qa, Qb, IQ, Ta, Tb, MAK, NBR, NKR = range(10)
            def ev_N(bi, pa, r_pb):
                evn = int(os.environ.get("EVN", 2))
                if evn == 0:
                    return
                if evn == 1:
                    V(lambda e: e.tensor_tensor(out=w3(Pa, bi), in0=p3(pa), in1=mbc("su"), op=ALU.mult), [r_pb, r_msk], [Wd[Pa][1]])
                    return
                V(lambda e: e.tensor_tensor(out=w3(Pa, bi), in0=p3(pa), in1=mbc("su"), op=ALU.mult), [r_pb, r_msk], [Wd[Pa][1]])
                G(lambda e: e.tensor_tensor(out=w3(Ta, bi), in0=w3(Pa, bi), in1=ibc, op=ALU.add), [Wd[Pa][1], r_ident], [Wd[Ta][1]])
            product(lambda c, h: [fmop("Bf", c, h) + fmm("Af", c, h)], ev_N)
            product(lambda c, h: [fmop("Af", c, h) + fmm("Bf", c, h)], ev_mask(Qa, "sl"))
            product(lambda c, h: [fmop("Kf", c, h) + fmm("Af", c, h)], ev_mask(MAK, "su"))
            product(lambda c, h: [fmop("Bf", c, h) + fmm("Rf", c, h)], ev_mask(NBR, "iu"))
            product(lambda c, h: [fmop("Kf", c, h) + fmm("Rf", c, h)], ev_mask(NKR, "iu"))
            Pc, Pn, Qc, Qn, Tc, Tn = Pa, Pb_, Qa, Qb, Ta, Tb
            for kq in range(5):
                if kq < 4:
                    product(lambda c, h, Qc=Qc, Pc=Pc: [wop(Qc, c, h) + wop(Pc, c, h)], ev_copy(Pn, "vector"))
                product(lambda c, h, Qc=Qc, Pc=Pc: [wop(Pc, c, h) + wop(Qc, c, h)], ev_copy_plus_ident(Qn if kq < 4 else None, IQ))
                product(lambda c, h, Tc=Tc: [wop(IQ, c, h) + wop(Tc, c, h)], ev_copy(Tn, "scalar"))
                Pc, Pn, Qc, Qn, Tc, Tn = Pn, Pc, Qn, Qc, Tn, Tc
            TT = Tc
            Z, UV, ATT, RH, YV, GT, HH = Pa, Pb_, Qa, Qb, IQ, (Ta if TT == Tb else Tb), MAK
            product(lambda c, h: [wop(MAK, c, h) + tmop("Vt", c, h)], ev_copy(Z, "vector"))
            product(lambda c, h: [wop(TT, c, h) + wop(Z, c, h)], ev_copy(UV, "scalar"))
            product(lambda c, h: [wop(TT, c, h) + tmop("At", c, h)], ev_copy(ATT, "vector"))

            def icat_op(h):
                return icat[:, :], r_icat

            def dg_op(c, h):
                t_, r_ = Dgm[h // 2, h % 2]
                return t_[:, c, :], r_
            def ev_add(wi, wsrc):
                def f(bi, pa, r_pb):
                    V(lambda e: e.tensor_tensor(out=Wd[wi][0][0:64, bi * 512:(bi + 1) * 512], in0=pa,
                                                in1=Wd[wsrc][0][0:64, bi * 512:(bi + 1) * 512], op=ALU.add),
                      [r_pb, Wd[wsrc][1]], [Wd[wi][1]])
                return f
            product(lambda c, h: [icat_op(h) + fmm("Rf", c, h)], ev_copy(RH, "scalar"))
            product(lambda c, h: [wop(ATT, c, h) + wop(NBR, c, h)], ev_add(RH, RH))
            product(lambda c, h: [wop(NBR, c, h) + wop(UV, c, h), wop(NKR, c, h) + tmop("Vt", c, h)], ev_copy(YV, "vector"))
            product(lambda c, h: [icat_op(h) + dg_op(c, h)], ev_copy(GT, "scalar"))
            product(lambda c, h: [wop(ATT, c, h) + tmop("Bht", c, h)], ev_add(GT, GT))
            product(lambda c, h: [tmop("Bht", c, h) + wop(UV, c, h), tmop("Kht", c, h) + tmop("Vt", c, h)], ev_copy(HH, "vector"))
            if stage < 5:
                continue
            corder = list(range(NC)) if d == 0 else list(range(NC - 1, -1, -1))
            for ci_, c in enumerate(corder):
                pb, r_pb = nextbank()
                for h in range(4):
                    ga, rg = wop(GT, c, h)
                    sc = (h % 2) * 128 + (h // 2) * 64
                    M(lambda e, pb=pb, sc=sc, ga=ga, ci_=ci_: e.matmul(pb[0:64, sc:sc + 64], ga,
                                                                       Sst[:, ci_, sc:sc + 64], start=True, stop=True),
                      [rg, r_Sst], [r_pb])
                hh_ = Wd[HH][0][0:64, :].rearrange("p (b n) -> p b n", b=2)[:, :, c * 128:(c + 1) * 128]
                V(lambda e, pb=pb, ci_=ci_, hh_=hh_: e.tensor_tensor(out=Sst[0:64, ci_ + 1, :].rearrange("p (b n) -> p b n", b=2),
                                                                     in0=pb[0:64, 0:256].rearrange("p (b n) -> p b n", b=2),
                                                                     in1=hh_, op=ALU.add),
                  [r_pb, Wd[HH][1]], [r_Sst])
            pby = [nextbank() for _ in range(2)]
            for ci_, c in enumerate(corder):
                pb, r_pb = pby[c // 2]
                for h in range(4):
                    ra_, rr_ = wop(RH, c, h)
                    sc = (h % 2) * 128 + (h // 2) * 64
                    M(lambda e, pb=pb, c=c, sc=sc, ra_=ra_, ci_=ci_: e.matmul(
                        pb[0:64, (c % 2) * 256 + sc:(c % 2) * 256 + sc + 64], ra_, Sst[:, ci_, sc:sc + 64],
                        start=True, stop=True), [rr_, r_Sst], [r_pb])
            for c in range(NC):
                pb, r_pb = pby[c // 2]
                yv_ = Wd[YV][0][0:64, :].rearrange("p (b n) -> p b n", b=2)[:, :, c * 128:(c + 1) * 128].rearrange("p hp (hq n) -> p hp hq n", hq=2)
                V(lambda e, pb=pb, c=c, yv_=yv_: e.tensor_tensor(
                    out=yo[:, c, :].rearrange("p (hq hp n) -> p hp hq n", hq=2, hp=2),
                    in0=pb[0:64, (c % 2) * 256:(c % 2 + 1) * 256].rearrange("p (hp hq n) -> p hp hq n", hp=2, hq=2),
                    in1=yv_, op=ALU.add), [r_pb, Wd[YV][1]], [r_yo])
            V(lambda e: e.tensor_copy(out=Sst[0:64, 0, :], in_=Sst[0:64, NC, :]), [r_Sst], [r_Sst])
            ydst = io["yf"][s0:s0 + SEG, :].rearrange("(c p) n -> p c n", p=64)
            if d == 0:
                P.dma("sync", ydst, yo[:, :, :], reads=[r_yo], writes=[yf_res])
            else:
                if 0 in dirs:
                    P.dma("sync", yfb[:, :, :], ydst, reads=[yf_res], writes=[r_yfb])
                    V(lambda e: e.tensor_tensor(out=yo[:, :, :], in0=yo[:, :, :], in1=yfb[:, :, :], op=ALU.add), [r_yo, r_yfb], [r_yo])
                y4 = yo[:, :, :].rearrange("p c (h n) -> p (c h) n", n=64)
                NH_ = NC * 4
                mean = stat[:, 0:NH_]
                var = stat[:, NH_:2 * NH_]
                V(lambda e: e.tensor_reduce(out=mean, in_=y4, axis=AX.X, op=ALU.add), [r_yo], [r_stat])
                V(lambda e: e.tensor_scalar(out=mean, in0=mean, scalar1=1.0 / 64, scalar2=None, op0=ALU.mult), [r_stat], [r_stat])
                V(lambda e: e.tensor_tensor(out=y4, in0=y4, in1=mean.unsqueeze(2).broadcast_to([64, NH_, 64]), op=ALU.subtract),
                  [r_yo, r_stat], [r_yo])
                yq = yfb[:, :, :].rearrange("p c (h n) -> p (c h) n", n=64)
                V(lambda e: e.tensor_tensor(out=yq, in0=y4, in1=y4, op=ALU.mult), [r_yo], [r_yfb])
                V(lambda e: e.tensor_reduce(out=var, in_=yq, axis=AX.X, op=ALU.add), [r_yfb], [r_stat])
                A(lambda e: e.activation(out=var, in_=var, func=AF.Sqrt, scale=1.0 / 64, bias=LNX_EPS), [r_stat], [r_stat])
                V(lambda e: e.reciprocal(out=var, in_=var), [r_stat], [r_stat])
                V(lambda e: e.tensor_tensor(out=y4, in0=y4, in1=var.unsqueeze(2).broadcast_to([64, NH_, 64]), op=ALU.mult),
                  [r_yo, r_stat], [r_yo])
                V(lambda e: e.tensor_tensor(out=yo[:, :, :], in0=yo[:, :, :], in1=lng[:, :].unsqueeze(1).broadcast_to([64, NC, 256]),
                                            op=ALU.mult), [r_yo, r_lng], [r_yo])
                V(lambda e: e.tensor_tensor(out=yo[:, :, :], in0=yo[:, :, :], in1=lnb[:, :].unsqueeze(1).broadcast_to([64, NC, 256]),
                                            op=ALU.add), [r_yo, r_lnb], [r_yo])
                V(lambda e: e.tensor_tensor(out=yo[:, :, :], in0=yo[:, :, :], in1=XT["bont"][0][0:64, :, :], op=ALU.add),
                  [r_yo, XT["bont"][1]], [r_yo])
                V(lambda e: e.tensor_tensor(out=yo[:, :, :], in0=yo[:, :, :], in1=gat[:, :, :], op=ALU.mult), [r_yo, r_gat], [r_yo])
                P.dma("sync", io["ya"][s0:s0 + SEG, :].rearrange("(c p) n -> p c n", p=64), yo[:, :, :], reads=[r_yo])


def build_rwkv(**kw):
    nc = bass.Bass("TRN2", target_bir_lowering=False)
    di = lambda name, shape: nc.dram_tensor(name, shape, F32, kind="ExternalInput").ap()
    io = {
        "rkv": di("rkv", [3, 256, T + 2]), "wd": di("wd", [64, T + 2]), "ad": di("ad", [64, T + 2]), "gd": di("gd", [96, T + 2]),
        "ident": di("ident", [128, 128]), "icat": di("icat", [128, 64]), "bones": di("bones", [128, 128]),
        "msk": di("msk", [64, 4, 64]), "rmask": di("rmask", [128, SEG]), "prm": di("prm", [128, 32]), "mul": di("mul", [128, 3]),
        "w2s": di("w2s", [128, 512]), "a2s": di("a2s", [128, 512]), "g2s": di("g2s", [128, 256]), "hm": di("hm", [128, 2]),
        "lng": di("lng", [64, 256]), "lnb": di("lnb", [64, 256]),
        "yf": nc.dram_tensor("yf", [T, 256], F32, kind="ExternalOutput").ap(),
        "ya": nc.dram_tensor("ya", [T, 256], F32, kind="ExternalOutput").ap(),
    }
    with ExitStack() as st:
        P = Prog(nc, st)
        emit_rwkv(nc, P, st, io, **kw)
        P.finish()
        P.emit()
    return nc


def rwkv_consts():
    idx = np.arange(64)
    su = (idx[:, None] < idx[None, :]).astype(np.float32)
    iu = (idx[:, None] <= idx[None, :]).astype(np.float32)
    msk = np.stack([su, su.T, iu, iu.T], axis=1)
    ident = np.eye(128, dtype=np.float32)
    icat = np.concatenate([np.eye(64), np.eye(64)], axis=0).astype(np.float32)
    bones = np.kron(np.eye(2), np.ones((64, 64))).astype(np.float32)
    rmask = np.ones((128, SEG), np.float32)
    rmask[:, ::L] = 0.0
    return {"msk": np.ascontiguousarray(msk), "ident": ident, "icat": icat, "bones": bones, "rmask": rmask}


def lora_pad(w):
    o = np.zeros((128, 512), np.float32)
    for d in range(2):
        o[d * 32:(d + 1) * 32, d * 256:(d + 1) * 256] = w[d]
    return o


def rwkv_inputs(rw_colsT, hq, prm_in):
    ch = slice(hq * 256, (hq + 1) * 256)
    padT = lambda a: np.pad(a, ((0, 0), (1, 1)))
    rkv = np.stack([padT(rw_colsT[w * 512 + hq * 256: w * 512 + (hq + 1) * 256]) for w in range(3)])
    wd = padT(rw_colsT[1536:1600]); ad = padT(rw_colsT[1600:1664]); gd = padT(rw_colsT[1664:1760])
    mu = prm_in["rw_mu"]
    prm = np.zeros((128, 32), np.float32)
    for w in range(3):
        for pc in range(2):
            prm[:, w * 2 + pc] = mu[w * 512 + hq * 256 + pc * 128: w * 512 + hq * 256 + (pc + 1) * 128]
    for d in range(2):
        for pc in range(2):
            cs = slice(hq * 256 + pc * 128, hq * 256 + (pc + 1) * 128)
            prm[:, 6 + d * 2 + pc] = prm_in["rw_w0"][d, cs]
            prm[:, 10 + d * 2 + pc] = prm_in["rw_a0"][d, cs]
    for pc in range(2):
        cs = slice(hq * 256 + pc * 128, hq * 256 + (pc + 1) * 128)
        prm[:, 14 + pc] = prm_in["rw_k_k"][cs]
        prm[:, 16 + pc] = prm_in["rw_k_a"][cs]
        prm[:, 20 + pc] = prm_in["rw_r_k"].reshape(-1)[cs]
    mul = np.zeros((128, 3), np.float32)
    mul[0:64, 0] = mu[1536:1600]; mul[0:64, 1] = mu[1600:1664]; mul[0:96, 2] = mu[1664:1760]
    m = dict(rwkv_consts())
    m.update({
        "rkv": np.ascontiguousarray(rkv), "wd": np.ascontiguousarray(wd), "ad": np.ascontiguousarray(ad), "gd": np.ascontiguousarray(gd),
        "prm": prm, "mul": mul,
        "w2s": lora_pad(prm_in["rw_w2"][:, :, ch]), "a2s": lora_pad(prm_in["rw_a2"][:, :, ch]),
        "g2s": np.ascontiguousarray(np.pad(prm_in["rw_g2"][:, ch], ((0, 32), (0, 0)))),
        "hm": np.ascontiguousarray(np.stack([(np.arange(128) < 64), (np.arange(128) >= 64)], axis=1).astype(np.float32)),
        "lng": np.ascontiguousarray(np.broadcast_to(prm_in["rw_lnx_g"][ch][None, :], (64, 256))),
        "lnb": np.ascontiguousarray(np.broadcast_to(prm_in["rw_lnx_b"][ch][None, :], (64, 256))),
    })
    return m


NORM_EPS = 1e-6
D = 2048
KC = 16
NT = 2048
TI = 512
TW = TI + 2
HB = TW // 2
DFF = 5632
NF = DFF // 128
FG = 2
GELU_C = 1.5957691216057308


def emit_post(nc, P, st, io, final, ntiles=NT // TI, nfg=NF // FG, nseg=1):
    sb = lambda name, shape, dt=F32: st.enter_context(nc.sbuf_tensor("sb_" + name, shape, dt))
    ps = lambda name, shape, dt=F32: st.enter_context(nc.psum_tensor("ps_" + name, shape, dt))
    V = lambda fn, r, w: P.op("vector", fn, reads=r, writes=w)
    G = lambda fn, r, w: P.op("gpsimd", fn, reads=r, writes=w)
    A = lambda fn, r, w: P.op("scalar", fn, reads=r, writes=w)
    M = lambda fn, r, w: P.op("tensor", fn, reads=r, writes=w)

    def tile(name, shape, dt=F32):
        return sb(name, shape, dt), P.res(name)

    x, r_x = tile("x", [128, KC, TW])
    h, r_h = tile("h", [128, KC, TW], BF16)
    mg, r_mg = tile("mg", [128, KC, TW], BF16)
    ybf, r_ybf = tile("ybf", [128, 12, TW], BF16)
    stg = [tile("stg%d" % i, [128, 4096]) for i in range(2)]
    wbf = [tile("wbf%d" % i, [128, 4096], BF16) for i in range(4)]
    gt = [tile("gt%d" % i, [128, TW]) for i in range(3)]
    tmp = [tile("tmp%d" % i, [128, TW]) for i in range(4)]
    abf, r_abf = tile("abf", [128, FG, TI], BF16)
    rs, r_rs = tile("rs", [128, TW])
    sq, r_sq = tile("sq", [128, KC, TW])
    ones, r_ones = tile("ones", [128, 128])
    nrm, r_nrm = tile("nrm", [128, 4, KC])
    cw, r_cw = tile("cw", [128, NF, 4])
    pbank = [(ps("pb%d" % i, [128, 512]), P.res("pb%d" % i)) for i in range(8)]
    bk = {"i": 0}
    dq = {"i": 0}

    def bank2():
        i = bk["i"] % 4
        bk["i"] += 1
        return pbank[2 * i], pbank[2 * i + 1]

    def dmaq():
        dq["i"] += 1
        return "sync" if dq["i"] % 2 == 0 else "gpsimd"

    V(lambda e: e.memset(ones[:, :], 1.0), [], [r_ones])
    P.dma("sync", nrm[:, :, :], io["nrm"][:, :, :], writes=[r_nrm])
    P.dma("sync", cw[:, :, :], io["cw"][:, :, :], writes=[r_cw])
    cvi = {"i": 0}

    def load_w(dst_i, src_ap, rows, cols):
        s_i = cvi["i"] % 2
        cvi["i"] += 1
        stt, r_st = stg[s_i]
        wb, r_wb = wbf[dst_i]
        n = rows * cols
        P.dma(dmaq(), stt[:, 0:n].rearrange("p (r c) -> p r c", c=cols), src_ap, writes=[r_st])
        (G if s_i == 0 else V)(lambda e: e.tensor_copy(out=wb[:, 0:n], in_=stt[:, 0:n]), [r_st], [r_wb])
        return wb[:, 0:n].rearrange("p (r c) -> p r c", c=cols), r_wb

    def mm_blocks(lhs_list, rhs_fn, r_list, ntok_off=0, ntok=TW):
        (b0, r0), (b1, r1) = bank2()
        half = ntok // 2
        for blk, (pb, r_pb) in enumerate(((b0, r0), (b1, r1))):
            for k, la in enumerate(lhs_list):
                M(lambda e, pb=pb, la=la, k=k, blk=blk: e.matmul(pb[:, 0:half], la, rhs_fn(k, ntok_off + blk * half, half),
                                                                 start=(k == 0), stop=(k == len(lhs_list) - 1)),
                  r_list, [r_pb])
        return ((b0, r0), (b1, r1)), half

    def rmsnorm_to_h(gain_idx, off, ntok):
        A(lambda e: e.activation(out=sq[:, :, 0:ntok], in_=x[:, :, off:off + ntok], func=AF.Square), [r_x], [r_sq])
        half = ntok // 2
        (b0, r0), (b1, r1) = bank2()
        for blk, (pb, r_pb) in enumerate(((b0, r0), (b1, r1))):
            for k in range(KC):
                M(lambda e, pb=pb, k=k, blk=blk: e.matmul(pb[:, 0:half], ones[:, :], sq[:, k, blk * half:(blk + 1) * half],
                                                          start=(k == 0), stop=(k == KC - 1)), [r_ones, r_sq], [r_pb])
            A(lambda e, pb=pb, blk=blk: e.activation(out=rs[:, blk * half:(blk + 1) * half], in_=pb[:, 0:half], func=AF.Sqrt,
                                                     scale=1.0 / D, bias=NORM_EPS), [r_pb], [r_rs])
        V(lambda e: e.reciprocal(out=rs[:, 0:ntok], in_=rs[:, 0:ntok]), [r_rs], [r_rs])
        for k in range(KC):
            V(lambda e, k=k: e.scalar_tensor_tensor(out=h[:, k, off:off + ntok], in0=x[:, k, off:off + ntok],
                                                    scalar=nrm[:, gain_idx, k:k + 1], op0=ALU.mult, in1=rs[:, 0:ntok], op1=ALU.mult),
              [r_x, r_nrm, r_rs], [r_h])

    wbr = [(io["w_br_a"], 4, 0), (io["w_br_b"], 2, 4), (io["w_br_c"], 6, 6)]

    for tl_all in range(ntiles * nseg):
        sg, tl = tl_all // ntiles, tl_all % ntiles
        xv = io["xT"][sg].rearrange("(c p) t -> p c t", p=128)
        yv = io["yT"][sg].rearrange("(c p) t -> p c t", p=128)
        gv = io["gT"][sg].rearrange("(g c p) t -> g c p t", g=3, p=128)
        pv = io["pT"][sg].rearrange("(c p) t -> p c t", p=128)
        ov = io["out"][sg].rearrange("(c p) t -> p c t", p=128)
        c0 = tl * TI
        for q in range(4):
            P.dma(dmaq(), x[:, q * 4:(q + 1) * 4, :], xv[:, q * 4:(q + 1) * 4, c0:c0 + TW], writes=[r_x])
        for q in range(3):
            stt, r_st = stg[q % 2]
            P.dma(dmaq(), stt[:, 0:4 * TW].rearrange("p (r c) -> p r c", c=TW), yv[:, q * 4:(q + 1) * 4, c0:c0 + TW], writes=[r_st])
            V(lambda e, stt=stt, q=q: e.tensor_copy(out=ybf[:, q * 4:(q + 1) * 4, :],
                                                    in_=stt[:, 0:4 * TW].rearrange("p (r c) -> p r c", c=TW)), [r_st], [r_ybf])
        for jg in range(8):
            wts = []
            for bi, (wap, nk, yoff) in enumerate(wbr):
                wv_ = wap.rearrange("(c p) n -> p c n", p=128)
                wts.append(load_w(bi, wv_[:, :, jg * 256:(jg + 1) * 256], nk, 256))
            for jj in range(2):
                j = jg * 2 + jj
                for bi, (wap, nk, yoff) in enumerate(wbr):
                    g_t, r_g = gt[bi]
                    P.dma(dmaq(), g_t[:, :], gv[bi, j, :, c0:c0 + TW], writes=[r_g])
                    A(lambda e, g_t=g_t: e.activation(out=g_t[:, :], in_=g_t[:, :], func=AF.Sigmoid), [r_g], [r_g])
                for bi, (wap, nk, yoff) in enumerate(wbr):
                    w3_, r_w = wts[bi]
                    g_t, r_g = gt[bi]
                    banks, half = mm_blocks([w3_[:, k, jj * 128:(jj + 1) * 128] for k in range(nk)],
                                            lambda k, o, n, yoff=yoff: ybf[:, yoff + k, o:o + n], [r_w, r_ybf])
                    for blk, (pb, r_pb) in enumerate(banks):
                        sl = slice(blk * half, (blk + 1) * half)
                        if bi == 0:
                            V(lambda e, pb=pb, sl=sl, g_t=g_t: e.tensor_tensor(out=tmp[0][0][:, sl], in0=pb[:, 0:half], in1=g_t[:, sl], op=ALU.mult),
                              [r_pb, r_g], [tmp[0][1]])
                        else:
                            V(lambda e, pb=pb, sl=sl, g_t=g_t: e.tensor_tensor(out=tmp[1][0][:, sl], in0=pb[:, 0:half], in1=g_t[:, sl], op=ALU.mult),
                              [r_pb, r_g], [tmp[1][1]])
                            if bi == 1:
                                G(lambda e, sl=sl: e.tensor_tensor(out=tmp[0][0][:, sl], in0=tmp[0][0][:, sl], in1=tmp[1][0][:, sl], op=ALU.add),
                                  [tmp[0][1], tmp[1][1]], [tmp[0][1]])
                            else:
                                G(lambda e, sl=sl, j=j: e.tensor_tensor(out=mg[:, j, sl], in0=tmp[0][0][:, sl], in1=tmp[1][0][:, sl], op=ALU.add),
                                  [tmp[0][1], tmp[1][1]], [r_mg])
        wov = io["w_out"].rearrange("(c p) n -> p c n", p=128)
        for jg in range(8):
            w3_, r_w = load_w(3, wov[:, :, jg * 256:(jg + 1) * 256], KC, 256)
            for jj in range(2):
                j = jg * 2 + jj
                banks, half = mm_blocks([w3_[:, k, jj * 128:(jj + 1) * 128] for k in range(KC)],
                                        lambda k, o, n: mg[:, k, o:o + n], [r_w, r_mg])
                for blk, (pb, r_pb) in enumerate(banks):
                    sl = slice(blk * half, (blk + 1) * half)
                    V(lambda e, pb=pb, sl=sl, j=j: e.tensor_tensor(out=x[:, j, sl], in0=x[:, j, sl], in1=pb[:, 0:half], op=ALU.add),
                      [r_pb, r_x], [r_x])
        rmsnorm_to_h(0, 0, TW)
        wgv = io["w_gate"].rearrange("(c p) n -> p c n", p=128)
        wuv = io["w_up"].rearrange("(c p) n -> p c n", p=128)
        wdv = io["w_down"].rearrange("(c p) n -> p c n", p=128)
        for fg in range(nfg):
            f0 = fg * FG
            wg3, r_wg = load_w(0, wgv[:, :, f0 * 128:(f0 + FG) * 128], KC, FG * 128)
            wu3, r_wu = load_w(1, wuv[:, :, f0 * 128:(f0 + FG) * 128], KC, FG * 128)
            for ff in range(FG):
                f = f0 + ff
                gb_, half = mm_blocks([wg3[:, k, ff * 128:(ff + 1) * 128] for k in range(KC)],
                                      lambda k, o, n: h[:, k, o:o + n], [r_wg, r_h])
                g_s, r_gs = tmp[0]
                for blk, (pb, r_pb) in enumerate(gb_):
                    A(lambda e, pb=pb, blk=blk: e.activation(out=g_s[:, blk * half:(blk + 1) * half], in_=pb[:, 0:half], func=AF.Copy),
                      [r_pb], [r_gs])
                ub_, half = mm_blocks([wu3[:, k, ff * 128:(ff + 1) * 128] for k in range(KC)],
                                      lambda k, o, n: h[:, k, o:o + n], [r_wu, r_h], ntok_off=1, ntok=TI)
                c_s, r_cs = tmp[1]
                t_s, r_ts = tmp[2]
                V(lambda e, f=f: e.tensor_scalar(out=c_s[:, 0:TI], in0=g_s[:, 1:TI + 1], scalar1=cw[:, f, 1:2], scalar2=cw[:, f, 3:4],
                                                 op0=ALU.mult, op1=ALU.add), [r_gs, r_cw], [r_cs])
                V(lambda e, f=f: e.scalar_tensor_tensor(out=c_s[:, 0:TI], in0=g_s[:, 0:TI], scalar=cw[:, f, 0:1], op0=ALU.mult,
                                                        in1=c_s[:, 0:TI], op1=ALU.add), [r_gs, r_cw, r_cs], [r_cs])
                V(lambda e, f=f: e.scalar_tensor_tensor(out=c_s[:, 0:TI], in0=g_s[:, 2:TI + 2], scalar=cw[:, f, 2:3], op0=ALU.mult,
                                                        in1=c_s[:, 0:TI], op1=ALU.add), [r_gs, r_cw, r_cs], [r_cs])
                A(lambda e: e.activation(out=t_s[:, 0:TI], in_=c_s[:, 0:TI], func=AF.Square), [r_cs], [r_ts])
                G(lambda e: e.tensor_scalar(out=t_s[:, 0:TI], in0=t_s[:, 0:TI], scalar1=0.044715, scalar2=1.0, op0=ALU.mult, op1=ALU.add),
                  [r_ts], [r_ts])
                G(lambda e: e.tensor_tensor(out=t_s[:, 0:TI], in0=t_s[:, 0:TI], in1=c_s[:, 0:TI], op=ALU.mult), [r_ts, r_cs], [r_ts])
                A(lambda e: e.activation(out=t_s[:, 0:TI], in_=t_s[:, 0:TI], func=AF.Sigmoid, scale=GELU_C), [r_ts], [r_ts])
                G(lambda e: e.tensor_tensor(out=t_s[:, 0:TI], in0=t_s[:, 0:TI], in1=c_s[:, 0:TI], op=ALU.mult), [r_ts, r_cs], [r_ts])
                for blk, (pb, r_pb) in enumerate(ub_):
                    V(lambda e, pb=pb, blk=blk, ff=ff: e.tensor_tensor(out=abf[:, ff, blk * half:(blk + 1) * half],
                                                                      in0=pb[:, 0:half], in1=t_s[:, blk * half:(blk + 1) * half], op=ALU.mult),
                      [r_pb, r_ts], [r_abf])
            for hc in range(2):
                wd3, r_wd = load_w(2, wdv[:, f0:f0 + FG, hc * 1024:(hc + 1) * 1024], FG, 1024)
                for jj in range(8):
                    j = hc * 8 + jj
                    banks, half = mm_blocks([wd3[:, k, jj * 128:(jj + 1) * 128] for k in range(FG)],
                                            lambda k, o, n: abf[:, k, o:o + n], [r_wd, r_abf], ntok_off=0, ntok=TI)
                    for blk, (pb, r_pb) in enumerate(banks):
                        sl = slice(1 + blk * half, 1 + (blk + 1) * half)
                        V(lambda e, pb=pb, sl=sl, j=j: e.tensor_tensor(out=x[:, j, sl], in0=x[:, j, sl], in1=pb[:, 0:half], op=ALU.add),
                          [r_pb, r_x], [r_x])
        rmsnorm_to_h(1, 1, TI)
        stt, r_st = stg[0]
        P.dma(dmaq(), stt[:, 0:2 * TI].rearrange("p (r c) -> p r c", c=TI), pv[:, :, tl * TI:(tl + 1) * TI], writes=[r_st])
        V(lambda e: e.tensor_copy(out=ybf[:, 0:2, 0:TI], in_=stt[:, 0:2 * TI].rearrange("p (r c) -> p r c", c=TI)), [r_st], [r_ybf])
        wpv = io["w_pg"].rearrange("(c p) n -> p c n", p=128)
        wlv = io["w_ple"].rearrange("(c p) n -> p c n", p=128)
        for jg in range(8):
            wp3, r_wp = load_w(0, wpv[:, :, jg * 256:(jg + 1) * 256], KC, 256)
            wl3, r_wl = load_w(1, wlv[:, :, jg * 256:(jg + 1) * 256], 2, 256)
            for jj in range(2):
                j = jg * 2 + jj
                gb_, half = mm_blocks([wp3[:, k, jj * 128:(jj + 1) * 128] for k in range(KC)],
                                      lambda k, o, n: h[:, k, o:o + n], [r_wp, r_h], ntok_off=1, ntok=TI)
                g_s, r_gs = tmp[0]
                for blk, (pb, r_pb) in enumerate(gb_):
                    A(lambda e, pb=pb, blk=blk: e.activation(out=g_s[:, blk * half:(blk + 1) * half], in_=pb[:, 0:half], func=AF.Sigmoid),
                      [r_pb], [r_gs])
                pb_, half = mm_blocks([wl3[:, k, jj * 128:(jj + 1) * 128] for k in range(2)],
                                      lambda k, o, n: ybf[:, k, o:o + n], [r_wl, r_ybf], ntok_off=0, ntok=TI)
                for blk, (pb, r_pb) in enumerate(pb_):
                    V(lambda e, pb=pb, blk=blk: e.tensor_tensor(out=g_s[:, blk * half:(blk + 1) * half], in0=pb[:, 0:half],
                                                                in1=g_s[:, blk * half:(blk + 1) * half], op=ALU.mult), [r_pb, r_gs], [r_gs])
                G(lambda e, j=j: e.tensor_tensor(out=x[:, j, 1:TI + 1], in0=x[:, j, 1:TI + 1], in1=g_s[:, 0:TI], op=ALU.add),
                  [r_x, r_gs], [r_x])
        if final:
            A(lambda e: e.activation(out=sq[:, :, 0:TI], in_=x[:, :, 1:TI + 1], func=AF.Square), [r_x], [r_sq])
            half = TI // 2
            (b0, r0), (b1, r1) = bank2()
            for blk, (pb, r_pb) in enumerate(((b0, r0), (b1, r1))):
                for k in range(KC):
                    M(lambda e, pb=pb, k=k, blk=blk: e.matmul(pb[:, 0:half], ones[:, :], sq[:, k, blk * half:(blk + 1) * half],
                                                              start=(k == 0), stop=(k == KC - 1)), [r_ones, r_sq], [r_pb])
                A(lambda e, pb=pb, blk=blk: e.activation(out=rs[:, blk * half:(blk + 1) * half], in_=pb[:, 0:half], func=AF.Sqrt,
                                                         scale=1.0 / D, bias=NORM_EPS), [r_pb], [r_rs])
            V(lambda e: e.reciprocal(out=rs[:, 0:TI], in_=rs[:, 0:TI]), [r_rs], [r_rs])
            for k in range(KC):
                V(lambda e, k=k: e.scalar_tensor_tensor(out=sq[:, k, 0:TI], in0=x[:, k, 1:TI + 1], scalar=nrm[:, 2, k:k + 1],
                                                        op0=ALU.mult, in1=rs[:, 0:TI], op1=ALU.mult), [r_x, r_nrm, r_rs], [r_sq])
            for q in range(4):
                P.dma(dmaq(), ov[:, q * 4:(q + 1) * 4, tl * TI:(tl + 1) * TI], sq[:, q * 4:(q + 1) * 4, 0:TI], reads=[r_sq])
        else:
            for q in range(4):
                P.dma(dmaq(), ov[:, q * 4:(q + 1) * 4, tl * TI:(tl + 1) * TI], x[:, q * 4:(q + 1) * 4, 1:TI + 1], reads=[r_x])


def build_post(final, nseg=1, **kw):
    nc = bass.Bass("TRN2", target_bir_lowering=False)
    di = lambda name, shape: nc.dram_tensor(name, shape, F32, kind="ExternalInput").ap()
    io = {
        "xT": di("xT", [nseg, D, NT + 2]), "yT": di("yT", [nseg, 1536, NT + 2]), "gT": di("gT", [nseg, 3 * D, NT + 2]), "pT": di("pT", [nseg, 256, NT]),
        "nrm": di("nrm", [128, 4, KC]), "cw": di("cw", [128, NF, 4]),
        "w_br_a": di("w_br_a", [512, D]), "w_br_b": di("w_br_b", [256, D]), "w_br_c": di("w_br_c", [768, D]),
        "w_out": di("w_out", [D, D]), "w_gate": di("w_gate", [D, DFF]), "w_up": di("w_up", [D, DFF]), "w_down": di("w_down", [DFF, D]),
        "w_pg": di("w_pg", [D, D]), "w_ple": di("w_ple", [256, D]),
        "out": nc.dram_tensor("out", [nseg, D, NT], F32, kind="ExternalOutput").ap(),
    }
    with ExitStack() as st:
        P = Prog(nc, st)
        emit_post(nc, P, st, io, final, nseg=nseg, **kw)
        P.finish()
        P.emit()
    return nc


def halo(a, t0, n, axis=0):
    T_ = a.shape[axis]
    lo, hi = t0 - 1, t0 + n + 1
    sl = [slice(None)] * a.ndim
    sl[axis] = slice(max(lo, 0), min(hi, T_))
    pw = [(0, 0)] * a.ndim
    pw[axis] = (max(0, -lo), max(0, hi - T_))
    return np.pad(a[tuple(sl)], pw)


def post_inputs(x_b, ya_b, yb_b, yc_b, gates_b, p_b, half, W, li, final_g):
    t0 = half * NT
    yall = np.concatenate([ya_b, yb_b, yc_b], axis=1)
    nrm = np.zeros((128, 4, KC), np.float32)
    nrm[:, 0, :] = W["norm_ffn"][li].reshape(KC, 128).T
    nrm[:, 1, :] = W["norm_ple"][li].reshape(KC, 128).T
    nrm[:, 2, :] = final_g.reshape(KC, 128).T
    cw = np.zeros((128, NF, 4), np.float32)
    for i in range(3):
        cw[:, :, i] = W["ffn_conv_w"][li][i].reshape(NF, 128).T
    cw[:, :, 3] = W["ffn_conv_b"][li].reshape(NF, 128).T
    return {
        "xT": np.ascontiguousarray(halo(x_b, t0, NT).T), "yT": np.ascontiguousarray(halo(yall, t0, NT).T),
        "gT": np.ascontiguousarray(halo(gates_b, t0, NT).T), "pT": np.ascontiguousarray(p_b[t0:t0 + NT].T),
        "nrm": nrm, "cw": cw,
        "w_br_a": W["w_br_a"][li], "w_br_b": W["w_br_b"][li], "w_br_c": W["w_br_c"][li], "w_out": W["w_out"][li],
        "w_gate": W["w_ffn_gate"][li], "w_up": W["w_ffn_up"][li], "w_down": W["w_ffn_down"][li],
        "w_pg": W["w_ple_gate"][li], "w_ple": W["w_ple"][li],
    }


NDENSE = 8
OFF = {"rw": 0, "dq": 1760, "dk": 2528, "dv": 3296, "nq": 4064, "nk": 4832, "nv": 5600, "g": 6368}


def attn_inputs_fm(zb, na_bias, half):
    C, S = rope_consts()
    T_ = 4096
    t0 = half * NTQ
    rows = lambda k: zb[OFF[k]:OFF[k] + 768].reshape(12, 64, T_)

    def vaug(k):
        vh = rows(k).transpose(0, 2, 1)
        va = np.concatenate([vh, np.ones((12, T_, 1), np.float32)], axis=2)
        vw = window(va, half, 1)
        return np.ascontiguousarray(vw.reshape(12, 32, 128, 65).transpose(0, 2, 1, 3))
    return {
        "cq": np.ascontiguousarray(C[:, t0:t0 + NTQ]), "sq": np.ascontiguousarray(S[:, t0:t0 + NTQ]),
        "ck": np.ascontiguousarray(window(C, half, 1)), "sk": np.ascontiguousarray(window(S, half, 1)),
        "md": dil_masks(),
        "dq": np.ascontiguousarray(rows("dq")[:, :, t0:t0 + NTQ]), "dk": np.ascontiguousarray(window(rows("dk"), half, 2)),
        "dv": vaug("dv"),
        "nq": np.ascontiguousarray(rows("nq")[:, :, t0:t0 + NTQ]), "nk": np.ascontiguousarray(window(rows("nk"), half, 2)),
        "nv": vaug("nv"),
        "ne": np.ascontiguousarray(np.stack([na_tables(na_bias[h], na_slot_m(half)) for h in range(12)])),
    }


def post_inputs_fm(xfm_b, yT_b, gT_b, pT_b, half, W, li):
    t0 = half * NT
    nrm = np.zeros((128, 4, KC), np.float32)
    nrm[:, 0, :] = W["norm_ffn"][li].reshape(KC, 128).T
    nrm[:, 1, :] = W["norm_ple"][li].reshape(KC, 128).T
    nrm[:, 2, :] = W["norm_final"].reshape(KC, 128).T
    cw = np.zeros((128, NF, 4), np.float32)
    for i in range(3):
        cw[:, :, i] = W["ffn_conv_w"][li][i].reshape(NF, 128).T
    cw[:, :, 3] = W["ffn_conv_b"][li].reshape(NF, 128).T
    return {
        "xT": halo(xfm_b, t0, NT, axis=1), "yT": halo(yT_b, t0, NT, axis=1), "gT": halo(gT_b, t0, NT, axis=1),
        "pT": pT_b[:, t0:t0 + NT], "nrm": nrm, "cw": cw,
    }


def kernel(**inp):
    inp = {k: np.asarray(v) for k, v in inp.items()}
    x, p = inp["x"], inp["p"]
    W = inp
    B_, T_ = 4, 4096
    segs = [(b, hf) for b in range(B_) for hf in range(2)]
    nseg = 8 // NDENSE
    xfm = [np.ascontiguousarray(x[b].T) for b in range(B_)]
    ncA = build_inproj(2048, 2048, 12512, nseg=nseg)
    ncB1 = build_attn()
    ncB2 = build_rwkv()
    out = np.empty((B_, T_, 2048), np.float32)
    for li in range(2):
        gl = np.ascontiguousarray(W["norm_mix"][li].reshape(16, 128).T)
        in_maps = []
        for c in range(NDENSE):
            xs = np.stack([xfm[b][:, hf * 2048:(hf + 1) * 2048] for (b, hf) in segs[c * nseg:(c + 1) * nseg]])
            in_maps.append({"xT": np.ascontiguousarray(xs), "w": np.ascontiguousarray(W["w_in"][li]), "g": gl})
        res = run_bass_kernel_spmd(ncA, in_maps, core_ids=list(range(NDENSE)))
        zfull = [np.empty((12512, T_), np.float32) for _ in range(B_)]
        for c in range(NDENSE):
            for j, (b, hf) in enumerate(segs[c * nseg:(c + 1) * nseg]):
                zfull[b][:, hf * 2048:(hf + 1) * 2048] = res.results[c]["zT"][j]
        del res, in_maps
        in_maps = [attn_inputs_fm(zfull[c // 2], W["na_bias"][li], c % 2) for c in range(8)]
        resa = run_bass_kernel_spmd(ncB1, in_maps, core_ids=list(range(8))).results
        del in_maps
        prm_in = {k: W[k][li] for k in W if k.startswith("rw_")}
        in_maps = [rwkv_inputs(zfull[c // 2][0:1760], c % 2, prm_in) for c in range(8)]
        resr = run_bass_kernel_spmd(ncB2, in_maps, core_ids=list(range(8))).results
        del in_maps
        final = (li == 1)
        ncC = build_post(final, nseg=nseg)
        wmap = {"w_br_a": W["w_br_a"][li], "w_br_b": W["w_br_b"][li], "w_br_c": W["w_br_c"][li], "w_out": W["w_out"][li],
                "w_gate": W["w_ffn_gate"][li], "w_up": W["w_ffn_up"][li], "w_down": W["w_ffn_down"][li],
                "w_pg": W["w_ple_gate"][li], "w_ple": W["w_ple"][li]}
        wmap = {k: np.ascontiguousarray(v) for k, v in wmap.items()}
        yT = []
        for b in range(B_):
            ya = np.concatenate([resr[2 * b]["ya"], resr[2 * b + 1]["ya"]], axis=1)
            yb = np.concatenate([resa[2 * b]["yb"], resa[2 * b + 1]["yb"]], axis=0)
            yc = np.concatenate([resa[2 * b]["yc"], resa[2 * b + 1]["yc"]], axis=0)
            yT.append(np.ascontiguousarray(np.concatenate([ya, yb, yc], axis=1).T))
        in_maps = []
        for c in range(NDENSE):
            per = [post_inputs_fm(xfm[b], yT[b], zfull[b][OFF["g"]:], np.ascontiguousarray(p[li, b].T), hf, W, li)
                   for (b, hf) in segs[c * nseg:(c + 1) * nseg]]
            m = {k: np.ascontiguousarray(np.stack([q[k] for q in per])) for k in ("xT", "yT", "gT", "pT")}
            m["nrm"] = per[0]["nrm"]
            m["cw"] = per[0]["cw"]
            m.update(wmap)
            in_maps.append(m)
        res = run_bass_kernel_spmd(ncC, in_maps, core_ids=list(range(NDENSE)))
        for c in range(NDENSE):
            for j, (b, hf) in enumerate(segs[c * nseg:(c + 1) * nseg]):
                o = res.results[c]["out"][j]
                if final:
                    out[b, hf * 2048:(hf + 1) * 2048, :] = o.T
                else:
                    xfm[b][:, hf * 2048:(hf + 1) * 2048] = o
        del res, in_maps, zfull
    return out
```

```python
import os, sys, time
from contextlib import ExitStack
import numpy as np
import concourse.bass as bass
import concourse.mybir as mybir
from concourse.bass_utils import run_bass_kernel_spmd

F32 = mybir.dt.float32
BF16 = mybir.dt.bfloat16
AF = mybir.ActivationFunctionType
ALU = mybir.AluOpType
AX = mybir.AxisListType

EPOCH = 30000


class Res:
    __slots__ = ("name", "lw", "rd", "dsem", "dval")

    def __init__(self, name):
        self.name = name
        self.lw = None
        self.rd = {}
        self.dsem = None
        self.dval = 0


class _Rec:
    def __init__(self):
        self.call = None

    def __getattr__(self, name):
        def f(*a, **k):
            self.call = (name, a, k)
            return self
        return f


class Prog:
    ENG = ("tensor", "vector", "scalar", "gpsimd", "sync")

    def __init__(self, nc, stack):
        self.nc = nc
        self.stack = stack
        self.streams = {e: [] for e in self.ENG}
        self.cur = {}
        self.sems = {}
        self.seen = {e: {} for e in self.ENG}
        self.nsem = 0
        self.dma_final = {}
        for e in self.ENG:
            self._new_epoch(e)

    def _mksem(self, name):
        h = self.stack.enter_context(self.nc.semaphore(name))
        key = name
        self.sems[key] = h
        self.nsem += 1
        return key

    def _new_epoch(self, e):
        k = self._mksem("s_%s_%d" % (e, self.nsem))
        self.cur[e] = [k, 0]

    def res(self, name):
        return Res(name)

    def _deps(self, eng, reads, writes, skip_key=None):
        need = {}
        def add(dep):
            if dep is None:
                return
            k, v = dep
            if k == skip_key:
                return
            if need.get(k, 0) < v:
                need[k] = v
        for r in reads:
            if r.lw is not None and need.get(r.lw[0], 0) < r.lw[1]:
                need[r.lw[0]] = r.lw[1]
        for w in writes:
            add(w.lw)
            for k, v in w.rd.items():
                if need.get(k, 0) < v:
                    need[k] = v
        out = []
        seen = self.seen[eng]
        mykey = self.cur[eng][0]
        for k, v in need.items():
            if eng == "tensor" and k == mykey:
                continue
            if seen.get(k, 0) >= v:
                continue
            seen[k] = v
            out.append((k, v))
        return out

    def op(self, eng, fn, reads=(), writes=()):
        if self.cur[eng][1] >= EPOCH:
            self._new_epoch(eng)
        waits = self._deps(eng, reads, writes)
        cur = self.cur[eng]
        cur[1] += 1
        key, val = cur[0], cur[1]
        rec = _Rec()
        fn(rec)
        name, a, k = rec.call
        self.streams[eng].append((waits, (lambda e, name=name, a=a, k=k: getattr(e, name)(*a, **k)), key, 1))
        for w in writes:
            w.lw = (key, val)
            w.rd = {}
        for r in reads:
            if r not in writes:
                r.rd[key] = val

    def dma(self, eng, out, in_, reads=(), writes=(), owner=None):
        if owner is None:
            owner = writes[0] if writes else reads[0]
        if owner.dsem is None:
            owner.dsem = self._mksem("d_%s_%d" % (owner.name, self.nsem))
        key = owner.dsem
        waits = self._deps(eng, reads, writes, skip_key=key)
        owner.dval += 16
        val = owner.dval
        self.dma_final[key] = val

        def fn(e, out=out, in_=in_):
            return e.dma_start(out=out, in_=in_)
        self.streams[eng].append((waits, fn, key, 16))
        for w in writes:
            w.lw = (key, val)
            w.rd = {}
        for r in reads:
            if r not in writes:
                r.rd[key] = val

    def finish(self):
        waits = []
        for k, v in self.dma_final.items():
            if self.seen["sync"].get(k, 0) < v:
                waits.append((k, v))
        self.streams["sync"].append((waits, None, None, 0))

    def emit(self):
        nc = self.nc
        sems = self.sems
        streams = self.streams
        with nc.Block() as block:
            def mk(ename):
                def body(e):
                    for waits, fn, key, inc in streams[ename]:
                        for k, v in waits:
                            e.wait_ge(sems[k], v)
                        if fn is not None:
                            fn(e).then_inc(sems[key], inc)
                return body
            block.tensor(mk("tensor"))
            block.vector(mk("vector"))
            block.scalar(mk("scalar"))
            block.gpsimd(mk("gpsimd"))
            block.sync(mk("sync"))


NORM_EPS = 1e-6


def build_inproj(D, NT, NCOLS, CG=256, TB=256, nseg=1):
    KC = D // 128
    nc = bass.Bass("TRN2", target_bir_lowering=False)
    xT_all = nc.dram_tensor("xT", [nseg, D, NT], F32, kind="ExternalInput").ap()
    w = nc.dram_tensor("w", [D, NCOLS], F32, kind="ExternalInput").ap()
    g = nc.dram_tensor("g", [128, KC], F32, kind="ExternalInput").ap()
    zT_all = nc.dram_tensor("zT", [nseg, NCOLS, NT], F32, kind="ExternalOutput").ap()
    with ExitStack() as st:
        P = Prog(nc, st)
        sb = lambda name, shape, dt: st.enter_context(nc.sbuf_tensor("sb_" + name, shape, dt))
        ps = lambda name, shape, dt: st.enter_context(nc.psum_tensor("ps_" + name, shape, dt))
        hT = sb("hT", [128, KC, NT], BF16); r_hT = P.res("hT")
        gs = sb("gs", [128, KC], F32); r_gs = P.res("gs")
        ones = sb("ones", [128, 128], F32); r_ones = P.res("ones")
        xb = [sb("xb%d" % i, [128, KC, TB], F32) for i in range(2)]
        r_xb = [P.res("xb%d" % i) for i in range(2)]
        sq = [sb("sq%d" % i, [128, KC, TB], F32) for i in range(2)]
        r_sq = [P.res("sq%d" % i) for i in range(2)]
        rs = [sb("rs%d" % i, [128, TB], F32) for i in range(2)]
        r_rs = [P.res("rs%d" % i) for i in range(2)]
        wf = [sb("wf%d" % i, [128, KC, CG], F32) for i in range(2)]
        r_wf = [P.res("wf%d" % i) for i in range(2)]
        wb = [sb("wb%d" % i, [128, KC, CG], BF16) for i in range(2)]
        r_wb = [P.res("wb%d" % i) for i in range(2)]
        ob = [sb("ob%d" % i, [128, NT], F32) for i in range(2)]
        r_ob = [P.res("ob%d" % i) for i in range(2)]
        pn = ps("pn", [128, 512], F32); r_pn = P.res("pn")
        pm = [ps("pm%d" % i, [128, 512], F32) for i in range(4)]
        r_pm = [P.res("pm%d" % i) for i in range(4)]

        P.dma("sync", gs[:, :], g[:, :], writes=[r_gs])
        P.op("vector", lambda e: e.memset(ones[:, :], 1.0), writes=[r_ones])
        wv = w.rearrange("(c p) n -> p c n", p=128)
        oc = 0
        pmi = 0
        for sg in range(nseg):
            xTv = xT_all[sg].rearrange("(c p) t -> p c t", p=128)
            zT = zT_all[sg]
            for tb in range(NT // TB):
                i = tb % 2
                P.dma("sync" if i == 0 else "gpsimd", xb[i][:, :, :], xTv[:, :, tb * TB:(tb + 1) * TB], writes=[r_xb[i]])
                P.op("scalar", lambda e, i=i: e.activation(out=sq[i][:, :, :], in_=xb[i][:, :, :], func=AF.Square),
                     reads=[r_xb[i]], writes=[r_sq[i]])
                for kc in range(KC):
                    P.op("tensor", lambda e, i=i, kc=kc: e.matmul(pn[:, 0:TB], ones[:, :], sq[i][:, kc, :],
                                                                   start=(kc == 0), stop=(kc == KC - 1)),
                         reads=[r_ones, r_sq[i]], writes=[r_pn])
                P.op("scalar", lambda e, i=i: e.activation(out=rs[i][:, :], in_=pn[:, 0:TB], func=AF.Sqrt,
                                                           scale=1.0 / D, bias=NORM_EPS),
                     reads=[r_pn], writes=[r_rs[i]])
                P.op("vector", lambda e, i=i: e.reciprocal(out=rs[i][:, :], in_=rs[i][:, :]),
                     reads=[r_rs[i]], writes=[r_rs[i]])
                for kc in range(KC):
                    P.op("vector", lambda e, i=i, kc=kc, tb=tb: e.scalar_tensor_tensor(
                        out=hT[:, kc, tb * TB:(tb + 1) * TB], in0=xb[i][:, kc, :], scalar=gs[:, kc:kc + 1],
                        op0=ALU.mult, in1=rs[i][:, :], op1=ALU.mult),
                        reads=[r_xb[i], r_gs, r_rs[i]], writes=[r_hT])
            ncg = (NCOLS + CG - 1) // CG
            for cg in range(ncg):
                i = cg % 2
                c0 = cg * CG
                cw = min(CG, NCOLS - c0)
                P.dma("sync" if i == 0 else "gpsimd", wf[i][:, :, 0:cw], wv[:, :, c0:c0 + cw], writes=[r_wf[i]])
                P.op("gpsimd" if i == 0 else "vector",
                     lambda e, i=i, cw=cw: e.tensor_copy(out=wb[i][:, :, 0:cw], in_=wf[i][:, :, 0:cw]),
                     reads=[r_wf[i]], writes=[r_wb[i]])
                for s0 in range(0, cw, 128):
                    sw = min(128, cw - s0)
                    o = oc % 2
                    oc += 1
                    for t4 in range(NT // 512):
                        b = pmi % 4
                        pmi += 1
                        for kc in range(KC):
                            P.op("tensor", lambda e, i=i, kc=kc, s0=s0, sw=sw, t4=t4, b=b: e.matmul(
                                pm[b][0:sw, :], wb[i][:, kc, s0:s0 + sw], hT[:, kc, t4 * 512:(t4 + 1) * 512],
                                start=(kc == 0), stop=(kc == KC - 1)),
                                reads=[r_wb[i], r_hT], writes=[r_pm[b]])
                        if t4 % 2 == 0:
                            P.op("scalar", lambda e, o=o, sw=sw, t4=t4, b=b: e.activation(
                                out=ob[o][0:sw, t4 * 512:(t4 + 1) * 512], in_=pm[b][0:sw, :], func=AF.Copy),
                                reads=[r_pm[b]], writes=[r_ob[o]])
                        else:
                            P.op("vector", lambda e, o=o, sw=sw, t4=t4, b=b: e.tensor_copy(
                                out=ob[o][0:sw, t4 * 512:(t4 + 1) * 512], in_=pm[b][0:sw, :]),
                                reads=[r_pm[b]], writes=[r_ob[o]])
                    P.dma("sync", zT[c0 + s0:c0 + s0 + sw, :], ob[o][0:sw, :], reads=[r_ob[o]])
        P.finish()
        P.emit()
    return nc


SCALE = 64 ** -0.5
NTQ = 2048
WIN = 4096
NQB = NTQ // 128
KOFF = 8
DIL = ((128, 1), (512, 4), (2048, 16))
NEGFILL = -30000.0


def dil_deltas():
    out = []
    for g, (window, dil) in enumerate(DIL):
        r = (64 * dil) // 128 if dil > 1 else 1
        out.append(list(range(-r, r + 1)))
    return out


def dil_masks():
    ms = []
    kk = np.arange(128)[:, None]
    qq = np.arange(128)[None, :]
    for g, (window, dil) in enumerate(DIL):
        for d in dil_deltas()[g]:
            diff = (kk + 128 * d) - qq
            ok = (diff % dil == 0) & (np.abs(diff) <= 64 * dil)
            ms.append(ok.astype(np.float32))
    return np.ascontiguousarray(np.stack(ms, axis=1))


def na_tables(na_bias_h, m_list):
    out = np.full((128, len(m_list) * 7, 128), NEGFILL, np.float32)
    for mi, m in enumerate(m_list):
        q = m * 128 + np.arange(128)
        qr, qc = q // 64, q % 64
        r0 = np.clip(qr - 4, 0, 56)
        wc0 = np.clip(qc - 8, 0, 48)
        for di, d in enumerate(range(-3, 4)):
            kt = m + d
            if kt < 0 or kt >= 32:
                continue
            k = kt * 128 + np.arange(128)
            kr, kc = k // 64, k % 64
            ok = ((kr[:, None] >= r0[None, :]) & (kr[:, None] < r0[None, :] + 8)
                  & (kc[:, None] >= wc0[None, :]) & (kc[:, None] < wc0[None, :] + 16))
            dy = np.clip(kr[:, None] - qr[None, :], -7, 7) + 7
            dx = np.clip(kc[:, None] - qc[None, :], -15, 15) + 15
            vals = na_bias_h[dy, dx]
            out[:, mi * 7 + di, :] = np.where(ok, vals, NEGFILL)
    return out


def na_slot_of_qb(qb):
    if qb < 2:
        return qb
    if qb >= NQB - 2:
        return 3 + (qb - (NQB - 2))
    return 2


def na_slot_m(half):
    base = half * NQB
    return [base + 0, base + 1, base + 2, base + NQB - 2, base + NQB - 1]


def emit_attn(nc, P, st, io, n_hg=4, n_na=12, n_qb=NQB):
    sb = lambda name, shape, dt: st.enter_context(nc.sbuf_tensor("sb_" + name, shape, dt))
    ps = lambda name, shape, dt: st.enter_context(nc.psum_tensor("ps_" + name, shape, dt))
    cq = sb("cq", [64, NTQ], F32); r_cq = P.res("cq")
    sq_ = sb("sq_", [64, NTQ], F32); r_sq = P.res("sq_")
    ck = sb("ck", [64, WIN], F32); r_ck = P.res("ck")
    sk = sb("sk", [64, WIN], F32); r_sk = P.res("sk")
    P.dma("sync", cq[:, :], io["cq"][:, :], writes=[r_cq])
    P.dma("sync", sq_[:, :], io["sq"][:, :], writes=[r_sq])
    P.dma("gpsimd", ck[:, :], io["ck"][:, :], writes=[r_ck])
    P.dma("gpsimd", sk[:, :], io["sk"][:, :], writes=[r_sk])
    nef = sb("nef", [128, 25 * 128], F32); r_nef = P.res("nef")
    nebs = [sb("neb%d" % i, [128, 7 * 128], BF16) for i in range(5)]; r_nebs = [P.res("neb%d" % i) for i in range(5)]
    mdf = nef[:, 0:25 * 128]; r_mdf = r_nef
    md = sb("md", [128, 25 * 128], BF16); r_md = P.res("md")
    P.dma("sync", mdf, io["md"].rearrange("p a q -> p (a q)"), writes=[r_mdf])
    P.op("vector", lambda e: e.tensor_copy(out=md[:, :], in_=mdf), reads=[r_mdf], writes=[r_md])

    stq = sb("stq", [64, NTQ], F32); r_stq = P.res("stq")
    stqp = sb("stqp", [64, NTQ], F32); r_stqp = P.res("stqp")
    stk = sb("stk", [64, WIN], F32); r_stk = P.res("stk")
    stkp = sb("stkp", [64, WIN], F32); r_stkp = P.res("stkp")
    stv = sb("stv", [128, 32, 65], F32); r_stv = P.res("stv")
    qb_ = [sb("qb%d" % i, [64, NTQ], BF16) for i in range(3)]; r_qb = [P.res("qb%d" % i) for i in range(3)]
    kb_ = [sb("kb%d" % i, [64, WIN], BF16) for i in range(3)]; r_kb = [P.res("kb%d" % i) for i in range(3)]
    vb_ = [sb("vb%d" % i, [128, 32, 65], BF16) for i in range(3)]; r_vb = [P.res("vb%d" % i) for i in range(3)]
    pss = [ps("pss%d" % i, [128, 512], F32) for i in range(2)]; r_pss = [P.res("pss%d" % i) for i in range(2)]
    pso = [ps("pso%d" % i, [128, 512], F32) for i in range(2)]; r_pso = [P.res("pso%d" % i) for i in range(2)]
    pe = [sb("pe%d" % i, [128, 512], BF16) for i in range(2)]; r_pe = [P.res("pe%d" % i) for i in range(2)]
    pm = [sb("pm%d" % i, [128, 512], BF16) for i in range(2)]; r_pm = [P.res("pm%d" % i) for i in range(2)]
    rec = sb("rec", [128, 2], F32); r_rec = [P.res("rec0"), P.res("rec1")]
    yh = [sb("yh%d" % i, [128, NQB, 64], F32) for i in range(2)]; r_yh = [P.res("yh%d" % i) for i in range(2)]
    cnt = {"c": 0, "o": 0}

    def load_head(slot, qsrc, qpsrc, ksrc, kpsrc, vsrc, rope):
        P.dma("sync", stq[:, :], qsrc, writes=[r_stq])
        P.dma("sync", stk[:, :], ksrc, writes=[r_stk])
        P.dma("gpsimd", stv[:, :, :], vsrc, writes=[r_stv])
        if rope:
            P.dma("gpsimd", stqp[0:32, :], qsrc[32:64, :], writes=[r_stqp])
            P.dma("gpsimd", stqp[32:64, :], qsrc[0:32, :], writes=[r_stqp])
            P.dma("gpsimd", stkp[0:32, :], ksrc[32:64, :], writes=[r_stkp])
            P.dma("gpsimd", stkp[32:64, :], ksrc[0:32, :], writes=[r_stkp])
            P.op("vector", lambda e: e.tensor_tensor(out=stq[:, :], in0=stq[:, :], in1=cq[:, :], op=ALU.mult),
                 reads=[r_stq, r_cq], writes=[r_stq])
            P.op("gpsimd", lambda e: e.tensor_tensor(out=stqp[:, :], in0=stqp[:, :], in1=sq_[:, :], op=ALU.mult),
                 reads=[r_stqp, r_sq], writes=[r_stqp])
            P.op("vector", lambda e: e.tensor_tensor(out=qb_[slot][:, :], in0=stq[:, :], in1=stqp[:, :], op=ALU.add),
                 reads=[r_stq, r_stqp], writes=[r_qb[slot]])
            P.op("vector", lambda e: e.tensor_tensor(out=stk[:, :], in0=stk[:, :], in1=ck[:, :], op=ALU.mult),
                 reads=[r_stk, r_ck], writes=[r_stk])
            P.op("gpsimd", lambda e: e.tensor_tensor(out=stkp[:, :], in0=stkp[:, :], in1=sk[:, :], op=ALU.mult),
                 reads=[r_stkp, r_sk], writes=[r_stkp])
            P.op("vector", lambda e: e.tensor_tensor(out=kb_[slot][:, :], in0=stk[:, :], in1=stkp[:, :], op=ALU.add),
                 reads=[r_stk, r_stkp], writes=[r_kb[slot]])
        else:
            P.op("vector", lambda e: e.tensor_copy(out=qb_[slot][:, :], in_=stq[:, :]), reads=[r_stq], writes=[r_qb[slot]])
            P.op("gpsimd", lambda e: e.tensor_copy(out=kb_[slot][:, :], in_=stk[:, :]), reads=[r_stk], writes=[r_kb[slot]])
        P.op("gpsimd", lambda e: e.tensor_copy(out=vb_[slot][:, :, :], in_=stv[:, :, :]), reads=[r_stv], writes=[r_vb[slot]])

    def qblock(qb, chunks, E, r_E, ydst, r_y):
        o = cnt["o"] % 2
        cnt["o"] += 1
        ntile = sum(len(c) for c in chunks)
        ti = 0
        cids = []
        for ch in chunks:
            cids.append(cnt["c"] % 2)
            cnt["c"] += 1

        def issue_scores(ci):
            ch, c = chunks[ci], cids[ci]
            for j, (s, kt, ei) in enumerate(ch):
                P.op("tensor", lambda e, c=c, j=j, s=s, kt=kt: e.matmul(
                    pss[c][:, j * 128:(j + 1) * 128], kb_[s][:, kt * 128:(kt + 1) * 128],
                    qb_[s][:, qb * 128:(qb + 1) * 128], start=True, stop=True),
                    reads=[r_kb[s], r_qb[s]], writes=[r_pss[c]])

        issue_scores(0)
        for ci, ch in enumerate(chunks):
            c = cids[ci]
            n = len(ch)
            if ci + 1 < len(chunks):
                issue_scores(ci + 1)
            P.op("scalar", lambda e, c=c, n=n: e.activation(out=pe[c][:, 0:n * 128], in_=pss[c][:, 0:n * 128],
                                                           func=AF.Exp, scale=SCALE),
                 reads=[r_pss[c]], writes=[r_pe[c]])
            e0 = ch[0][2]
            P.op("vector" if c == 0 else "gpsimd", lambda e, c=c, n=n, e0=e0: e.tensor_tensor(
                out=pm[c][:, 0:n * 128], in0=pe[c][:, 0:n * 128], in1=E[:, e0 * 128:(e0 + n) * 128], op=ALU.mult),
                reads=[r_pe[c], r_E], writes=[r_pm[c]])
            for j, (s, kt, ei) in enumerate(ch):
                P.op("tensor", lambda e, c=c, j=j, s=s, kt=kt, ti=ti: e.matmul(
                    pso[o][:, 0:65], pm[c][:, j * 128:(j + 1) * 128], vb_[s][:, kt, :],
                    start=(ti == 0), stop=(ti == ntile - 1)),
                    reads=[r_pm[c], r_vb[s]], writes=[r_pso[o]])
                ti += 1
        P.op("vector", lambda e, o=o: e.reciprocal(out=rec[:, o:o + 1], in_=pso[o][:, 64:65]),
             reads=[r_pso[o]], writes=[r_rec[o]])
        P.op("vector", lambda e, o=o: e.tensor_scalar(out=ydst, in0=pso[o][:, 0:64], scalar1=rec[:, o:o + 1],
                                                      scalar2=None, op0=ALU.mult),
             reads=[r_pso[o], r_rec[o]], writes=[r_y])

    dd = dil_deltas()
    ebase = [0, 3, 8]
    for hg in range(n_hg):
        for g in range(3):
            h = g * 4 + hg
            load_head(g, io["dq"][h], None, io["dk"][h], None, io["dv"][h], True)
        for qb in range(n_qb):
            chunks = []
            for g in range(3):
                tl = [(g, KOFF + qb + d, ebase[g] + di) for di, d in enumerate(dd[g])]
                for a in range(0, len(tl), 4):
                    chunks.append(tl[a:a + 4])
            qblock(qb, chunks, md, r_md, yh[hg % 2][:, qb, :], r_yh[hg % 2])
        P.dma("sync", io["yb"].rearrange("(qb p) c -> p qb c", p=128)[:, :, hg * 64:(hg + 1) * 64], yh[hg % 2][:, :, :],
              reads=[r_yh[hg % 2]])
    for h in range(n_na):
        load_head(0, io["nq"][h], None, io["nk"][h], None, io["nv"][h], False)
        for sl_ in range(5):
            P.dma("sync", nef[:, 0:7 * 128], io["ne"][h][:, sl_ * 7:(sl_ + 1) * 7, :].rearrange("p a q -> p (a q)"), writes=[r_nef])
            P.op("scalar", lambda e, sl_=sl_: e.activation(out=nebs[sl_][:, :], in_=nef[:, 0:7 * 128], func=AF.Exp),
                 reads=[r_nef], writes=[r_nebs[sl_]])
        for qb in range(n_qb):
            sl = na_slot_of_qb(qb)
            tl = [(0, KOFF + qb + d, di) for di, d in enumerate(range(-3, 4))]
            chunks = [tl[0:4], tl[4:7]]
            qblock(qb, chunks, nebs[sl], r_nebs[sl], yh[h % 2][:, qb, :], r_yh[h % 2])
        P.dma("sync", io["yc"].rearrange("(qb p) c -> p qb c", p=128)[:, :, h * 64:(h + 1) * 64], yh[h % 2][:, :, :],
              reads=[r_yh[h % 2]])


def build_attn(**kw):
    nc = bass.Bass("TRN2", target_bir_lowering=False)
    di = lambda name, shape: nc.dram_tensor(name, shape, F32, kind="ExternalInput").ap()
    io = {
        "cq": di("cq", [64, NTQ]), "sq": di("sq", [64, NTQ]), "ck": di("ck", [64, WIN]), "sk": di("sk", [64, WIN]),
        "md": di("md", [128, 25, 128]),
        "dq": di("dq", [12, 64, NTQ]),
        "dk": di("dk", [12, 64, WIN]), "dv": di("dv", [12, 128, 32, 65]),
        "nq": di("nq", [12, 64, NTQ]), "nk": di("nk", [12, 64, WIN]), "nv": di("nv", [12, 128, 32, 65]),
        "ne": di("ne", [12, 128, 35, 128]),
        "yb": nc.dram_tensor("yb", [NTQ, 256], F32, kind="ExternalOutput").ap(),
        "yc": nc.dram_tensor("yc", [NTQ, 768], F32, kind="ExternalOutput").ap(),
    }
    with ExitStack() as st:
        P = Prog(nc, st)
        emit_attn(nc, P, st, io, **kw)
        P.finish()
        P.emit()
    return nc


def rope_consts():
    inv = 10000.0 ** (-np.arange(0, 64, 2, dtype=np.float32) / 64)
    ang = np.arange(4096, dtype=np.float32)[:, None] * inv[None, :]
    cos, sin = np.cos(ang).astype(np.float32), np.sin(ang).astype(np.float32)
    C = np.concatenate([cos, cos], axis=1).T
    S = np.concatenate([-sin, sin], axis=1).T
    return np.ascontiguousarray(C), np.ascontiguousarray(S)


def window(a, half, axis):
    t0 = half * NTQ
    lo, hi = t0 - 1024, t0 + 3072
    pad_lo, pad_hi = max(0, -lo), max(0, hi - 4096)
    sl = [slice(None)] * a.ndim
    sl[axis] = slice(max(lo, 0), min(hi, 4096))
    b = a[tuple(sl)]
    pw = [(0, 0)] * a.ndim
    pw[axis] = (pad_lo, pad_hi)
    return np.pad(b, pw)


def attn_inputs(dq, dk, dv, nq, nk, nv, na_bias, half):
    C, S = rope_consts()
    T = 4096
    t0 = half * NTQ
    perm = np.concatenate([np.arange(32, 64), np.arange(0, 32)])
    def fm(a):
        return a.reshape(T, 12, 64).transpose(1, 2, 0)
    def vaug(v):
        vh = v.reshape(T, 12, 64).transpose(1, 0, 2)
        va = np.concatenate([vh, np.ones((12, T, 1), np.float32)], axis=2)
        vw = window(va, half, 1)
        return np.ascontiguousarray(vw.reshape(12, 32, 128, 65).transpose(0, 2, 1, 3))
    dqf, dkf, nqf, nkf = fm(dq), fm(dk), fm(nq), fm(nk)
    m = {
        "cq": np.ascontiguousarray(C[:, t0:t0 + NTQ]), "sq": np.ascontiguousarray(S[:, t0:t0 + NTQ]),
        "ck": np.ascontiguousarray(window(C, half, 1)), "sk": np.ascontiguousarray(window(S, half, 1)),
        "md": dil_masks(),
        "dq": np.ascontiguousarray(dqf[:, :, t0:t0 + NTQ]),
        "dk": np.ascontiguousarray(window(dkf, half, 2)),
        "dv": vaug(dv),
        "nq": np.ascontiguousarray(nqf[:, :, t0:t0 + NTQ]), "nk": np.ascontiguousarray(window(nkf, half, 2)),
        "nv": vaug(nv),
        "ne": np.ascontiguousarray(np.stack([na_tables(na_bias[h], na_slot_m(half)) for h in range(12)])),
    }
    return m


T = 4096
L = 64
SEG = 256
NC = SEG // L
NSEG = T // SEG
NST = NC * 4
NBK = NST // 8
C0 = -float(np.exp(-0.5))
LNX_EPS = 64e-5


def emit_rwkv(nc, P, st, io, nseg=NSEG, dirs=(0, 1), stage=9):
    sb = lambda name, shape, dt=F32: st.enter_context(nc.sbuf_tensor("sb_" + name, shape, dt))
    ps = lambda name, shape, dt=F32: st.enter_context(nc.psum_tensor("ps_" + name, shape, dt))
    V = lambda fn, r, w: P.op("vector", fn, reads=r, writes=w)
    G = lambda fn, r, w: P.op("gpsimd", fn, reads=r, writes=w)
    A = lambda fn, r, w: P.op("scalar", fn, reads=r, writes=w)
    M = lambda fn, r, w: P.op("tensor", fn, reads=r, writes=w)

    def tile(name, shape, dt=F32):
        return sb(name, shape, dt), P.res(name)

    ident, r_ident = tile("ident", [128, 128])
    icat, r_icat = tile("icat", [128, 64])
    bones, r_bones = tile("bones", [128, 128])
    msk, r_msk = tile("msk", [64, 4, 64])
    rmask, r_rmask = tile("rmask", [128, SEG])
    prm, r_prm = tile("prm", [128, 32])
    mul_, r_mul = tile("mul", [128, 3])
    w2s, r_w2s = tile("w2s", [128, 512])
    a2s, r_a2s = tile("a2s", [128, 512])
    g2s, r_g2s = tile("g2s", [128, 256])
    hm, r_hm = tile("hm", [128, 2])
    lng, r_lng = tile("lng", [64, 256])
    lnb, r_lnb = tile("lnb", [64, 256])
    for (t_, r_, src) in ((ident, r_ident, "ident"), (icat, r_icat, "icat"), (bones, r_bones, "bones"),
                          (rmask, r_rmask, "rmask"), (prm, r_prm, "prm"), (mul_, r_mul, "mul"),
                          (w2s, r_w2s, "w2s"), (a2s, r_a2s, "a2s"), (g2s, r_g2s, "g2s"), (hm, r_hm, "hm"),
                          (lng, r_lng, "lng"), (lnb, r_lnb, "lnb")):
        P.dma("sync", t_[:, :], io[src][:, :], writes=[r_])
    P.dma("sync", msk[:, :, :], io["msk"][:, :, :], writes=[r_msk])
    V(lambda e: e.tensor_scalar(out=prm[:, 18:20], in0=prm[:, 16:18], scalar1=-1.0, scalar2=1.0, op0=ALU.mult, op1=ALU.add),
      [r_prm], [r_prm])
    pc_col = lambda base, pc: prm[:, base + pc:base + pc + 1]

    SP = SEG + 2
    raw = {}
    for nm in ("r0", "r1", "k0", "k1", "v0", "v1"):
        raw[nm] = tile("raw_" + nm, [128, SP])
    raw["wd"] = tile("raw_wd", [64, SP]); raw["ad"] = tile("raw_ad", [64, SP]); raw["gd"] = tile("raw_gd", [96, SP])
    shf = {}
    for nm in ("r0", "r1", "k0", "k1", "v0", "v1"):
        shf[nm] = tile("shf_" + nm, [128, SEG])
    shf["wd"] = tile("shf_wd", [64, SEG]); shf["ad"] = tile("shf_ad", [128, SEG]); shf["gd"] = tile("shf_gd", [96, SEG])
    tmpa, r_tmpa = tile("tmpa", [128, SEG]); tmpb, r_tmpb = tile("tmpb", [128, SEG])
    tw, r_tw = tile("tw", [128, SEG])
    sgd, r_sgd = tile("sgd", [128, SEG])
    fm = {}
    for nm in ("lw", "lr", "kk", "kd", "bb", "cs", "ci", "ce", "e1", "e2", "e3", "e4", "t1", "t2", "lr0", "kd0", "bon"):
        fm[nm] = tile("fm_" + nm, [128, SEG])
    tot, r_tot = tile("tot", [128, NC]); WL, r_WL = tile("WL", [128, NC])
    outf = {}
    for nm in ("Af", "Bf", "Kf", "Rf", "Bhf", "Khf"):
        for pc in range(2):
            outf[nm, pc] = tile("of_%s%d" % (nm, pc), [128, SEG])
    Dg = [tile("Dg%d" % pc, [128, NC, 64]) for pc in range(2)]
    XT = {}
    for nm in ("At", "Bht", "Kht", "Vt", "bont"):
        XT[nm] = tile("xt_" + nm, [128, NC, 256])
    gat, r_gat = tile("gat", [64, NC, 256])
    if os.environ.get("PADKB"):
        tile("padx", [128, 256 * int(os.environ["PADKB"])])
    Wd = [tile("wide%d" % i, [128, NST * 64]) for i in range(10)]
    Sst, r_Sst = tile("Sst", [128, NC + 1, 256])
    mskd = {}
    for nm in ("Af", "Bf", "Rf"):
        for pc in range(2):
            for hh in range(2):
                mskd[nm, pc, hh] = tile("mk_%s%d%d" % (nm, pc, hh), [128, SEG])
    Dgm = {(pc, hh): tile("Dgm%d%d" % (pc, hh), [128, NC, 64]) for pc in range(2) for hh in range(2)}
    for (t_, r_) in [Wd[i] for i in range(10)] + [XT[k] for k in XT]:
        G(lambda e, t_=t_: e.memset(t_[:], 0.0), [], [r_])
    for (t_, r_) in ((tw, r_tw), (sgd, r_sgd), shf["ad"]):
        G(lambda e, t_=t_: e.memset(t_[:, :], 0.0), [], [r_])
    V(lambda e: e.memset(Sst[:, :, :], 0.0), [], [r_Sst])
    yfb, r_yfb = tile("yfb", [64, NC, 256])
    yo, r_yo = tile("yo", [64, NC, 256])
    stat, r_stat = tile("stat", [64, NC * 4 * 2])
    pbank = [(ps("pb%d" % i, [128, 512]), P.res("pb%d" % i)) for i in range(8)]
    bk = {"i": 0}

    def nextbank():
        b = pbank[bk["i"] % 8]
        bk["i"] += 1
        return b

    def shift(nm, rows, mu_ap, r_mu):
        (rw, r_rw), (o, r_o) = raw[nm], shf[nm]
        G(lambda e: e.tensor_tensor(out=tmpa[0:rows, :], in0=rw[0:rows, 0:SEG], in1=rw[0:rows, 2:SEG + 2], op=ALU.add),
          [r_rw], [r_tmpa])
        V(lambda e: e.scalar_tensor_tensor(out=tmpb[0:rows, :], in0=tmpa[0:rows, :], scalar=0.5, op0=ALU.mult,
                                           in1=rw[0:rows, 1:SEG + 1], op1=ALU.subtract), [r_tmpa, r_rw], [r_tmpb])
        V(lambda e: e.scalar_tensor_tensor(out=o[0:rows, :], in0=tmpb[0:rows, :], scalar=mu_ap, op0=ALU.mult,
                                           in1=rw[0:rows, 1:SEG + 1], op1=ALU.add), [r_tmpb, r_rw, r_mu], [r_o])

    def lora_sig(d, pc, src, r_src, wts, r_wts, bias_base, out, r_out):
        pb, r_pb = nextbank()
        M(lambda e: e.matmul(pb[:, 0:SEG], wts[:, d * 256 + pc * 128:d * 256 + (pc + 1) * 128],
                             src[:, :], start=True, stop=True), [r_wts, r_src], [r_pb])
        A(lambda e: e.activation(out=out[:, :], in_=pb[:, 0:SEG], func=AF.Sigmoid,
                                 bias=prm[:, bias_base + d * 2 + pc:bias_base + d * 2 + pc + 1]), [r_pb, r_prm], [r_out])

    def kd_from(pc, lr_t, r_lr, out, r_out):
        kp, r_kp = shf["k%d" % pc]
        V(lambda e: e.tensor_scalar(out=fm["t1"][0][:, :], in0=lr_t[:, :], scalar1=pc_col(16, pc), scalar2=pc_col(18, pc),
                                    op0=ALU.mult, op1=ALU.add), [r_lr, r_prm], [fm["t1"][1]])
        V(lambda e: e.tensor_tensor(out=out[:, :], in0=fm["t1"][0][:, :], in1=kp[:, :], op=ALU.mult),
          [fm["t1"][1], r_kp], [r_out])

    yf_res = P.res("yf_dram")

    for d in dirs:
        segs = list(range(nseg)) if d == 0 else list(range(nseg - 1, -1, -1))
        V(lambda e: e.memset(Sst[0:64, 0, :], 0.0), [], [r_Sst])
        for s in segs:
            s0 = s * SEG
            for wi, wn in enumerate(("r", "k", "v")):
                for pc in range(2):
                    t_, r_ = raw["%s%d" % (wn, pc)]
                    P.dma("sync" if pc == 0 else "gpsimd", t_[:, :], io["rkv"][wi, pc * 128:(pc + 1) * 128, s0:s0 + SP], writes=[r_])
            for nm, rows in (("wd", 64), ("ad", 64), ("gd", 96)):
                t_, r_ = raw[nm]
                P.dma("sync", t_[:, :], io[nm][:, s0:s0 + SP], writes=[r_])
            for wi, wn in enumerate(("r", "k", "v")):
                for pc in range(2):
                    shift("%s%d" % (wn, pc), 128, prm[:, wi * 2 + pc:wi * 2 + pc + 1], r_prm)
            shift("wd", 64, mul_[0:64, 0:1], r_mul)
            shift("ad", 64, mul_[0:64, 1:2], r_mul)
            if d == 1:
                shift("gd", 96, mul_[0:96, 2:3], r_mul)
            if stage < 2:
                continue
            A(lambda e: e.activation(out=tw[0:64, :], in_=shf["wd"][0][:, :], func=AF.Tanh), [shf["wd"][1]], [r_tw])
            for pc in range(2):
                kp, r_kp = shf["k%d" % pc]
                rp, r_rp = shf["r%d" % pc]
                F = lambda nm: fm[nm][0]
                Rr = lambda nm: fm[nm][1]
                lora_sig(d, pc, tw, r_tw, w2s, r_w2s, 6, F("lw"), Rr("lw"))
                lora_sig(d, pc, shf["ad"][0], shf["ad"][1], a2s, r_a2s, 10, F("lr"), Rr("lr"))
                V(lambda e: e.tensor_scalar(out=F("lw")[:, :], in0=F("lw")[:, :], scalar1=C0, scalar2=None, op0=ALU.mult),
                  [Rr("lw")], [Rr("lw")])
                V(lambda e: e.tensor_scalar(out=F("t1")[:, :], in0=kp[:, :], scalar1=pc_col(14, pc), scalar2=None, op0=ALU.mult),
                  [r_kp, r_prm], [Rr("t1")])
                A(lambda e: e.activation(out=F("t2")[:, :], in_=F("t1")[:, :], func=AF.Square), [Rr("t1")], [Rr("t2")])
                pb, r_pb = nextbank()
                M(lambda e, pb=pb: e.matmul(pb[:, 0:SEG], bones[:, :], F("t2")[:, :], start=True, stop=True),
                  [r_bones, Rr("t2")], [r_pb])
                A(lambda e, pb=pb: e.activation(out=F("t2")[:, :], in_=pb[:, 0:SEG], func=AF.Sqrt), [r_pb], [Rr("t2")])
                V(lambda e: e.tensor_scalar(out=F("t2")[:, :], in0=F("t2")[:, :], scalar1=1e-12, scalar2=None, op0=ALU.max),
                  [Rr("t2")], [Rr("t2")])
                V(lambda e: e.reciprocal(out=F("t2")[:, :], in_=F("t2")[:, :]), [Rr("t2")], [Rr("t2")])
                V(lambda e: e.tensor_tensor(out=F("kk")[:, :], in0=F("t1")[:, :], in1=F("t2")[:, :], op=ALU.mult),
                  [Rr("t1"), Rr("t2")], [Rr("kk")])
                kd_from(pc, F("lr"), Rr("lr"), F("kd"), Rr("kd"))
                G(lambda e: e.tensor_tensor(out=F("bb")[:, :], in0=F("kk")[:, :], in1=F("lr")[:, :], op=ALU.mult),
                  [Rr("kk"), Rr("lr")], [Rr("bb")])
                V(lambda e: e.tensor_tensor_scan(out=F("cs")[:, :], data0=rmask[:, :], data1=F("lw")[:, :], initial=0.0,
                                                 op0=ALU.mult, op1=ALU.add), [r_rmask, Rr("lw")], [Rr("cs")])
                cs3 = F("cs")[:, :].rearrange("p (c l) -> p c l", l=L)
                V(lambda e: e.tensor_copy(out=tot[:, :].unsqueeze(2), in_=cs3[:, :, L - 1:L]), [Rr("cs")], [r_tot])
                totbc = tot[:, :].unsqueeze(2).broadcast_to([128, NC, L])
                v3 = lambda t_: t_[:, :].rearrange("p (c l) -> p c l", l=L)
                if d == 0:
                    ci, r_ci = F("cs"), Rr("cs")
                else:
                    ci, r_ci = F("ci"), Rr("ci")
                    V(lambda e: e.tensor_tensor(out=F("t1")[:, :], in0=F("lw")[:, :], in1=F("cs")[:, :], op=ALU.subtract),
                      [Rr("lw"), Rr("cs")], [Rr("t1")])
                    V(lambda e: e.tensor_tensor(out=v3(F("ci")), in0=v3(F("t1")), in1=totbc, op=ALU.add),
                      [Rr("t1"), r_tot], [Rr("ci")])
                V(lambda e: e.tensor_tensor(out=F("ce")[:, :], in0=ci[:, :], in1=F("lw")[:, :], op=ALU.subtract),
                  [r_ci, Rr("lw")], [Rr("ce")])
                V(lambda e: e.tensor_tensor(out=v3(F("t2")), in0=totbc, in1=v3(ci), op=ALU.subtract),
                  [r_ci, r_tot], [Rr("t2")])
                A(lambda e: e.activation(out=F("e1")[:, :], in_=F("ce")[:, :], func=AF.Exp), [Rr("ce")], [Rr("e1")])
                A(lambda e: e.activation(out=F("e2")[:, :], in_=ci[:, :], func=AF.Exp, scale=-1.0), [r_ci], [Rr("e2")])
                A(lambda e: e.activation(out=F("e3")[:, :], in_=ci[:, :], func=AF.Exp), [r_ci], [Rr("e3")])
                A(lambda e: e.activation(out=F("e4")[:, :], in_=F("t2")[:, :], func=AF.Exp), [Rr("t2")], [Rr("e4")])
                A(lambda e: e.activation(out=WL[:, :], in_=tot[:, :], func=AF.Exp), [r_tot], [r_WL])
                O = lambda nm: outf[nm, pc][0]
                Ro = lambda nm: outf[nm, pc][1]
                V(lambda e: e.scalar_tensor_tensor(out=O("Af")[:, :], in0=F("kk")[:, :], scalar=-1.0, op0=ALU.mult,
                                                   in1=F("e1")[:, :], op1=ALU.mult), [Rr("kk"), Rr("e1")], [Ro("Af")])
                G(lambda e: e.tensor_tensor(out=O("Bf")[:, :], in0=F("bb")[:, :], in1=F("e2")[:, :], op=ALU.mult),
                  [Rr("bb"), Rr("e2")], [Ro("Bf")])
                V(lambda e: e.tensor_tensor(out=O("Kf")[:, :], in0=F("kd")[:, :], in1=F("e2")[:, :], op=ALU.mult),
                  [Rr("kd"), Rr("e2")], [Ro("Kf")])
                G(lambda e: e.tensor_tensor(out=O("Rf")[:, :], in0=rp[:, :], in1=F("e3")[:, :], op=ALU.mult),
                  [r_rp, Rr("e3")], [Ro("Rf")])
                V(lambda e: e.tensor_tensor(out=O("Bhf")[:, :], in0=F("bb")[:, :], in1=F("e4")[:, :], op=ALU.mult),
                  [Rr("bb"), Rr("e4")], [Ro("Bhf")])
                G(lambda e: e.tensor_tensor(out=O("Khf")[:, :], in0=F("kd")[:, :], in1=F("e4")[:, :], op=ALU.mult),
                  [Rr("kd"), Rr("e4")], [Ro("Khf")])
                for nm_ in ("Af", "Bf", "Rf"):
                    for hh in range(2):
                        mt, r_mt = mskd[nm_, pc, hh]
                        (G if hh == 0 else V)(lambda e, mt=mt, nm_=nm_, hh=hh: e.tensor_scalar(
                            out=mt[:, :], in0=O(nm_)[:, :], scalar1=hm[:, hh:hh + 1], scalar2=None, op0=ALU.mult),
                            [Ro(nm_), r_hm], [r_mt])
                dg, r_dg = Dg[pc]
                V(lambda e, dg=dg: e.tensor_tensor(out=dg[:, :, :], in0=icat[:, :].unsqueeze(1).broadcast_to([128, NC, 64]),
                                                   in1=WL[:, :].unsqueeze(2).broadcast_to([128, NC, 64]), op=ALU.mult),
                  [r_icat, r_WL], [r_dg])
                for hh in range(2):
                    dm, r_dm = Dgm[pc, hh]
                    V(lambda e, dm=dm, dg=dg, hh=hh: e.tensor_scalar(out=dm[:, :, :], in0=dg[:, :, :], scalar1=hm[:, hh:hh + 1],
                                                                   scalar2=None, op0=ALU.mult), [r_dg, r_hm], [r_dm])
                if d == 1:
                    lora_sig(0, pc, shf["ad"][0], shf["ad"][1], a2s, r_a2s, 10, F("lr0"), Rr("lr0"))
                    kd_from(pc, F("lr0"), Rr("lr0"), F("kd0"), Rr("kd0"))
                    V(lambda e: e.tensor_tensor(out=F("kd0")[:, :], in0=F("kd0")[:, :], in1=F("kd")[:, :], op=ALU.add),
                      [Rr("kd0"), Rr("kd")], [Rr("kd0")])
                    V(lambda e: e.scalar_tensor_tensor(out=F("t1")[:, :], in0=rp[:, :], scalar=pc_col(20, pc), op0=ALU.mult,
                                                       in1=F("kd0")[:, :], op1=ALU.mult), [r_rp, r_prm, Rr("kd0")], [Rr("t1")])
                    pb, r_pb = nextbank()
                    M(lambda e, pb=pb: e.matmul(pb[:, 0:SEG], bones[:, :], F("t1")[:, :], start=True, stop=True),
                      [r_bones, Rr("t1")], [r_pb])
                    vp, r_vp = shf["v%d" % pc]
                    V(lambda e, pb=pb: e.tensor_tensor(out=F("bon")[:, :], in0=pb[:, 0:SEG], in1=vp[:, :], op=ALU.mult),
                      [r_pb, r_vp], [Rr("bon")])
                if stage < 3:
                    continue
                tlist = [("At", O("Af"), Ro("Af")), ("Bht", O("Bhf"), Ro("Bhf")), ("Kht", O("Khf"), Ro("Khf")),
                         ("Vt", shf["v%d" % pc][0], shf["v%d" % pc][1])]
                if d == 1:
                    tlist.append(("bont", F("bon"), Rr("bon")))
                for (xn, src, r_src) in tlist:
                    pb, r_pb = nextbank()
                    for c in range(NC):
                        M(lambda e, pb=pb, c=c, src=src: e.transpose(pb[0:64, c * 128:(c + 1) * 128], src[:, c * L:(c + 1) * L],
                                                                     ident[:, :]), [r_src, r_ident], [r_pb])
                    xt, r_xt = XT[xn]
                    A(lambda e, pb=pb, xt=xt: e.activation(out=xt[0:64, :, pc * 128:(pc + 1) * 128],
                                                           in_=pb[0:64, 0:NC * 128].rearrange("p (c n) -> p c n", n=128),
                                                           func=AF.Copy), [r_pb], [r_xt])
            if d == 1:
                A(lambda e: e.activation(out=sgd[0:96, :], in_=shf["gd"][0][:, :], func=AF.Sigmoid), [shf["gd"][1]], [r_sgd])
                pb, r_pb = nextbank()
                pb2, r_pb2 = nextbank()
                for c in range(NC):
                    tgt, r_tgt = (pb, r_pb) if c < 2 else (pb2, r_pb2)
                    M(lambda e, tgt=tgt, c=c: e.matmul(tgt[0:64, (c % 2) * 256:(c % 2 + 1) * 256], sgd[:, c * L:(c + 1) * L],
                                                       g2s[:, :], start=True, stop=True), [r_sgd, r_g2s], [r_tgt])
                V(lambda e, pb=pb: e.tensor_copy(out=gat[:, 0:2, :], in_=pb[0:64, :].rearrange("p (c n) -> p c n", n=256)),
                  [r_pb], [r_gat])
                V(lambda e, pb2=pb2: e.tensor_copy(out=gat[:, 2:4, :], in_=pb2[0:64, :].rearrange("p (c n) -> p c n", n=256)),
                  [r_pb2], [r_gat])

            if stage < 4:
                continue
            pcnt = {'n': 0}
            def fmop(nm, c, h):
                t_, r_ = outf[nm, h // 2]
                return t_[:, c * L:(c + 1) * L], r_

            def fmm(nm, c, h):
                t_, r_ = mskd[nm, h // 2, h % 2]
                return t_[:, c * L:(c + 1) * L], r_

            def tmop(nm, c, h):
                t_, r_ = XT[nm]
                return t_[:, c, h * 64:(h + 1) * 64], r_

            def wop(i, c, h):
                t_, r_ = Wd[i]
                stn = (h % 2) * 8 + c * 2 + h // 2
                return t_[:, stn * 64:(stn + 1) * 64], r_

            def product(terms_fn, evac_fn):
                pcnt["n"] += 1
                if pcnt["n"] > int(os.environ.get("MAXP", 999)):
                    return
                banks = [nextbank() for _ in range(NBK)]
                for c in range(NC):
                    for h in range(4):
                        if os.environ.get("EVENH") and h % 2 == 1:
                            continue
                        stn = (h % 2) * 8 + c * 2 + h // 2
                        pb, r_pb = banks[stn // 8]
                        terms = terms_fn(c, h)
                        for ti, (la, rl, ra, rr) in enumerate(terms):
                            M(lambda e, pb=pb, stn=stn, la=la, ra=ra, ti=ti, n=len(terms): e.matmul(
                                pb[0:64, (stn % 8) * 64:(stn % 8 + 1) * 64], la, ra, start=(ti == 0), stop=(ti == n - 1)),
                                [rl, rr], [r_pb])
                for bi, (pb, r_pb) in enumerate(banks):
                    evac_fn(bi, pb[0:64, :], r_pb)

            mi = {"su": 0, "sl": 1, "iu": 2, "il": 3}
            if d == 1:
                mi = {"su": 1, "sl": 0, "iu": 3, "il": 2}
            mbc = lambda nm: msk[:, mi[nm], :].unsqueeze(1).broadcast_to([64, 8, 64])
            w3 = lambda i, bi: Wd[i][0][0:64, bi * 512:(bi + 1) * 512].rearrange("p (s n) -> p s n", n=64)
            p3 = lambda pa: pa.rearrange("p (s n) -> p s n", n=64)
            ibc = ident[0:64, 0:64].unsqueeze(1).broadcast_to([64, 8, 64])
            eng_rr = {"i": 0}

            def ev_mask(wi, mname):
                def f(bi, pa, r_pb):
                    eng_rr["i"] += 1
                    V(lambda e: e.tensor_tensor(out=w3(wi, bi), in0=p3(pa), in1=mbc(mname), op=ALU.mult),
                      [r_pb, r_msk], [Wd[wi][1]])
                return f

            def ev_copy(wi, eng="scalar"):
                def f(bi, pa, r_pb):
                    if eng == "scalar":
                        A(lambda e: e.activation(out=Wd[wi][0][0:64, bi * 512:(bi + 1) * 512], in_=pa, func=AF.Copy),
                          [r_pb], [Wd[wi][1]])
                    else:
                        V(lambda e: e.tensor_copy(out=Wd[wi][0][0:64, bi * 512:(bi + 1) * 512], in_=pa), [r_pb], [Wd[wi][1]])
                return f

            def ev_copy_plus_ident(wi_raw, wi_id):
                def f(bi, pa, r_pb):
                    if wi_raw is not None:
                        A(lambda e: e.activation(out=Wd[wi_raw][0][0:64, bi * 512:(bi + 1) * 512], in_=pa, func=AF.Copy),
                          [r_pb], [Wd[wi_raw][1]])
                        G(lambda e: e.tensor_tensor(out=w3(wi_id, bi), in0=w3(wi_raw, bi), in1=ibc, op=ALU.add),
                          [Wd[wi_raw][1], r_ident], [Wd[wi_id][1]])
                    else:
                        V(lambda e: e.tensor_tensor(out=w3(wi_id, bi), in0=p3(pa), in1=ibc, op=ALU.add),
                          [r_pb, r_ident], [Wd[wi_id][1]])
                return f

            Pa, Pb_, Qa, Qb, IQ, Ta, Tb, MAK, NBR, NKR = range(10)
            def ev_N(bi, pa, r_pb):
                evn = int(os.environ.get("EVN", 2))
                if evn == 0:
                    return
                if evn == 1:
                    V(lambda e: e.tensor_tensor(out=w3(Pa, bi), in0=p3(pa), in1=mbc("su"), op=ALU.mult), [r_pb, r_msk], [Wd[Pa][1]])
                    return
                V(lambda e: e.tensor_tensor(out=w3(Pa, bi), in0=p3(pa), in1=mbc("su"), op=ALU.mult), [r_pb, r_msk], [Wd[Pa][1]])
                G(lambda e: e.tensor_tensor(out=w3(Ta, bi), in0=w3(Pa, bi), in1=ibc, op=ALU.add), [Wd[Pa][1], r_ident], [Wd[Ta][1]])
            product(lambda c, h: [fmop("Bf", c, h) + fmm("Af", c, h)], ev_N)
            product(lambda c, h: [fmop("Af", c, h) + fmm("Bf", c, h)], ev_mask(Qa, "sl"))
            product(lambda c, h: [fmop("Kf", c, h) + fmm("Af", c, h)], ev_mask(MAK, "su"))
            product(lambda c, h: [fmop("Bf", c, h) + fmm("Rf", c, h)], ev_mask(NBR, "iu"))
            product(lambda c, h: [fmop("Kf", c, h) + fmm("Rf", c, h)], ev_mask(NKR, "iu"))
            Pc, Pn, Qc, Qn, Tc, Tn = Pa, Pb_, Qa, Qb, Ta, Tb
            for kq in range(5):
                if kq < 4:
                    product(lambda c, h, Qc=Qc, Pc=Pc: [wop(Qc, c, h) + wop(Pc, c, h)], ev_copy(Pn, "vector"))
                product(lambda c, h, Qc=Qc, Pc=Pc: [wop(Pc, c, h) + wop(Qc, c, h)], ev_copy_plus_ident(Qn if kq < 4 else None, IQ))
                product(lambda c, h, Tc=Tc: [wop(IQ, c, h) + wop(Tc, c, h)], ev_copy(Tn, "scalar"))
                Pc, Pn, Qc, Qn, Tc, Tn = Pn, Pc, Qn, Qc, Tn, Tc
            TT = Tc
            Z, UV, ATT, RH, YV, GT, HH = Pa, Pb_, Qa, Qb, IQ, (Ta if TT == Tb else Tb), MAK
            product(lambda c, h: [wop(MAK, c, h) + tmop("Vt", c, h)], ev_copy(Z, "vector"))
            product(lambda c, h: [wop(TT, c, h) + wop(Z, c, h)], ev_copy(UV, "scalar"))
            product(lambda c, h: [wop(TT, c, h) + tmop("At", c, h)], ev_copy(ATT, "vector"))

            def icat_op(h):
                return icat[:, :], r_icat

            def dg_op(c, h):
                t_, r_ = Dgm[h // 2, h % 2]
                return t_[:, c, :], r_
            def ev_add(wi, wsrc):
                def f(bi, pa, r_pb):
                    V(lambda e: e.tensor_tensor(out=Wd[wi][0][0:64, bi * 512:(bi + 1) * 512], in0=pa,
                                                in1=Wd[wsrc][0][0:64, bi * 512:(bi + 1) * 512], op=ALU.add),
                      [r_pb, Wd[wsrc][1]], [Wd[wi][1]])
                return f
            product(lambda c, h: [icat_op(h) + fmm("Rf", c, h)], ev_copy(RH, "scalar"))
            product(lambda c, h: [wop(ATT, c, h) + wop(NBR, c, h)], ev_add(RH, RH))
            product(lambda c, h: [wop(NBR, c, h) + wop(UV, c, h), wop(NKR, c, h) + tmop("Vt", c, h)], ev_copy(YV, "vector"))
            product(lambda c, h: [icat_op(h) + dg_op(c, h)], ev_copy(GT, "scalar"))
            product(lambda c, h: [wop(ATT, c, h) + tmop("Bht", c, h)], ev_add(GT, GT))
            product(lambda c, h: [tmop("Bht", c, h) + wop(UV, c, h), tmop("Kht", c, h) + tmop("Vt", c, h)], ev_copy(HH, "vector"))
            if stage < 5:
                continue
            corder = list(range(NC)) if d == 0 else list(range(NC - 1, -1, -1))
            for ci_, c in enumerate(corder):
                pb, r_pb = nextbank()
                for h in range(4):
                    ga, rg = wop(GT, c, h)
                    sc = (h % 2) * 128 + (h // 2) * 64
                    M(lambda e, pb=pb, sc=sc, ga=ga, ci_=ci_: e.matmul(pb[0:64, sc:sc + 64], ga,
                                                                       Sst[:, ci_, sc:sc + 64], start=True, stop=True),
                      [rg, r_Sst], [r_pb])
                hh_ = Wd[HH][0][0:64, :].rearrange("p (b n) -> p b n", b=2)[:, :, c * 128:(c + 1) * 128]
                V(lambda e, pb=pb, ci_=ci_, hh_=hh_: e.tensor_tensor(out=Sst[0:64, ci_ + 1, :].rearrange("p (b n) -> p b n", b=2),
                                                                     in0=pb[0:64, 0:256].rearrange("p (b n) -> p b n", b=2),
                                                                     in1=hh_, op=ALU.add),
                  [r_pb, Wd[HH][1]], [r_Sst])
            pby = [nextbank() for _ in range(2)]
            for ci_, c in enumerate(corder):
                pb, r_pb = pby[c // 2]
                for h in range(4):
                    ra_, rr_ = wop(RH, c, h)
                    sc = (h % 2) * 128 + (h // 2) * 64
                    M(lambda e, pb=pb, c=c, sc=sc, ra_=ra_, ci_=ci_: e.matmul(
                        pb[0:64, (c % 2) * 256 + sc:(c % 2) * 256 + sc + 64], ra_, Sst[:, ci_, sc:sc + 64],
                        start=True, stop=True), [rr_, r_Sst], [r_pb])
            for c in range(NC):
                pb, r_pb = pby[c // 2]
                yv_ = Wd[YV][0][0:64, :].rearrange("p (b n) -> p b n", b=2)[:, :, c * 128:(c + 1) * 128].rearrange("p hp (hq n) -> p hp hq n", hq=2)
                V(lambda e, pb=pb, c=c, yv_=yv_: e.tensor_tensor(
                    out=yo[:, c, :].rearrange("p (hq hp n) -> p hp hq n", hq=2, hp=2),
                    in0=pb[0:64, (c % 2) * 256:(c % 2 + 1) * 256].rearrange("p (hp hq n) -> p hp hq n", hp=2, hq=2),
                    in1=yv_, op=ALU.add), [r_pb, Wd[YV][1]], [r_yo])
            V(lambda e: e.tensor_copy(out=Sst[0:64, 0, :], in_=Sst[0:64, NC, :]), [r_Sst], [r_Sst])
            ydst = io["yf"][s0:s0 + SEG, :].rearrange("(c p) n -> p c n", p=64)
            if d == 0:
                P.dma("sync", ydst, yo[:, :, :], reads=[r_yo], writes=[yf_res])
            else:
                if 0 in dirs:
                    P.dma("sync", yfb[:, :, :], ydst, reads=[yf_res], writes=[r_yfb])
                    V(lambda e: e.tensor_tensor(out=yo[:, :, :], in0=yo[:, :, :], in1=yfb[:, :, :], op=ALU.add), [r_yo, r_yfb], [r_yo])
                y4 = yo[:, :, :].rearrange("p c (h n) -> p (c h) n", n=64)
                NH_ = NC * 4
                mean = stat[:, 0:NH_]
                var = stat[:, NH_:2 * NH_]
                V(lambda e: e.tensor_reduce(out=mean, in_=y4, axis=AX.X, op=ALU.add), [r_yo], [r_stat])
                V(lambda e: e.tensor_scalar(out=mean, in0=mean, scalar1=1.0 / 64, scalar2=None, op0=ALU.mult), [r_stat], [r_stat])
                V(lambda e: e.tensor_tensor(out=y4, in0=y4, in1=mean.unsqueeze(2).broadcast_to([64, NH_, 64]), op=ALU.subtract),
                  [r_yo, r_stat], [r_yo])
                yq = yfb[:, :, :].rearrange("p c (h n) -> p (c h) n", n=64)
                V(lambda e: e.tensor_tensor(out=yq, in0=y4, in1=y4, op=ALU.mult), [r_yo], [r_yfb])
                V(lambda e: e.tensor_reduce(out=var, in_=yq, axis=AX.X, op=ALU.add), [r_yfb], [r_stat])
                A(lambda e: e.activation(out=var, in_=var, func=AF.Sqrt, scale=1.0 / 64, bias=LNX_EPS), [r_stat], [r_stat])
                V(lambda e: e.reciprocal(out=var, in_=var), [r_stat], [r_stat])
                V(lambda e: e.tensor_tensor(out=y4, in0=y4, in1=var.unsqueeze(2).broadcast_to([64, NH_, 64]), op=ALU.mult),
                  [r_yo, r_stat], [r_yo])
                V(lambda e: e.tensor_tensor(out=yo[:, :, :], in0=yo[:, :, :], in1=lng[:, :].unsqueeze(1).broadcast_to([64, NC, 256]),
                                            op=ALU.mult), [r_yo, r_lng], [r_yo])
                V(lambda e: e.tensor_tensor(out=yo[:, :, :], in0=yo[:, :, :], in1=lnb[:, :].unsqueeze(1).broadcast_to([64, NC, 256]),
                                            op=ALU.add), [r_yo, r_lnb], [r_yo])
                V(lambda e: e.tensor_tensor(out=yo[:, :, :], in0=yo[:, :, :], in1=XT["bont"][0][0:64, :, :], op=ALU.add),
                  [r_yo, XT["bont"][1]], [r_yo])
                V(lambda e: e.tensor_tensor(out=yo[:, :, :], in0=yo[:, :, :], in1=gat[:, :, :], op=ALU.mult), [r_yo, r_gat], [r_yo])
                P.dma("sync", io["ya"][s0:s0 + SEG, :].rearrange("(c p) n -> p c n", p=64), yo[:, :, :], reads=[r_yo])


def build_rwkv(**kw):
    nc = bass.Bass("TRN2", target_bir_lowering=False)
    di = lambda name, shape: nc.dram_tensor(name, shape, F32, kind="ExternalInput").ap()
    io = {
        "rkv": di("rkv", [3, 256, T + 2]), "wd": di("wd", [64, T + 2]), "ad": di("ad", [64, T + 2]), "gd": di("gd", [96, T + 2]),
        "ident": di("ident", [128, 128]), "icat": di("icat", [128, 64]), "bones": di("bones", [128, 128]),
        "msk": di("msk", [64, 4, 64]), "rmask": di("rmask", [128, SEG]), "prm": di("prm", [128, 32]), "mul": di("mul", [128, 3]),
        "w2s": di("w2s", [128, 512]), "a2s": di("a2s", [128, 512]), "g2s": di("g2s", [128, 256]), "hm": di("hm", [128, 2]),
        "lng": di("lng", [64, 256]), "lnb": di("lnb", [64, 256]),
        "yf": nc.dram_tensor("yf", [T, 256], F32, kind="ExternalOutput").ap(),
        "ya": nc.dram_tensor("ya", [T, 256], F32, kind="ExternalOutput").ap(),
    }
    with ExitStack() as st:
        P = Prog(nc, st)
        emit_rwkv(nc, P, st, io, **kw)
        P.finish()
        P.emit()
    return nc


def rwkv_consts():
    idx = np.arange(64)
    su = (idx[:, None] < idx[None, :]).astype(np.float32)
    iu = (idx[:, None] <= idx[None, :]).astype(np.float32)
    msk = np.stack([su, su.T, iu, iu.T], axis=1)
    ident = np.eye(128, dtype=np.float32)
    icat = np.concatenate([np.eye(64), np.eye(64)], axis=0).astype(np.float32)
    bones = np.kron(np.eye(2), np.ones((64, 64))).astype(np.float32)
    rmask = np.ones((128, SEG), np.float32)
    rmask[:, ::L] = 0.0
    return {"msk": np.ascontiguousarray(msk), "ident": ident, "icat": icat, "bones": bones, "rmask": rmask}


def lora_pad(w):
    o = np.zeros((128, 512), np.float32)
    for d in range(2):
        o[d * 32:(d + 1) * 32, d * 256:(d + 1) * 256] = w[d]
    return o


def rwkv_inputs(rw_colsT, hq, prm_in):
    ch = slice(hq * 256, (hq + 1) * 256)
    padT = lambda a: np.pad(a, ((0, 0), (1, 1)))
    rkv = np.stack([padT(rw_colsT[w * 512 + hq * 256: w * 512 + (hq + 1) * 256]) for w in range(3)])
    wd = padT(rw_colsT[1536:1600]); ad = padT(rw_colsT[1600:1664]); gd = padT(rw_colsT[1664:1760])
    mu = prm_in["rw_mu"]
    prm = np.zeros((128, 32), np.float32)
    for w in range(3):
        for pc in range(2):
            prm[:, w * 2 + pc] = mu[w * 512 + hq * 256 + pc * 128: w * 512 + hq * 256 + (pc + 1) * 128]
    for d in range(2):
        for pc in range(2):
            cs = slice(hq * 256 + pc * 128, hq * 256 + (pc + 1) * 128)
            prm[:, 6 + d * 2 + pc] = prm_in["rw_w0"][d, cs]
            prm[:, 10 + d * 2 + pc] = prm_in["rw_a0"][d, cs]
    for pc in range(2):
        cs = slice(hq * 256 + pc * 128, hq * 256 + (pc + 1) * 128)
        prm[:, 14 + pc] = prm_in["rw_k_k"][cs]
        prm[:, 16 + pc] = prm_in["rw_k_a"][cs]
        prm[:, 20 + pc] = prm_in["rw_r_k"].reshape(-1)[cs]
    mul = np.zeros((128, 3), np.float32)
    mul[0:64, 0] = mu[1536:1600]; mul[0:64, 1] = mu[1600:1664]; mul[0:96, 2] = mu[1664:1760]
    m = dict(rwkv_consts())
    m.update({
        "rkv": np.ascontiguousarray(rkv), "wd": np.ascontiguousarray(wd), "ad": np.ascontiguousarray(ad), "gd": np.ascontiguousarray(gd),
        "prm": prm, "mul": mul,
        "w2s": lora_pad(prm_in["rw_w2"][:, :, ch]), "a2s": lora_pad(prm_in["rw_a2"][:, :, ch]),
        "g2s": np.ascontiguousarray(np.pad(prm_in["rw_g2"][:, ch], ((0, 32), (0, 0)))),
        "hm": np.ascontiguousarray(np.stack([(np.arange(128) < 64), (np.arange(128) >= 64)], axis=1).astype(np.float32)),
        "lng": np.ascontiguousarray(np.broadcast_to(prm_in["rw_lnx_g"][ch][None, :], (64, 256))),
        "lnb": np.ascontiguousarray(np.broadcast_to(prm_in["rw_lnx_b"][ch][None, :], (64, 256))),
    })
    return m


NORM_EPS = 1e-6
D = 2048
KC = 16
NT = 2048
TI = 512
TW = TI + 2
HB = TW // 2
DFF = 5632
NF = DFF // 128
FG = 2
GELU_C = 1.5957691216057308


def emit_post(nc, P, st, io, final, ntiles=NT // TI, nfg=NF // FG, nseg=1):
    sb = lambda name, shape, dt=F32: st.enter_context(nc.sbuf_tensor("sb_" + name, shape, dt))
    ps = lambda name, shape, dt=F32: st.enter_context(nc.psum_tensor("ps_" + name, shape, dt))
    V = lambda fn, r, w: P.op("vector", fn, reads=r, writes=w)
    G = lambda fn, r, w: P.op("gpsimd", fn, reads=r, writes=w)
    A = lambda fn, r, w: P.op("scalar", fn, reads=r, writes=w)
    M = lambda fn, r, w: P.op("tensor", fn, reads=r, writes=w)

    def tile(name, shape, dt=F32):
        return sb(name, shape, dt), P.res(name)

    x, r_x = tile("x", [128, KC, TW])
    h, r_h = tile("h", [128, KC, TW], BF16)
    mg, r_mg = tile("mg", [128, KC, TW], BF16)
    ybf, r_ybf = tile("ybf", [128, 12, TW], BF16)
    stg = [tile("stg%d" % i, [128, 4096]) for i in range(2)]
    wbf = [tile("wbf%d" % i, [128, 4096], BF16) for i in range(4)]
    gt = [tile("gt%d" % i, [128, TW]) for i in range(3)]
    tmp = [tile("tmp%d" % i, [128, TW]) for i in range(4)]
    abf, r_abf = tile("abf", [128, FG, TI], BF16)
    rs, r_rs = tile("rs", [128, TW])
    sq, r_sq = tile("sq", [128, KC, TW])
    ones, r_ones = tile("ones", [128, 128])
    nrm, r_nrm = tile("nrm", [128, 4, KC])
    cw, r_cw = tile("cw", [128, NF, 4])
    pbank = [(ps("pb%d" % i, [128, 512]), P.res("pb%d" % i)) for i in range(8)]
    bk = {"i": 0}
    dq = {"i": 0}

    def bank2():
        i = bk["i"] % 4
        bk["i"] += 1
        return pbank[2 * i], pbank[2 * i + 1]

    def dmaq():
        dq["i"] += 1
        return "sync" if dq["i"] % 2 == 0 else "gpsimd"

    V(lambda e: e.memset(ones[:, :], 1.0), [], [r_ones])
    P.dma("sync", nrm[:, :, :], io["nrm"][:, :, :], writes=[r_nrm])
    P.dma("sync", cw[:, :, :], io["cw"][:, :, :], writes=[r_cw])
    cvi = {"i": 0}

    def load_w(dst_i, src_ap, rows, cols):
        s_i = cvi["i"] % 2
        cvi["i"] += 1
        stt, r_st = stg[s_i]
        wb, r_wb = wbf[dst_i]
        n = rows * cols
        P.dma(dmaq(), stt[:, 0:n].rearrange("p (r c) -> p r c", c=cols), src_ap, writes=[r_st])
        (G if s_i == 0 else V)(lambda e: e.tensor_copy(out=wb[:, 0:n], in_=stt[:, 0:n]), [r_st], [r_wb])
        return wb[:, 0:n].rearrange("p (r c) -> p r c", c=cols), r_wb

    def mm_blocks(lhs_list, rhs_fn, r_list, ntok_off=0, ntok=TW):
        (b0, r0), (b1, r1) = bank2()
        half = ntok // 2
        for blk, (pb, r_pb) in enumerate(((b0, r0), (b1, r1))):
            for k, la in enumerate(lhs_list):
                M(lambda e, pb=pb, la=la, k=k, blk=blk: e.matmul(pb[:, 0:half], la, rhs_fn(k, ntok_off + blk * half, half),
                                                                 start=(k == 0), stop=(k == len(lhs_list) - 1)),
                  r_list, [r_pb])
        return ((b0, r0), (b1, r1)), half

    def rmsnorm_to_h(gain_idx, off, ntok):
        A(lambda e: e.activation(out=sq[:, :, 0:ntok], in_=x[:, :, off:off + ntok], func=AF.Square), [r_x], [r_sq])
        half = ntok // 2
        (b0, r0), (b1, r1) = bank2()
        for blk, (pb, r_pb) in enumerate(((b0, r0), (b1, r1))):
            for k in range(KC):
                M(lambda e, pb=pb, k=k, blk=blk: e.matmul(pb[:, 0:half], ones[:, :], sq[:, k, blk * half:(blk + 1) * half],
                                                          start=(k == 0), stop=(k == KC - 1)), [r_ones, r_sq], [r_pb])
            A(lambda e, pb=pb, blk=blk: e.activation(out=rs[:, blk * half:(blk + 1) * half], in_=pb[:, 0:half], func=AF.Sqrt,
                                                     scale=1.0 / D, bias=NORM_EPS), [r_pb], [r_rs])
        V(lambda e: e.reciprocal(out=rs[:, 0:ntok], in_=rs[:, 0:ntok]), [r_rs], [r_rs])
        for k in range(KC):
            V(lambda e, k=k: e.scalar_tensor_tensor(out=h[:, k, off:off + ntok], in0=x[:, k, off:off + ntok],
                                                    scalar=nrm[:, gain_idx, k:k + 1], op0=ALU.mult, in1=rs[:, 0:ntok], op1=ALU.mult),
              [r_x, r_nrm, r_rs], [r_h])

    wbr = [(io["w_br_a"], 4, 0), (io["w_br_b"], 2, 4), (io["w_br_c"], 6, 6)]

    for tl_all in range(ntiles * nseg):
        sg, tl = tl_all // ntiles, tl_all % ntiles
        xv = io["xT"][sg].rearrange("(c p) t -> p c t", p=128)
        yv = io["yT"][sg].rearrange("(c p) t -> p c t", p=128)
        gv = io["gT"][sg].rearrange("(g c p) t -> g c p t", g=3, p=128)
        pv = io["pT"][sg].rearrange("(c p) t -> p c t", p=128)
        ov = io["out"][sg].rearrange("(c p) t -> p c t", p=128)
        c0 = tl * TI
        for q in range(4):
            P.dma(dmaq(), x[:, q * 4:(q + 1) * 4, :], xv[:, q * 4:(q + 1) * 4, c0:c0 + TW], writes=[r_x])
        for q in range(3):
            stt, r_st = stg[q % 2]
            P.dma(dmaq(), stt[:, 0:4 * TW].rearrange("p (r c) -> p r c", c=TW), yv[:, q * 4:(q + 1) * 4, c0:c0 + TW], writes=[r_st])
            V(lambda e, stt=stt, q=q: e.tensor_copy(out=ybf[:, q * 4:(q + 1) * 4, :],
                                                    in_=stt[:, 0:4 * TW].rearrange("p (r c) -> p r c", c=TW)), [r_st], [r_ybf])
        for jg in range(8):
            wts = []
            for bi, (wap, nk, yoff) in enumerate(wbr):
                wv_ = wap.rearrange("(c p) n -> p c n", p=128)
                wts.append(load_w(bi, wv_[:, :, jg * 256:(jg + 1) * 256], nk, 256))
            for jj in range(2):
                j = jg * 2 + jj
                for bi, (wap, nk, yoff) in enumerate(wbr):
                    g_t, r_g = gt[bi]
                    P.dma(dmaq(), g_t[:, :], gv[bi, j, :, c0:c0 + TW], writes=[r_g])
                    A(lambda e, g_t=g_t: e.activation(out=g_t[:, :], in_=g_t[:, :], func=AF.Sigmoid), [r_g], [r_g])
                for bi, (wap, nk, yoff) in enumerate(wbr):
                    w3_, r_w = wts[bi]
                    g_t, r_g = gt[bi]
                    banks, half = mm_blocks([w3_[:, k, jj * 128:(jj + 1) * 128] for k in range(nk)],
                                            lambda k, o, n, yoff=yoff: ybf[:, yoff + k, o:o + n], [r_w, r_ybf])
                    for blk, (pb, r_pb) in enumerate(banks):
                        sl = slice(blk * half, (blk + 1) * half)
                        if bi == 0:
                            V(lambda e, pb=pb, sl=sl, g_t=g_t: e.tensor_tensor(out=tmp[0][0][:, sl], in0=pb[:, 0:half], in1=g_t[:, sl], op=ALU.mult),
                              [r_pb, r_g], [tmp[0][1]])
                        else:
                            V(lambda e, pb=pb, sl=sl, g_t=g_t: e.tensor_tensor(out=tmp[1][0][:, sl], in0=pb[:, 0:half], in1=g_t[:, sl], op=ALU.mult),
                              [r_pb, r_g], [tmp[1][1]])
                            if bi == 1:
                                G(lambda e, sl=sl: e.tensor_tensor(out=tmp[0][0][:, sl], in0=tmp[0][0][:, sl], in1=tmp[1][0][:, sl], op=ALU.add),
                                  [tmp[0][1], tmp[1][1]], [tmp[0][1]])
                            else:
                                G(lambda e, sl=sl, j=j: e.tensor_tensor(out=mg[:, j, sl], in0=tmp[0][0][:, sl], in1=tmp[1][0][:, sl], op=ALU.add),
                                  [tmp[0][1], tmp[1][1]], [r_mg])
        wov = io["w_out"].rearrange("(c p) n -> p c n", p=128)
        for jg in range(8):
            w3_, r_w = load_w(3, wov[:, :, jg * 256:(jg + 1) * 256], KC, 256)
            for jj in range(2):
                j = jg * 2 + jj
                banks, half = mm_blocks([w3_[:, k, jj * 128:(jj + 1) * 128] for k in range(KC)],
                                        lambda k, o, n: mg[:, k, o:o + n], [r_w, r_mg])
                for blk, (pb, r_pb) in enumerate(banks):
                    sl = slice(blk * half, (blk + 1) * half)
                    V(lambda e, pb=pb, sl=sl, j=j: e.tensor_tensor(out=x[:, j, sl], in0=x[:, j, sl], in1=pb[:, 0:half], op=ALU.add),
                      [r_pb, r_x], [r_x])
        rmsnorm_to_h(0, 0, TW)
        wgv = io["w_gate"].rearrange("(c p) n -> p c n", p=128)
        wuv = io["w_up"].rearrange("(c p) n -> p c n", p=128)
        wdv = io["w_down"].rearrange("(c p) n -> p c n", p=128)
        for fg in range(nfg):
            f0 = fg * FG
            wg3, r_wg = load_w(0, wgv[:, :, f0 * 128:(f0 + FG) * 128], KC, FG * 128)
            wu3, r_wu = load_w(1, wuv[:, :, f0 * 128:(f0 + FG) * 128], KC, FG * 128)
            for ff in range(FG):
                f = f0 + ff
                gb_, half = mm_blocks([wg3[:, k, ff * 128:(ff + 1) * 128] for k in range(KC)],
                                      lambda k, o, n: h[:, k, o:o + n], [r_wg, r_h])
                g_s, r_gs = tmp[0]
                for blk, (pb, r_pb) in enumerate(gb_):
                    A(lambda e, pb=pb, blk=blk: e.activation(out=g_s[:, blk * half:(blk + 1) * half], in_=pb[:, 0:half], func=AF.Copy),
                      [r_pb], [r_gs])
                ub_, half = mm_blocks([wu3[:, k, ff * 128:(ff + 1) * 128] for k in range(KC)],
                                      lambda k, o, n: h[:, k, o:o + n], [r_wu, r_h], ntok_off=1, ntok=TI)
                c_s, r_cs = tmp[1]
                t_s, r_ts = tmp[2]
                V(lambda e, f=f: e.tensor_scalar(out=c_s[:, 0:TI], in0=g_s[:, 1:TI + 1], scalar1=cw[:, f, 1:2], scalar2=cw[:, f, 3:4],
                                                 op0=ALU.mult, op1=ALU.add), [r_gs, r_cw], [r_cs])
                V(lambda e, f=f: e.scalar_tensor_tensor(out=c_s[:, 0:TI], in0=g_s[:, 0:TI], scalar=cw[:, f, 0:1], op0=ALU.mult,
                                                        in1=c_s[:, 0:TI], op1=ALU.add), [r_gs, r_cw, r_cs], [r_cs])
                V(lambda e, f=f: e.scalar_tensor_tensor(out=c_s[:, 0:TI], in0=g_s[:, 2:TI + 2], scalar=cw[:, f, 2:3], op0=ALU.mult,
                                                        in1=c_s[:, 0:TI], op1=ALU.add), [r_gs, r_cw, r_cs], [r_cs])
                A(lambda e: e.activation(out=t_s[:, 0:TI], in_=c_s[:, 0:TI], func=AF.Square), [r_cs], [r_ts])
                G(lambda e: e.tensor_scalar(out=t_s[:, 0:TI], in0=t_s[:, 0:TI], scalar1=0.044715, scalar2=1.0, op0=ALU.mult, op1=ALU.add),
                  [r_ts], [r_ts])
                G(lambda e: e.tensor_tensor(out=t_s[:, 0:TI], in0=t_s[:, 0:TI], in1=c_s[:, 0:TI], op=ALU.mult), [r_ts, r_cs], [r_ts])
                A(lambda e: e.activation(out=t_s[:, 0:TI], in_=t_s[:, 0:TI], func=AF.Sigmoid, scale=GELU_C), [r_ts], [r_ts])
                G(lambda e: e.tensor_tensor(out=t_s[:, 0:TI], in0=t_s[:, 0:TI], in1=c_s[:, 0:TI], op=ALU.mult), [r_ts, r_cs], [r_ts])
                for blk, (pb, r_pb) in enumerate(ub_):
                    V(lambda e, pb=pb, blk=blk, ff=ff: e.tensor_tensor(out=abf[:, ff, blk * half:(blk + 1) * half],
                                                                      in0=pb[:, 0:half], in1=t_s[:, blk * half:(blk + 1) * half], op=ALU.mult),
                      [r_pb, r_ts], [r_abf])
            for hc in range(2):
                wd3, r_wd = load_w(2, wdv[:, f0:f0 + FG, hc * 1024:(hc + 1) * 1024], FG, 1024)
                for jj in range(8):
                    j = hc * 8 + jj
                    banks, half = mm_blocks([wd3[:, k, jj * 128:(jj + 1) * 128] for k in range(FG)],
                                            lambda k, o, n: abf[:, k, o:o + n], [r_wd, r_abf], ntok_off=0, ntok=TI)
                    for blk, (pb, r_pb) in enumerate(banks):
                        sl = slice(1 + blk * half, 1 + (blk + 1) * half)
                        V(lambda e, pb=pb, sl=sl, j=j: e.tensor_tensor(out=x[:, j, sl], in0=x[:, j, sl], in1=pb[:, 0:half], op=ALU.add),
                          [r_pb, r_x], [r_x])
        rmsnorm_to_h(1, 1, TI)
        stt, r_st = stg[0]
        P.dma(dmaq(), stt[:, 0:2 * TI].rearrange("p (r c) -> p r c", c=TI), pv[:, :, tl * TI:(tl + 1) * TI], writes=[r_st])
        V(lambda e: e.tensor_copy(out=ybf[:, 0:2, 0:TI], in_=stt[:, 0:2 * TI].rearrange("p (r c) -> p r c", c=TI)), [r_st], [r_ybf])
        wpv = io["w_pg"].rearrange("(c p) n -> p c n", p=128)
        wlv = io["w_ple"].rearrange("(c p) n -> p c n", p=128)
        for jg in range(8):
            wp3, r_wp = load_w(0, wpv[:, :, jg * 256:(jg + 1) * 256], KC, 256)
            wl3, r_wl = load_w(1, wlv[:, :, jg * 256:(jg + 1) * 256], 2, 256)
            for jj in range(2):
                j = jg * 2 + jj
                gb_, half = mm_blocks([wp3[:, k, jj * 128:(jj + 1) * 128] for k in range(KC)],
                                      lambda k, o, n: h[:, k, o:o + n], [r_wp, r_h], ntok_off=1, ntok=TI)
                g_s, r_gs = tmp[0]
                for blk, (pb, r_pb) in enumerate(gb_):
                    A(lambda e, pb=pb, blk=blk: e.activation(out=g_s[:, blk * half:(blk + 1) * half], in_=pb[:, 0:half], func=AF.Sigmoid),
                      [r_pb], [r_gs])
                pb_, half = mm_blocks([wl3[:, k, jj * 128:(jj + 1) * 128] for k in range(2)],
                                      lambda k, o, n: ybf[:, k, o:o + n], [r_wl, r_ybf], ntok_off=0, ntok=TI)
                for blk, (pb, r_pb) in enumerate(pb_):
                    V(lambda e, pb=pb, blk=blk: e.tensor_tensor(out=g_s[:, blk * half:(blk + 1) * half], in0=pb[:, 0:half],
                                                                in1=g_s[:, blk * half:(blk + 1) * half], op=ALU.mult), [r_pb, r_gs], [r_gs])
                G(lambda e, j=j: e.tensor_tensor(out=x[:, j, 1:TI + 1], in0=x[:, j, 1:TI + 1], in1=g_s[:, 0:TI], op=ALU.add),
                  [r_x, r_gs], [r_x])
        if final:
            A(lambda e: e.activation(out=sq[:, :, 0:TI], in_=x[:, :, 1:TI + 1], func=AF.Square), [r_x], [r_sq])
            half = TI // 2
            (b0, r0), (b1, r1) = bank2()
            for blk, (pb, r_pb) in enumerate(((b0, r0), (b1, r1))):
                for k in range(KC):
                    M(lambda e, pb=pb, k=k, blk=blk: e.matmul(pb[:, 0:half], ones[:, :], sq[:, k, blk * half:(blk + 1) * half],
                                                              start=(k == 0), stop=(k == KC - 1)), [r_ones, r_sq], [r_pb])
                A(lambda e, pb=pb, blk=blk: e.activation(out=rs[:, blk * half:(blk + 1) * half], in_=pb[:, 0:half], func=AF.Sqrt,
                                                         scale=1.0 / D, bias=NORM_EPS), [r_pb], [r_rs])
            V(lambda e: e.reciprocal(out=rs[:, 0:TI], in_=rs[:, 0:TI]), [r_rs], [r_rs])
            for k in range(KC):
                V(lambda e, k=k: e.scalar_tensor_tensor(out=sq[:, k, 0:TI], in0=x[:, k, 1:TI + 1], scalar=nrm[:, 2, k:k + 1],
                                                        op0=ALU.mult, in1=rs[:, 0:TI], op1=ALU.mult), [r_x, r_nrm, r_rs], [r_sq])
            for q in range(4):
                P.dma(dmaq(), ov[:, q * 4:(q + 1) * 4, tl * TI:(tl + 1) * TI], sq[:, q * 4:(q + 1) * 4, 0:TI], reads=[r_sq])
        else:
            for q in range(4):
                P.dma(dmaq(), ov[:, q * 4:(q + 1) * 4, tl * TI:(tl + 1) * TI], x[:, q * 4:(q + 1) * 4, 1:TI + 1], reads=[r_x])


def build_post(final, nseg=1, **kw):
    nc = bass.Bass("TRN2", target_bir_lowering=False)
    di = lambda name, shape: nc.dram_tensor(name, shape, F32, kind="ExternalInput").ap()
    io = {
        "xT": di("xT", [nseg, D, NT + 2]), "yT": di("yT", [nseg, 1536, NT + 2]), "gT": di("gT", [nseg, 3 * D, NT + 2]), "pT": di("pT", [nseg, 256, NT]),
        "nrm": di("nrm", [128, 4, KC]), "cw": di("cw", [128, NF, 4]),
        "w_br_a": di("w_br_a", [512, D]), "w_br_b": di("w_br_b", [256, D]), "w_br_c": di("w_br_c", [768, D]),
        "w_out": di("w_out", [D, D]), "w_gate": di("w_gate", [D, DFF]), "w_up": di("w_up", [D, DFF]), "w_down": di("w_down", [DFF, D]),
        "w_pg": di("w_pg", [D, D]), "w_ple": di("w_ple", [256, D]),
        "out": nc.dram_tensor("out", [nseg, D, NT], F32, kind="ExternalOutput").ap(),
    }
    with ExitStack() as st:
        P = Prog(nc, st)
        emit_post(nc, P, st, io, final, nseg=nseg, **kw)
        P.finish()
        P.emit()
    return nc


def halo(a, t0, n, axis=0):
    T_ = a.shape[axis]
    lo, hi = t0 - 1, t0 + n + 1
    sl = [slice(None)] * a.ndim
    sl[axis] = slice(max(lo, 0), min(hi, T_))
    pw = [(0, 0)] * a.ndim
    pw[axis] = (max(0, -lo), max(0, hi - T_))
    return np.pad(a[tuple(sl)], pw)


def post_inputs(x_b, ya_b, yb_b, yc_b, gates_b, p_b, half, W, li, final_g):
    t0 = half * NT
    yall = np.concatenate([ya_b, yb_b, yc_b], axis=1)
    nrm = np.zeros((128, 4, KC), np.float32)
    nrm[:, 0, :] = W["norm_ffn"][li].reshape(KC, 128).T
    nrm[:, 1, :] = W["norm_ple"][li].reshape(KC, 128).T
    nrm[:, 2, :] = final_g.reshape(KC, 128).T
    cw = np.zeros((128, NF, 4), np.float32)
    for i in range(3):
        cw[:, :, i] = W["ffn_conv_w"][li][i].reshape(NF, 128).T
    cw[:, :, 3] = W["ffn_conv_b"][li].reshape(NF, 128).T
    return {
        "xT": np.ascontiguousarray(halo(x_b, t0, NT).T), "yT": np.ascontiguousarray(halo(yall, t0, NT).T),
        "gT": np.ascontiguousarray(halo(gates_b, t0, NT).T), "pT": np.ascontiguousarray(p_b[t0:t0 + NT].T),
        "nrm": nrm, "cw": cw,
        "w_br_a": W["w_br_a"][li], "w_br_b": W["w_br_b"][li], "w_br_c": W["w_br_c"][li], "w_out": W["w_out"][li],
        "w_gate": W["w_ffn_gate"][li], "w_up": W["w_ffn_up"][li], "w_down": W["w_ffn_down"][li],
        "w_pg": W["w_ple_gate"][li], "w_ple": W["w_ple"][li],
    }


NDENSE = 8
OFF = {"rw": 0, "dq": 1760, "dk": 2528, "dv": 3296, "nq": 4064, "nk": 4832, "nv": 5600, "g": 6368}


def attn_inputs_fm(zb, na_bias, half):
    C, S = rope_consts()
    T_ = 4096
    t0 = half * NTQ
    rows = lambda k: zb[OFF[k]:OFF[k] + 768].reshape(12, 64, T_)

    def vaug(k):
        vh = rows(k).transpose(0, 2, 1)
        va = np.concatenate([vh, np.ones((12, T_, 1), np.float32)], axis=2)
        vw = window(va, half, 1)
        return np.ascontiguousarray(vw.reshape(12, 32, 128, 65).transpose(0, 2, 1, 3))
    return {
        "cq": np.ascontiguousarray(C[:, t0:t0 + NTQ]), "sq": np.ascontiguousarray(S[:, t0:t0 + NTQ]),
        "ck": np.ascontiguousarray(window(C, half, 1)), "sk": np.ascontiguousarray(window(S, half, 1)),
        "md": dil_masks(),
        "dq": np.ascontiguousarray(rows("dq")[:, :, t0:t0 + NTQ]), "dk": np.ascontiguousarray(window(rows("dk"), half, 2)),
        "dv": vaug("dv"),
        "nq": np.ascontiguousarray(rows("nq")[:, :, t0:t0 + NTQ]), "nk": np.ascontiguousarray(window(rows("nk"), half, 2)),
        "nv": vaug("nv"),
        "ne": np.ascontiguousarray(np.stack([na_tables(na_bias[h], na_slot_m(half)) for h in range(12)])),
    }


def post_inputs_fm(xfm_b, yT_b, gT_b, pT_b, half, W, li):
    t0 = half * NT
    nrm = np.zeros((128, 4, KC), np.float32)
    nrm[:, 0, :] = W["norm_ffn"][li].reshape(KC, 128).T
    nrm[:, 1, :] = W["norm_ple"][li].reshape(KC, 128).T
    nrm[:, 2, :] = W["norm_final"].reshape(KC, 128).T
    cw = np.zeros((128, NF, 4), np.float32)
    for i in range(3):
        cw[:, :, i] = W["ffn_conv_w"][li][i].reshape(NF, 128).T
    cw[:, :, 3] = W["ffn_conv_b"][li].reshape(NF, 128).T
    return {
        "xT": halo(xfm_b, t0, NT, axis=1), "yT": halo(yT_b, t0, NT, axis=1), "gT": halo(gT_b, t0, NT, axis=1),
        "pT": pT_b[:, t0:t0 + NT], "nrm": nrm, "cw": cw,
    }


def kernel(**inp):
    inp = {k: np.asarray(v) for k, v in inp.items()}
    x, p = inp["x"], inp["p"]
    W = inp
    B_, T_ = 4, 4096
    segs = [(b, hf) for b in range(B_) for hf in range(2)]
    nseg = 8 // NDENSE
    xfm = [np.ascontiguousarray(x[b].T) for b in range(B_)]
    ncA = build_inproj(2048, 2048, 12512, nseg=nseg)
    ncB1 = build_attn()
    ncB2 = build_rwkv()
    out = np.empty((B_, T_, 2048), np.float32)
    for li in range(2):
        gl = np.ascontiguousarray(W["norm_mix"][li].reshape(16, 128).T)
        in_maps = []
        for c in range(NDENSE):
            xs = np.stack([xfm[b][:, hf * 2048:(hf + 1) * 2048] for (b, hf) in segs[c * nseg:(c + 1) * nseg]])
            in_maps.append({"xT": np.ascontiguousarray(xs), "w": np.ascontiguousarray(W["w_in"][li]), "g": gl})
        res = run_bass_kernel_spmd(ncA, in_maps, core_ids=list(range(NDENSE)))
        zfull = [np.empty((12512, T_), np.float32) for _ in range(B_)]
        for c in range(NDENSE):
            for j, (b, hf) in enumerate(segs[c * nseg:(c + 1) * nseg]):
                zfull[b][:, hf * 2048:(hf + 1) * 2048] = res.results[c]["zT"][j]
        del res, in_maps
        in_maps = [attn_inputs_fm(zfull[c // 2], W["na_bias"][li], c % 2) for c in range(8)]
        resa = run_bass_kernel_spmd(ncB1, in_maps, core_ids=list(range(8))).results
        del in_maps
        prm_in = {k: W[k][li] for k in W if k.startswith("rw_")}
        in_maps = [rwkv_inputs(zfull[c // 2][0:1760], c % 2, prm_in) for c in range(8)]
        resr = run_bass_kernel_spmd(ncB2, in_maps, core_ids=list(range(8))).results
        del in_maps
        final = (li == 1)
        ncC = build_post(final, nseg=nseg)
        wmap = {"w_br_a": W["w_br_a"][li], "w_br_b": W["w_br_b"][li], "w_br_c": W["w_br_c"][li], "w_out": W["w_out"][li],
                "w_gate": W["w_ffn_gate"][li], "w_up": W["w_ffn_up"][li], "w_down": W["w_ffn_down"][li],
                "w_pg": W["w_ple_gate"][li], "w_ple": W["w_ple"][li]}
        wmap = {k: np.ascontiguousarray(v) for k, v in wmap.items()}
        yT = []
        for b in range(B_):
            ya = np.concatenate([resr[2 * b]["ya"], resr[2 * b + 1]["ya"]], axis=1)
            yb = np.concatenate([resa[2 * b]["yb"], resa[2 * b + 1]["yb"]], axis=0)
            yc = np.concatenate([resa[2 * b]["yc"], resa[2 * b + 1]["yc"]], axis=0)
            yT.append(np.ascontiguousarray(np.concatenate([ya, yb, yc], axis=1).T))
        in_maps = []
        for c in range(NDENSE):
            per = [post_inputs_fm(xfm[b], yT[b], zfull[b][OFF["g"]:], np.ascontiguousarray(p[li, b].T), hf, W, li)
                   for (b, hf) in segs[c * nseg:(c + 1) * nseg]]
            m = {k: np.ascontiguousarray(np.stack([q[k] for q in per])) for k in ("xT", "yT", "gT", "pT")}
            m["nrm"] = per[0]["nrm"]
            m["cw"] = per[0]["cw"]
            m.update(wmap)
            in_maps.append(m)
        res = run_bass_kernel_spmd(ncC, in_maps, core_ids=list(range(NDENSE)))
        for c in range(NDENSE):
            for j, (b, hf) in enumerate(segs[c * nseg:(c + 1) * nseg]):
                o = res.results[c]["out"][j]
                if final:
                    out[b, hf * 2048:(hf + 1) * 2048, :] = o.T
                else:
                    xfm[b][:, hf * 2048:(hf + 1) * 2048] = o
        del res, in_maps, zfull
    return out
```
